# Optimizing a Trainium2 kernel written in Bass

```python
import math
import jax, jax.numpy as jnp
from jax import lax
import numpy as np

D_MODEL = 1024
BATCH = 4
SEQ = 8192
DEPTH = 1
DEC_BATCH = 32
DEC_SEQ = 4
PAST_LEN = 16384
PAGE_SIZE = 128

A_GROUPS = ((128, 1), (512, 4), (2048, 16))
N_A_GROUPS = len(A_GROUPS)
A_HEADS = 8
A_HEAD_DIM = 64
A_OUT = A_HEADS * A_HEAD_DIM
B_D_INNER = (3 * D_MODEL) // 2
B_HEAD_DIM = 64
B_HEADS = B_D_INNER // B_HEAD_DIM
B_GROUPS = 4
B_D_STATE = 128
B_CONV = 4
B_CHUNK = 128
B_CONV_DIM = B_D_INNER + 2 * B_GROUPS * B_D_STATE
D_FF = ((8 * D_MODEL // 3 + 127) // 128) * 128
RMS_EPS = 1e-6

OFF_A = 0
OFF_Z = OFF_A + N_A_GROUPS * 3 * A_OUT
OFF_XBC = OFF_Z + B_D_INNER
OFF_DT = OFF_XBC + B_CONV_DIM
OFF_GATE = OFF_DT + B_HEADS
N_IN_COLS = OFF_GATE + 2 * D_MODEL

kernel_name = "dilated_attn_ssd_gated_hybrid_step"


def rmsnorm(x, g):
    xf = x.astype(jnp.float32)
    y = xf * lax.rsqrt(jnp.mean(xf * xf, axis=-1, keepdims=True) + RMS_EPS)
    return (y * g.astype(jnp.float32)).astype(x.dtype)


def swiglu(x, wg, wu, wd):
    return (jax.nn.silu(x @ wg) * (x @ wu)) @ wd


def dilated_attn_prompt(q, k, v, window, dil):
    b, s, h, e = q.shape
    nw = window // dil
    ls = s // dil
    nb = -(-ls // nw)
    pad = nb * nw - ls

    def to_blocks(t):
        t = t.reshape(b, ls, dil, h, e).transpose(0, 2, 1, 3, 4)
        t = jnp.pad(t, ((0, 0), (0, 0), (0, pad), (0, 0), (0, 0)))
        return t.reshape(b, dil, nb, nw, h, e)

    def with_prev(t):
        prev = jnp.pad(t, ((0, 0), (0, 0), (1, 0), (0, 0), (0, 0), (0, 0)))[:, :, :nb]
        return jnp.concatenate([prev, t], axis=3)

    qb = to_blocks(q)
    kb = with_prev(to_blocks(k))
    vb = with_prev(to_blocks(v))
    scores = jnp.einsum("brnqhe,brnkhe->brnhqk", qb, kb).astype(jnp.float32) * (A_HEAD_DIM ** -0.5)
    qi = jnp.arange(nw)[:, None]
    kj = jnp.arange(2 * nw)[None, :]
    dist = qi + nw - kj
    blk = jnp.arange(nb)[:, None, None]
    valid = (dist >= 0) & (dist <= nw) & ((blk > 0) | (kj >= nw))
    scores = jnp.where(valid[None, None, :, None], scores, -jnp.inf)
    m = jnp.max(scores, axis=-1, keepdims=True)
    pr = jnp.exp(scores - m)
    den = jnp.sum(pr, axis=-1)
    o = jnp.einsum("brnhqk,brnkhe->brnqhe", pr, vb.astype(jnp.float32))
    o = o / jnp.moveaxis(den, -1, -2)[..., None]
    lse = jnp.moveaxis(m[..., 0] + jnp.log(den), -1, -2)

    def from_blocks(t):
        t = t.reshape((b, dil, nb * nw) + t.shape[4:])[:, :, :ls]
        t = jnp.moveaxis(t, 1, 2)
        return t.reshape((b, s) + t.shape[3:])

    return from_blocks(o), from_blocks(lse)


def dilated_attn_sample(q, k, v, kv_buf, window, dil):
    b, l, h, e = q.shape
    lb = kv_buf.shape[1]
    nw = window // dil
    kv = jnp.concatenate([kv_buf, jnp.stack([k, v], axis=2).astype(kv_buf.dtype)], axis=1)
    idx = lb + jnp.arange(l)[:, None] - dil * jnp.arange(nw + 1)[None, :]
    valid = idx >= 0
    idx = jnp.maximum(idx, 0)
    kg = kv[:, :, 0][:, idx]
    vg = kv[:, :, 1][:, idx]
    scores = jnp.einsum("blhe,bljhe->blhj", q, kg).astype(jnp.float32) * (A_HEAD_DIM ** -0.5)
    scores = jnp.where(valid[None, :, None, :], scores, -jnp.inf)
    m = jnp.max(scores, axis=-1, keepdims=True)
    pr = jnp.exp(scores - m)
    den = jnp.sum(pr, axis=-1)
    o = jnp.einsum("blhj,bljhe->blhe", pr, vg.astype(jnp.float32)) / den[..., None]
    lse = m[..., 0] + jnp.log(den)
    new_buf = kv[:, -min(window, lb + l):]
    return o, lse, new_buf


def causal_conv(xbc, buf, w, bias):
    l = xbc.shape[1]
    xp = jnp.concatenate([buf.astype(xbc.dtype), xbc], axis=1)
    y = bias
    for tap in range(B_CONV):
        y = y + xp[:, tap:tap + l] * w[tap]
    return jax.nn.silu(y), xp[:, -(B_CONV - 1):]


def ssd_chunked(x, dt, a_neg, bm, cm, init_state):
    f32 = jnp.float32
    b, l, nh, hp = x.shape
    rep = nh // B_GROUPS
    t = min(B_CHUNK, l)
    nc = -(-l // t)
    pad = nc * t - l

    def padt(a):
        return jnp.pad(a, [(0, 0), (0, pad)] + [(0, 0)] * (a.ndim - 2))

    xd = padt((x.astype(f32) * dt[..., None])).reshape(b, nc, t, B_GROUPS, rep, hp)
    da = padt(dt * a_neg).reshape(b, nc, t, B_GROUPS, rep)
    bc = padt(bm.astype(f32)).reshape(b, nc, t, B_GROUPS, B_D_STATE)
    cc = padt(cm.astype(f32)).reshape(b, nc, t, B_GROUPS, B_D_STATE)
    cs = jnp.cumsum(da, axis=2)
    seg = cs[:, :, :, None] - cs[:, :, None, :]
    causal = jnp.tril(jnp.ones((t, t), bool))[None, None, :, :, None, None]
    lmat = jnp.exp(jnp.where(causal, seg, -jnp.inf))
    cb = jnp.einsum("bcign,bcjgn->bcijg", cc, bc)
    y_diag = jnp.einsum("bcijgr,bcjgrp->bcigrp", cb[..., None] * lmat, xd)
    decay_to_end = jnp.exp(cs[:, :, -1:] - cs)
    chunk_states = jnp.einsum("bcjgn,bcjgrp->bcgrpn", bc, xd * decay_to_end[..., None])
    chunk_decay = jnp.exp(cs[:, :, -1])

    def step(s, inp):
        dec, st = inp
        return s * dec[..., None, None] + st, s

    s0 = init_state.astype(f32).reshape(b, B_GROUPS, rep, hp, B_D_STATE)
    final, prev = lax.scan(step, s0, (jnp.moveaxis(chunk_decay, 1, 0), jnp.moveaxis(chunk_states, 1, 0)))
    prev = jnp.moveaxis(prev, 0, 1)
    y_off = jnp.einsum("bcign,bcgrpn->bcigrp", cc, prev) * jnp.exp(cs)[..., None]
    y = (y_diag + y_off).reshape(b, nc * t, nh, hp)[:, :l]
    return y, final.reshape(b, nh, hp, B_D_STATE)


def token_mixer(hn, kv_bufs, conv_buf, ssm_state, p):
    f32 = jnp.float32
    b, l, _ = hn.shape
    proj = hn @ p["w_in"]
    qkv = proj[..., OFF_A:OFF_Z].reshape(b, l, N_A_GROUPS, 3, A_HEADS, A_HEAD_DIM)
    outs, lses, new_kv = [], [], []
    for g, (window, dil) in enumerate(A_GROUPS):
        q, k, v = qkv[:, :, g, 0], qkv[:, :, g, 1], qkv[:, :, g, 2]
        if kv_bufs is None:
            o, lse = dilated_attn_prompt(q, k, v, window, dil)
            kv_new = jnp.stack([k, v], axis=2)[:, -min(window, l):]
        else:
            o, lse, kv_new = dilated_attn_sample(q, k, v, kv_bufs[g], window, dil)
        outs.append(o)
        lses.append(lse)
        new_kv.append(kv_new)
    alpha = jax.nn.softmax(jnp.stack(lses, axis=2), axis=2)
    o_a = jnp.sum(alpha[..., None] * jnp.stack(outs, axis=2), axis=2)
    o_a = o_a.reshape(b, l, A_OUT).astype(hn.dtype) @ p["w_branch_a"]
    z = proj[..., OFF_Z:OFF_XBC]
    xbc, new_conv = causal_conv(proj[..., OFF_XBC:OFF_DT], conv_buf, p["conv_w"], p["conv_b"])
    xs = xbc[..., :B_D_INNER].reshape(b, l, B_HEADS, B_HEAD_DIM)
    bm = xbc[..., B_D_INNER:B_D_INNER + B_GROUPS * B_D_STATE].reshape(b, l, B_GROUPS, B_D_STATE)
    cm = xbc[..., B_D_INNER + B_GROUPS * B_D_STATE:].reshape(b, l, B_GROUPS, B_D_STATE)
    dt = jax.nn.softplus(proj[..., OFF_DT:OFF_GATE].astype(f32) + p["dt_bias"].astype(f32))
    a_neg = -jnp.exp(p["a_log"].astype(f32))
    y, new_ssm = ssd_chunked(xs, dt, a_neg, bm, cm, ssm_state)
    y = y + xs.astype(f32) * p["d_skip"].astype(f32)[:, None]
    y = y.reshape(b, l, B_D_INNER) * jax.nn.silu(z.astype(f32))
    yg = y.reshape(b, l, B_GROUPS, B_D_INNER // B_GROUPS)
    yg = yg * lax.rsqrt(jnp.mean(yg * yg, axis=-1, keepdims=True) + RMS_EPS)
    y = yg.reshape(b, l, B_D_INNER) * p["ssd_norm_w"].astype(f32)
    o_b = y.astype(hn.dtype) @ p["w_branch_b"]
    gates = jax.nn.sigmoid(proj[..., OFF_GATE:].astype(f32))
    merged = gates[..., :D_MODEL] * o_a.astype(f32) + gates[..., D_MODEL:] * o_b.astype(f32)
    out = merged.astype(hn.dtype) @ p["w_out"]
    return out, new_kv, new_conv, new_ssm.astype(ssm_state.dtype)


def trunk_layer(x, kv_bufs, conv_buf, ssm_state, p):
    f1 = swiglu(rmsnorm(x, p["g_pre_ffn1"]), p["ffn1_gate"], p["ffn1_up"], p["ffn1_down"])
    h = x + 0.5 * rmsnorm(f1, p["g_post_ffn1"])
    mix, new_kv, new_conv, new_ssm = token_mixer(rmsnorm(h, p["g_pre_mix"]), kv_bufs, conv_buf, ssm_state, p)
    h = h + rmsnorm(mix, p["g_post_mix"])
    f2 = swiglu(rmsnorm(h, p["g_pre_ffn2"]), p["ffn2_gate"], p["ffn2_up"], p["ffn2_down"])
    h = h + 0.5 * rmsnorm(f2, p["g_post_ffn2"])
    return h, new_kv, new_conv, new_ssm


def setup_inputs(seed: int = 0) -> dict:
    key = jax.random.key(seed)
    ks = iter(jax.random.split(key, 48))
    f32 = jnp.float32
    D = D_MODEL

    def nrm(shape, scale=1.0):
        return jax.random.normal(next(ks), shape, f32) * scale

    def gain(n):
        return 1.0 + nrm((DEPTH, n), 0.02)

    inp = {}
    inp["x_prompt"] = nrm((BATCH, SEQ, D))
    inp["x_sample"] = nrm((DEC_BATCH, DEC_SEQ, D))
    for window, _ in A_GROUPS:
        inp["cache_kv_w" + str(window)] = nrm((DEPTH, DEC_BATCH, min(window, PAST_LEN), 2, A_HEADS, A_HEAD_DIM))
    inp["state_conv"] = nrm((DEPTH, DEC_BATCH, B_CONV - 1, B_CONV_DIM))
    inp["state_ssm"] = nrm((DEPTH, DEC_BATCH, B_HEADS, B_HEAD_DIM, B_D_STATE), 0.1)
    inp["w_in"] = nrm((DEPTH, D, N_IN_COLS), D ** -0.5)
    inp["conv_w"] = nrm((DEPTH, B_CONV, B_CONV_DIM), B_CONV ** -0.5)
    inp["conv_b"] = nrm((DEPTH, B_CONV_DIM), 0.01)
    u = jax.random.uniform(next(ks), (DEPTH, B_HEADS), f32)
    dt0 = jnp.exp(u * (math.log(0.1) - math.log(0.001)) + math.log(0.001))
    inp["dt_bias"] = dt0 + jnp.log(-jnp.expm1(-dt0))
    inp["a_log"] = jnp.log(jax.random.uniform(next(ks), (DEPTH, B_HEADS), f32, 1.0, 16.0))
    inp["d_skip"] = 1.0 + nrm((DEPTH, B_HEADS), 0.1)
    inp["ssd_norm_w"] = gain(B_D_INNER)
    inp["w_branch_a"] = nrm((DEPTH, A_OUT, D), A_OUT ** -0.5)
    inp["w_branch_b"] = nrm((DEPTH, B_D_INNER, D), B_D_INNER ** -0.5)
    inp["w_out"] = nrm((DEPTH, D, D), D ** -0.5)
    inp["ffn1_gate"] = nrm((DEPTH, D, D_FF), D ** -0.5)
    inp["ffn1_up"] = nrm((DEPTH, D, D_FF), D ** -0.5)
    inp["ffn1_down"] = nrm((DEPTH, D_FF, D), D_FF ** -0.5)
    inp["ffn2_gate"] = nrm((DEPTH, D, D_FF), D ** -0.5)
    inp["ffn2_up"] = nrm((DEPTH, D, D_FF), D ** -0.5)
    inp["ffn2_down"] = nrm((DEPTH, D_FF, D), D_FF ** -0.5)
    inp["g_pre_ffn1"] = gain(D)
    inp["g_post_ffn1"] = gain(D)
    inp["g_pre_mix"] = gain(D)
    inp["g_post_mix"] = gain(D)
    inp["g_pre_ffn2"] = gain(D)
    inp["g_post_ffn2"] = gain(D)
    return inp


def reference(x_prompt, x_sample, cache_kv_w128, cache_kv_w512, cache_kv_w2048, state_conv, state_ssm,
              w_in, conv_w, conv_b, dt_bias, a_log, d_skip, ssd_norm_w, w_branch_a, w_branch_b, w_out,
              ffn1_gate, ffn1_up, ffn1_down, ffn2_gate, ffn2_up, ffn2_down,
              g_pre_ffn1, g_post_ffn1, g_pre_mix, g_post_mix, g_pre_ffn2, g_post_ffn2):
    y_p, y_s = x_prompt, x_sample
    pk128, pk512, pk2048, pconv, pssm = [], [], [], [], []
    sk128, sk512, sk2048, sconv, sssm = [], [], [], [], []
    nbp = x_prompt.shape[0]
    for layer in range(DEPTH):
        p = {"w_in": w_in[layer], "conv_w": conv_w[layer], "conv_b": conv_b[layer],
             "dt_bias": dt_bias[layer], "a_log": a_log[layer], "d_skip": d_skip[layer],
             "ssd_norm_w": ssd_norm_w[layer], "w_branch_a": w_branch_a[layer],
             "w_branch_b": w_branch_b[layer], "w_out": w_out[layer],
             "ffn1_gate": ffn1_gate[layer], "ffn1_up": ffn1_up[layer], "ffn1_down": ffn1_down[layer],
             "ffn2_gate": ffn2_gate[layer], "ffn2_up": ffn2_up[layer], "ffn2_down": ffn2_down[layer],
             "g_pre_ffn1": g_pre_ffn1[layer], "g_post_ffn1": g_post_ffn1[layer],
             "g_pre_mix": g_pre_mix[layer], "g_post_mix": g_post_mix[layer],
             "g_pre_ffn2": g_pre_ffn2[layer], "g_post_ffn2": g_post_ffn2[layer]}
        conv0 = jnp.zeros((nbp, B_CONV - 1, B_CONV_DIM), x_prompt.dtype)
        ssm0 = jnp.zeros((nbp, B_HEADS, B_HEAD_DIM, B_D_STATE), jnp.float32)
        y_p, kv_p, conv_p, ssm_p = trunk_layer(y_p, None, conv0, ssm0, p)
        bufs = (cache_kv_w128[layer], cache_kv_w512[layer], cache_kv_w2048[layer])
        y_s, kv_s, conv_s, ssm_s = trunk_layer(y_s, bufs, state_conv[layer], state_ssm[layer], p)
        pk128.append(kv_p[0]); pk512.append(kv_p[1]); pk2048.append(kv_p[2])
        pconv.append(conv_p); pssm.append(ssm_p)
        sk128.append(kv_s[0]); sk512.append(kv_s[1]); sk2048.append(kv_s[2])
        sconv.append(conv_s); sssm.append(ssm_s)
    return (y_p, y_s,
            jnp.stack(pk128), jnp.stack(pk512), jnp.stack(pk2048), jnp.stack(pconv), jnp.stack(pssm),
            jnp.stack(sk128), jnp.stack(sk512), jnp.stack(sk2048), jnp.stack(sconv), jnp.stack(sssm))
```

```python
import numpy as np
import os
STOP = int(os.environ.get('MIX_STOP', '9'))
DBG = int(os.environ.get('MIX_DBG', '0'))
DBG_G = int(os.environ.get('MIX_DBG_G', '0'))
ATT_PIPE = int(os.environ.get('ATT_PIPE', '0'))
import concourse.bass as bass
import concourse.mybir as mybir
from concourse.bass_utils import run_bass_kernel_spmd

F32 = mybir.dt.float32
BF16 = mybir.dt.bfloat16
AF = mybir.ActivationFunctionType
ALU = mybir.AluOpType

D = 1024
DFF = 2816
NIN = 10776
OFF_Z, OFF_XBC, OFF_DT, OFF_GATE = 4608, 6144, 8704, 8728
EPS = 1e-6
NCORES = 8
P = 128


class Sched:
    def __init__(self, nc):
        self.nc = nc
        self.engs = {"pe": nc.tensor, "act": nc.scalar, "dve": nc.vector, "pool": nc.gpsimd, "sp": nc.sync}
        self.stream = {e: [] for e in self.engs}
        self.sems, self.cnt = {}, {}
        self.waited = {e: {} for e in self.engs}
        self.lastw, self.readers = {}, {}
        for e in self.engs:
            self.sem(e)

    def sem(self, key):
        if key not in self.sems:
            self.sems[key] = self.nc.alloc_semaphore("s_" + key)
            self.cnt[key] = 0
        return self.sems[key]

    def _deps(self, e, reads, writes):
        need = {}

        def add(kv):
            if kv is not None and need.get(kv[0], 0) < kv[1]:
                need[kv[0]] = kv[1]

        for r in reads:
            add(self.lastw.get(r))
            if r.startswith("pb"):
                for kv in self.readers.get(r, {}).items():
                    if kv[0] != e:
                        add(kv)
        for w in writes:
            add(self.lastw.get(w))
            for kv in self.readers.get(w, {}).items():
                add(kv)
        out = []
        for k, v in need.items():
            if k == "pe" and e == "pe":
                continue
            if self.waited[e].get(k, 0) >= v:
                continue
            self.waited[e][k] = v
            out.append((k, v))
        return out

    def _mark(self, key, val, reads, writes):
        for r in reads:
            d = self.readers.setdefault(r, {})
            d[key] = max(d.get(key, 0), val)
        for w in writes:
            self.lastw[w] = (key, val)
            self.readers[w] = {}

    def op(self, e, fn, reads=(), writes=(), inc=True):
        waits = self._deps(e, reads, writes)
        if inc:
            self.cnt[e] += 1
            val = self.cnt[e]
        else:
            val = self.cnt[e] + 1
        self._mark(e, val, reads, writes)
        self.stream[e].append((waits, fn, (e, 1) if inc else None))

    def dma(self, e, semkey, out, in_, reads, writes):
        waits = self._deps(e, reads, writes)
        self.sem(semkey)
        self.cnt[semkey] += 16
        self._mark(semkey, self.cnt[semkey], reads, writes)
        self.stream[e].append((waits, lambda eng: eng.dma_start(out=out, in_=in_), (semkey, 16)))

    def finish(self):
        for k, v in self.cnt.items():
            if k not in self.engs and v > 0 and self.waited["sp"].get(k, 0) < v:
                self.stream["sp"].append(([(k, v)], None, None))

    def emit(self):
        nc = self.nc
        with nc.Block() as block:
            def mk(e):
                def body(eng):
                    for waits, fn, inc in self.stream[e]:
                        for k, v in waits:
                            eng.wait_ge(self.sems[k], v)
                        if fn is None:
                            continue
                        ins = fn(eng)
                        if inc is not None:
                            ins.then_inc(self.sems[inc[0]], inc[1])
                return body
            block.sync(mk("sp"))
            block.tensor(mk("pe"))
            block.scalar(mk("act"))
            block.vector(mk("dve"))
            block.gpsimd(mk("pool"))


WEIGHTS = {
    "w_in": (D, NIN), "w_branch_a": (512, D), "w_branch_b": (1536, D), "w_out": (D, D),
    "ffn1_gate": (D, DFF), "ffn1_up": (D, DFF), "ffn1_down": (DFF, D),
    "ffn2_gate": (D, DFF), "ffn2_up": (D, DFF), "ffn2_down": (DFF, D),
}
NSLOT = 2


class Builder:
    def __init__(self, n_pre, n_own, n_smp, n_kvpre=16):
        nc = self.nc = bass.Bass("TRN2", target_bir_lowering=False)
        self.S = Sched(nc)
        self.n_pre, self.n_own, self.n_smp, self.n_kvpre = n_pre, n_own, n_smp, n_kvpre
        self.kinds = ["pre"] * n_pre + ["own"] * n_own + ["smp"] * n_smp
        ng = len(self.kinds)
        nout = n_own + n_smp
        self.w32 = {n: nc.dram_tensor(n, [k, m], F32, kind="ExternalInput").ap() for n, (k, m) in WEIGHTS.items()}
        self.conv = {}
        self.xs = nc.dram_tensor("xs", [ng * P, D], F32, kind="ExternalInput").ap()
        self.gvec = nc.dram_tensor("gvec", [6, D], F32, kind="ExternalInput").ap()
        self.gpre_d = nc.dram_tensor("gpre_d", [P, 24], F32, kind="ExternalInput").ap()
        self.cst = nc.dram_tensor("cst", [P, 3 * P], F32, kind="ExternalInput").ap()
        self.ys = nc.dram_tensor("ys", [nout * P, D], F32, kind="ExternalOutput").ap()
        A = nc.alloc_sbuf_tensor
        self.ws = [A(f"ws{i}", [P, 8, 512], BF16) for i in range(NSLOT)]
        self.slot_i = 0
        self.nring = NSLOT + 1
        self.x = [A(f"x{i}", [P, D], F32) for i in range(2)]
        self.xn = A("xn", [P, D], BF16)
        self.junk = self.xn
        self.hnT = A("hnT", [P, 8, P], BF16)
        self.sg = A("sg", [P, 512], F32)
        self.gpre = A("gpre", [P, 3, 8], F32)
        self.gpost = A("gpost", [P, 3, D], F32)
        self.cstt = A("cstt", [P, 3 * P], F32)
        self.identb = A("identb", [P, P], BF16)
        self.st = A("st", [P, 16], F32)
        self.tmp = A("tmp", [P, 512], F32)
        DI = nc.dram_tensor
        self.cw_d = DI("cw_d", [P, 20 * 5], F32, kind="ExternalInput").ap()
        self.hv_d = DI("hv_d", [4, 24], F32, kind="ExternalInput").ap()
        self.nw_d = DI("nw_d", [P, 12], F32, kind="ExternalInput").ap()
        self.mask_d = DI("mask_d", [P, 24 * P], F32, kind="ExternalInput").ap()
        self.WIN = [128, 512, 2048]
        self.flg_d = DI("flg_d", [P, 2], F32, kind="ExternalInput").ap()
        self.kvo = [DI(f"kvo{g}", [self.WIN[g], 1024], F32, kind="ExternalOutput").ap() for g in range(3)]
        self.convo = DI("convo", [1 + n_smp, 3, 2560], F32, kind="ExternalOutput").ap()
        self.ssmo = DI("ssmo", [1 + n_smp, 1536, P], F32, kind="ExternalOutput").ap()
        if n_smp:
            self.cache = [DI(f"cache{g}", [n_smp, self.WIN[g], 1024], F32, kind="ExternalInput").ap() for g in range(3)]
            self.kvs = [DI(f"kvs{g}", [n_smp, self.WIN[g], 1024], F32, kind="ExternalOutput").ap() for g in range(3)]
            self.sconv = DI("sconv", [n_smp, 3, 2560], F32, kind="ExternalInput").ap()
            self.sssm = DI("sssm", [n_smp, 1536, P], F32, kind="ExternalInput").ap()
        self.flg = A("flg", [P, 2], F32)
        self.cw = A("cw", [P, 20, 5], F32)
        self.hv = A("hv", [P, 4, 24], F32)
        self.nw = A("nw", [P, 12], F32)
        self.maskf = A("maskf", [P, 512], F32)
        self.mask = A("mask", [P, 24, P], BF16)
        self.qb = A("qb", [P, 512], BF16)
        self.qb2 = A("qb2", [P, 512], BF16)
        self.qT = A("qT", [P, 12, P], BF16)
        self.NT = [2, 5, 17]
        self.kTh = [A(f"kTh{g}", [P, self.NT[g], 4, P], BF16) for g in range(3)]
        self.Vh = [A(f"Vh{g}", [P, self.NT[g], 8, 65], BF16) for g in range(3)]
        self.zs = A("zs", [P, 1536], F32)
        self.xraw = A("xraw", [P, 2560], BF16)
        self.stg = [A(f"stg{i}", [P, 10, 131], F32) for i in range(2)]
        self.carry = A("carry", [P, 20, 3], F32)
        self.caccT = A("cacc", [P, 2, 10, P], F32)
        self.cacc = [self.caccT[:, 0, :, :], self.caccT[:, 1, :, :]]
        self.xc = A("xc", [P, 20, P], BF16)
        self.xstm = A("xstm", [P, 16, P], BF16)
        self.sm = A("sm", [P, 10, 24], F32)
        self.ssdbuf = A("ssdbuf", [P, 3 * 1536], BF16)
        self.xd = self.ssdbuf[:, 0:1536].rearrange("p (h e) -> p h e", h=24)
        self.xdp = self.ssdbuf[:, 1536:3072].rearrange("p (h e) -> p h e", h=24)
        self.cbm = A("cbm", [P, 4, P], F32)
        self.Dt = A("Dt", [P, 4, P], F32)
        self.Lt = A("Lt", [P, 4, P], F32)
        self.Mt = self.ssdbuf[:, 3072:4608].rearrange("p (h e) -> p h e", h=12)
        self.y = A("y", [P, 1536], F32)
        self.ytmp = A("ytmp", [P, 1536], F32)
        self.act = self.zs[:].bitcast(BF16)[:, 0:DFF]
        self.actT = self.y[:].bitcast(BF16)[:, 0:DFF].rearrange("p (c t) -> p c t", c=22)
        self.ctmp = self.ytmp[:, 0:1280].rearrange("p (a b) -> p a b", a=10)
        self.ctmp1 = self.y[:, 0:1280].rearrange("p (a b) -> p a b", a=10)
        self.act2 = self.ytmp[:].bitcast(BF16)[:, 0:DFF]
        self.actT2 = self.caccT[:].rearrange("p a b c -> p (a b c)").bitcast(BF16)[:, 0:DFF].rearrange("p (c t) -> p c t", c=22)
        self.yn = A("yn", [P, 1536], BF16)
        self.ynT = A("ynT", [P, 12, P], BF16)
        self.ST = A("ST", [P, 1536], F32)
        self.STb = A("STb", [P, 1536], BF16)
        self.gates = A("gates", [P, 1024], F32)
        self.mrg = A("mrg", [P, 1024], F32)
        self.cur_x = [(self.x[0], "x0"), (self.x[1], "x1")]
        self.cur_g = [(self.gates, "gates"), (self.mrg, "mrg")]
        self.mrgb = A("mrgb", [P, 1024], BF16)
        self.mrgT = A("mrgT", [P, 8, P], BF16)
        self.PT = A("PT", [P, 8, P], BF16)
        self.oa = A("oa", [P, 512], BF16)
        self.oaT = A("oaT", [P, 4, P], BF16)
        self.pb = [nc.alloc_psum_tensor(f"pb{i}", [P, 512], F32) for i in range(8)]
        self.pb_i = 0

    def dump(self, name, ap, res, g=0, only_g=None):
        if not DBG or g != DBG_G:
            return
        shp = [int(v) for v in ap.shape]
        d = self.nc.dram_tensor("dbg_" + name, shp, ap.dtype, kind="ExternalOutput").ap()
        self.S.dma("sp", "dbg_" + name, d, ap, reads=[res], writes=[])

    @staticmethod
    def L(r):
        return [r] if isinstance(r, str) else list(r)

    def bank(self):
        i = self.pb_i
        self.pb_i = (i + 1) % 8
        return self.pb[i], f"pb{i}"

    def slab(self, wname, k0, nk, c0, ncols):
        S = self.S
        i = self.slot_i % self.nring
        self.slot_i = (i + 1) % self.nring
        if i < NSLOT:
            slot, wres = self.ws[i], [f"ws{i}"]
        else:
            slot, wres = self.ssdbuf[:, 0:4096].rearrange("p (k n) -> p k n", k=8), ["xd", "xdp", "Mt"]
        dst = slot[:, 0:nk, 0:ncols]
        key = (wname, k0, c0)
        res_scr = f"scr_{wname}_{k0}_{c0}"
        if key not in self.conv:
            scr = self.nc.dram_tensor(res_scr, [P, nk * ncols], BF16).ap().rearrange("p (k n) -> p k n", k=nk)
            self.conv[key] = scr
            src = self.w32[wname][k0 * P:(k0 + nk) * P, c0:c0 + ncols].rearrange("(kc p) n -> p kc n", p=P)
            S.dma("pool", f"lc{i}", dst, src, reads=[], writes=wres)
            S.dma("sp", f"sv{i}", scr, dst, reads=wres, writes=[res_scr])
        else:
            S.dma("sp", f"ld{i}", dst, self.conv[key], reads=[res_scr], writes=wres)
        return slot, wres

    def linear(self, actT, actT_res, KC, wname, c0, ncols, evac):
        self.linear_multi([(actT, actT_res)], KC, wname, c0, ncols, [evac])

    def linear_multi(self, acts, KC, wname, c0, ncols, evacs):
        S = self.S
        cb = 0
        while cb < ncols:
            n = min(512, ncols - cb)
            banks = [self.bank() for _ in acts]
            k0 = 0
            while k0 < KC:
                nk = min(8, KC - k0)
                slot, sr = self.slab(wname, k0, nk, c0 + cb, n)
                for (actT, ares), (ps, psr) in zip(acts, banks):
                    for j in range(nk):
                        first = (k0 + j == 0)
                        last = (k0 + j == KC - 1)
                        S.op("pe", (lambda eng, ps=ps, a=actT[:, k0 + j, :], r=slot[:, j, 0:n], f=first, l=last, n=n:
                                    eng.matmul(ps[:, 0:n], lhsT=a, rhs=r, start=f, stop=l)),
                             reads=self.L(ares) + sr, writes=[psr], inc=last or j == nk - 1)
                k0 += nk
            for (ps, psr), evac in zip(banks, evacs):
                evac(ps, psr, cb, n)
            cb += n

    def transpose_to(self, src, src_res, nchunks, dstT, dst_res, scale=None, scale_res="gpre"):
        S = self.S
        c = 0
        while c < nchunks:
            n = min(8, nchunks - c)
            ps, psr = self.bank()
            psb = ps[:].bitcast(BF16)
            for j in range(n):
                S.op("pe", (lambda eng, o=psb[:, j * P:(j + 1) * P], i=src[:, (c + j) * P:(c + j + 1) * P]:
                            eng.transpose(o, i, self.identb[:])),
                     reads=self.L(src_res) + ["identb"], writes=[psr], inc=(j == n - 1))
            src3 = psb[:, 0:n * P].rearrange("p (a b) -> p a b", a=n)
            if scale is None:
                S.op("act", (lambda eng, o=dstT[:, c:c + n, :], i=src3: eng.copy(out=o, in_=i)), reads=[psr], writes=self.L(dst_res))
            else:
                S.op("dve", (lambda eng, o=dstT[:, c:c + n, :], i=src3, sc=scale[:, c:c + n].unsqueeze(2).broadcast_to([P, n, P]):
                             eng.tensor_tensor(out=o, in0=i, in1=sc, op=ALU.mult)), reads=[psr, scale_res], writes=self.L(dst_res))
            c += n

    def rstd(self, ss_ap, out_ap, factor, res_in, res_out):
        S = self.S
        f2 = factor * factor
        S.op("dve", lambda eng: eng.tensor_scalar(out=out_ap, in0=ss_ap, scalar1=1.0 / (D * f2), scalar2=EPS / f2,
                                                  op0=ALU.mult, op1=ALU.add), reads=[res_in], writes=[res_out])
        S.op("act", lambda eng: eng.sqrt(out=out_ap, in_=out_ap), reads=[res_out], writes=[res_out])
        S.op("dve", lambda eng: eng.reciprocal(out=out_ap, in_=out_ap), reads=[res_out], writes=[res_out])

    def prenorm(self, x, xr, gi, hnT=None, hres="hnT"):
        S = self.S
        hnT = self.hnT if hnT is None else hnT
        S.op("act", lambda eng: eng.activation(out=self.junk[:], in_=x[:], func=AF.Square, accum_out=self.st[:, 0:1]),
             reads=[xr], writes=["xn", "st0"])
        self.rstd(self.st[:, 0:1], self.st[:, 1:2], 1.0, "st0", "st1")
        S.op("dve", lambda eng: eng.tensor_scalar(out=self.xn[:], in0=x[:], scalar1=self.st[:, 1:2], scalar2=None,
                                                  op0=ALU.mult), reads=[xr, "st1"], writes=["xn"])
        self.transpose_to(self.xn, "xn", 8, hnT, hres, scale=self.gpre[:, gi, :])

    def postnorm_evac(self, x, xr, gi, factor, sc=2):
        S = self.S
        held = []

        def evac(ps, psr, cb, n):
            k = len(held)
            S.op("act", lambda eng: eng.activation(out=self.junk[:, cb:cb + n], in_=ps[:, 0:n], func=AF.Square,
                                                   accum_out=self.st[:, sc + k:sc + 1 + k]),
                 reads=[psr], writes=["xn", f"st{sc + k}"])
            held.append((ps, psr, cb, n))

        def fin():
            S.op("dve", lambda eng: eng.tensor_tensor(out=self.st[:, sc + 2:sc + 3], in0=self.st[:, sc:sc + 1], in1=self.st[:, sc + 1:sc + 2],
                                                      op=ALU.add), reads=[f"st{sc}", f"st{sc + 1}"], writes=[f"st{sc + 2}"])
            self.rstd(self.st[:, sc + 2:sc + 3], self.st[:, sc + 3:sc + 4], factor, f"st{sc + 2}", f"st{sc + 3}")
            for ps, psr, cb, n in held:
                S.op("dve", lambda eng, ps=ps, cb=cb, n=n: eng.scalar_tensor_tensor(
                    out=self.tmp[:, 0:n], in0=ps[:, 0:n], scalar=self.st[:, sc + 3:sc + 4], in1=self.gpost[:, gi, cb:cb + n],
                    op0=ALU.mult, op1=ALU.mult), reads=[psr, f"st{sc + 3}", "gpost"], writes=["tmp"])
                S.op("dve", lambda eng, cb=cb, n=n: eng.tensor_tensor(out=x[:, cb:cb + n], in0=x[:, cb:cb + n],
                                                                     in1=self.tmp[:, 0:n], op=ALU.add),
                     reads=["tmp", xr], writes=[xr])
        return evac, fin

    def ffn(self, tiles, which):
        S = self.S
        pre, post = (0, 0) if which == 1 else (2, 2)
        gname, uname, dname = f"ffn{which}_gate", f"ffn{which}_up", f"ffn{which}_down"
        self.nring = NSLOT + 1
        bufs = [(self.hnT, "hnT", self.sg, "sg", self.act, "zs", self.actT, "y"),
                (self.mrgT, "mrgT", self.tmp, "tmp", self.act2, "ytmp", self.actT2, ("cacc0", "cacc1"))][:len(tiles)]
        for (x, xr), b in zip(tiles, bufs):
            self.prenorm(x, xr, pre, b[0], b[1])

        def mk_gate(b):
            def ev(ps, psr, cb, n):
                S.op("act", lambda eng, ps=ps: eng.activation(out=b[2][:, 0:n], in_=ps[:, 0:n], func=AF.Silu), reads=[psr], writes=[b[3]])
            return ev

        def mk_up(b, cb0):
            def ev(ps, psr, cb, n):
                S.op("dve", lambda eng, ps=ps: eng.tensor_tensor(out=b[4][:, cb0:cb0 + n], in0=ps[:, 0:n], in1=b[2][:, 0:n], op=ALU.mult),
                     reads=[psr, b[3]], writes=[b[5]])
            return ev
        acts = [(b[0], b[1]) for b in bufs]
        cb = 0
        while cb < DFF:
            n = min(512, DFF - cb)
            self.linear_multi(acts, 8, gname, cb, n, [mk_gate(b) for b in bufs])
            self.linear_multi(acts, 8, uname, cb, n, [mk_up(b, cb) for b in bufs])
            cb += n
        for b in bufs:
            self.transpose_to(b[4], b[5], 22, b[6], b[7])
        pn = [self.postnorm_evac(x, xr, post, 0.5, sc=2 + 6 * t) for t, (x, xr) in enumerate(tiles)]
        self.linear_multi([(b[6], b[7]) for b in bufs], 22, dname, 0, D, [p[0] for p in pn])
        for p in pn:
            p[1]()

    def bc(self, ap24, n):
        return ap24.unsqueeze(2).broadcast_to([P, ap24.shape[1], n])

    def mixer(self, x, xr, g, kind):
        S = self.S
        full = kind != "pre"
        do_kv = full or g >= self.n_pre - self.n_kvpre
        oi = g - self.n_pre if kind == "own" else -1
        bi = g - self.n_pre - self.n_own if kind == "smp" else -1
        (gates, gr), (mrg, mr) = self.cur_g
        V = lambda t, a, b2: t[:].rearrange("p (a b) -> p a b", a=a, b=b2)
        self.prenorm(x, xr, 1)
        triU = self.cstt[:, P:2 * P]
        ones = self.cstt[:, 2 * P:3 * P]
        slots = [0 if kind == "smp" else g % nt for nt in self.NT]
        if STOP <= 0:
            return
        def ev_x(ps, psr, cb, n):
            S.op("act", lambda eng, ps=ps: eng.copy(out=self.xraw[:, cb:cb + n], in_=ps[:, 0:n]), reads=[psr], writes=["xraw"])
        self.linear(self.hnT, "hnT", 8, "w_in", OFF_XBC, 2560, ev_x)
        for hf in range(2):
            c0 = hf * 10
            CE = "dve" if hf == 0 else "pool"
            ctmp, ctr = (self.ctmp, "ytmp") if hf == 0 else (self.ctmp1, "y")
            stg, cacc = self.stg[hf], self.cacc[hf]
            sr, ar = f"stg{hf}", f"cacc{hf}"
            S.op(CE, lambda eng, c0=c0, stg=stg: eng.tensor_copy(out=stg[:, :, 0:3], in_=self.carry[:, c0:c0 + 10, :]), reads=["carry"], writes=[sr])
            self.transpose_to(self.xraw[:, c0 * P:(c0 + 10) * P], "xraw", 10, stg[:, :, 3:131], sr)
            S.op(CE, lambda eng, c0=c0, stg=stg: eng.tensor_copy(out=self.carry[:, c0:c0 + 10, :], in_=stg[:, :, 128:131]), reads=[sr], writes=["carry"])
            for tap in range(4):
                wv = self.cw[:, c0:c0 + 10, tap:tap + 1].broadcast_to([P, 10, P])
                if tap == 0:
                    S.op(CE, lambda eng, wv=wv, stg=stg, cacc=cacc: eng.tensor_tensor(out=cacc[:], in0=stg[:, :, 0:P], in1=wv, op=ALU.mult), reads=[sr, "cw"], writes=[ar])
                else:
                    S.op(CE, lambda eng, wv=wv, stg=stg, tap=tap, ctmp=ctmp: eng.tensor_tensor(out=ctmp, in0=stg[:, :, tap:tap + P], in1=wv, op=ALU.mult), reads=[sr, "cw"], writes=[ctr])
                    S.op(CE, lambda eng, cacc=cacc, ctmp=ctmp: eng.tensor_tensor(out=cacc[:], in0=cacc[:], in1=ctmp, op=ALU.add), reads=[ar, ctr], writes=[ar])
            S.op(CE, lambda eng, c0=c0, cacc=cacc: eng.tensor_tensor(out=cacc[:], in0=cacc[:], in1=self.cw[:, c0:c0 + 10, 4:5].broadcast_to([P, 10, P]), op=ALU.add),
                 reads=[ar, "cw"], writes=[ar])
        for dg in range(3):
            if not do_kv:
                break
            base = dg * 1536
            sl = slots[dg]
            need_rows = kind == "smp" or (kind == "own" and self.WIN[dg] - (self.n_own - oi) * P >= 0)
            if full:
                def ev_q(ps, psr, cb, n):
                    S.op("act", lambda eng, ps=ps: eng.mul(out=self.qb[:], in_=ps[:, 0:512], mul=0.125), reads=[psr], writes=["qb"])
                self.linear(self.hnT, "hnT", 8, "w_in", base, 512, ev_q)
                self.transpose_to(self.qb, "qb", 4, self.qT[:, dg * 4:(dg + 1) * 4, :], "qT")

            def ev_k(ps, psr, cb, n):
                S.op("act", lambda eng, ps=ps: eng.copy(out=self.qb2[:], in_=ps[:, 0:512]), reads=[psr], writes=["qb2"])
                if need_rows:
                    S.op("dve", lambda eng, ps=ps: eng.tensor_copy(out=mrg[:, 0:512], in_=ps[:, 0:512]), reads=[psr], writes=[mr])
            self.linear(self.hnT, "hnT", 8, "w_in", base + 512, 512, ev_k)
            self.transpose_to(self.qb2, "qb2", 4, self.kTh[dg][:, sl, :, :], f"kTh{dg}_{sl}")

            def ev_v(ps, psr, cb, n, dg=dg, sl=sl):
                S.op("act", lambda eng, ps=ps: eng.copy(out=self.Vh[dg][:, sl, :, 0:64], in_=ps[:, 0:512].rearrange("p (h e) -> p h e", h=8)),
                     reads=[psr], writes=[f"Vh{dg}_{sl}"])
                if need_rows:
                    S.op("dve", lambda eng, ps=ps: eng.tensor_copy(out=mrg[:, 512:1024], in_=ps[:, 0:512]), reads=[psr], writes=[mr])
            self.linear(self.hnT, "hnT", 8, "w_in", base + 1024, 512, ev_v)
            if kind == "pre":
                S.op("dve", lambda eng, dg=dg, sl=sl: eng.tensor_copy(out=self.Vh[dg][:, sl, :, 64:65], in_=self.flg[:, 0:1].unsqueeze(1).broadcast_to([P, 8, 1])),
                     reads=["flg"], writes=[f"Vh{dg}_{sl}"])
            elif kind == "own":
                S.op("dve", lambda eng, dg=dg, sl=sl: eng.memset(self.Vh[dg][:, sl, :, 64:65], 1.0), writes=[f"Vh{dg}_{sl}"])
                r0 = self.WIN[dg] - (self.n_own - oi) * P
                if r0 >= 0:
                    S.dma("pool", f"kvo{dg}", self.kvo[dg][r0:r0 + P, :], mrg[:], reads=[mr], writes=[])
            else:
                W = self.WIN[dg]
                S.dma("pool", f"kvo{dg}", self.kvs[dg][bi, W - 4:W, :], mrg[0:4, :], reads=[mr], writes=[])
        if STOP == 10:
            return
        if full:
            def ev_z(ps, psr, cb, n):
                S.op("act", lambda eng: eng.activation(out=self.zs[:, cb:cb + n], in_=ps[:, 0:n], func=AF.Silu), reads=[psr], writes=["zs"])
            self.linear(self.hnT, "hnT", 8, "w_in", OFF_Z, 1536, ev_z)
        for hf in range(2):
            S.op("act", lambda eng, hf=hf: eng.activation(out=self.xc[:, hf * 10:(hf + 1) * 10, :], in_=self.cacc[hf][:], func=AF.Silu), reads=[f"cacc{hf}"], writes=["xc"])
        self.transpose_to(self.xc[:].rearrange("p a b -> p (a b)"), "xc", 16, self.xstm, "xstm")
        if STOP == 11:
            return
        sm = self.sm
        def ev_dt(ps, psr, cb, n):
            S.op("dve", lambda eng: eng.tensor_tensor(out=sm[:, 0, :], in0=ps[:, 0:24], in1=self.hv[:, 0, :], op=ALU.add), reads=[psr, "hv"], writes=["sm0"])
        self.linear(self.hnT, "hnT", 8, "w_in", OFF_DT, 24, ev_dt)
        S.op("act", lambda eng: eng.activation(out=sm[:, 0, :], in_=sm[:, 0, :], func=AF.Exp), reads=["sm0"], writes=["sm0"])
        S.op("act", lambda eng: eng.activation(out=sm[:, 1, :], in_=sm[:, 0, :], func=AF.Ln, bias=1.0), reads=["sm0"], writes=["sm1"])
        if kind != "own":
            fc = 0 if kind == "pre" else 1
            S.op("dve", lambda eng: eng.tensor_scalar(out=sm[:, 1, :], in0=sm[:, 1, :], scalar1=self.flg[:, fc:fc + 1], scalar2=None, op0=ALU.mult), reads=["sm1", "flg"], writes=["sm1"])
        S.op("dve", lambda eng: eng.tensor_tensor(out=sm[:, 2, :], in0=sm[:, 1, :], in1=self.hv[:, 1, :], op=ALU.mult), reads=["sm1", "hv"], writes=["sm2"])
        self.dump("zs", self.zs[:], "zs", g)
        self.dump("xc", self.xc[:], "xc", g)
        self.dump("xstm", self.xstm[:], "xstm", g)
        if STOP <= 1:
            return
        ps1, p1r = self.bank()
        S.op("pe", lambda eng: eng.matmul(ps1[:, 0:24], lhsT=triU, rhs=sm[:, 2, :], start=True, stop=True), reads=["sm2", "cstt"], writes=[p1r], inc=False)
        S.op("pe", lambda eng: eng.matmul(ps1[:, 32:56], lhsT=ones, rhs=sm[:, 2, :], start=True, stop=True), reads=["sm2", "cstt"], writes=[p1r])
        S.op("act", lambda eng: eng.copy(out=sm[:, 3, :], in_=ps1[:, 0:24]), reads=[p1r], writes=["sm3"])
        S.op("dve", lambda eng: eng.tensor_scalar(out=sm[:, 4, :], in0=ps1[:, 0:24], scalar1=-1.0, scalar2=None, op0=ALU.mult), reads=[p1r], writes=["sm4"])
        S.op("act", lambda eng: eng.activation(out=sm[:, 5, :], in_=ps1[:, 0:24], func=AF.Exp), reads=[p1r], writes=["sm5"])
        S.op("dve", lambda eng: eng.tensor_tensor(out=sm[:, 6, :], in0=ps1[:, 32:56], in1=sm[:, 3, :], op=ALU.subtract), reads=[p1r, "sm3"], writes=["sm6"])
        S.op("act", lambda eng: eng.activation(out=sm[:, 6, :], in_=sm[:, 6, :], func=AF.Exp), reads=["sm6"], writes=["sm6"])
        S.op("dve", lambda eng: eng.tensor_tensor(out=sm[:, 6, :], in0=sm[:, 6, :], in1=sm[:, 1, :], op=ALU.mult), reads=["sm6", "sm1"], writes=["sm6"])
        S.op("act", lambda eng: eng.activation(out=sm[:, 7, :], in_=ps1[:, 32:56], func=AF.Exp), reads=[p1r], writes=["sm7"])
        xs3 = self.xstm[:, 0:12, :].rearrange("p a (h e) -> p (a h) e", h=2)
        S.op("dve", lambda eng: eng.tensor_tensor(out=self.xd, in0=xs3, in1=self.bc(sm[:, 1, :], 64), op=ALU.mult), reads=["xstm", "sm1"], writes=["xd"])
        S.op("pool", lambda eng: eng.tensor_tensor(out=self.xdp, in0=xs3, in1=self.bc(sm[:, 6, :], 64), op=ALU.mult), reads=["xstm", "sm6"], writes=["xdp"])
        if full:
            ps2, p2r = self.bank()
            for gg in range(4):
                S.op("pe", lambda eng, gg=gg: eng.matmul(ps2[:, gg * P:(gg + 1) * P], lhsT=self.xc[:, 12 + gg, :], rhs=self.xc[:, 16 + gg, :], start=True, stop=True),
                     reads=["xc"], writes=[p2r], inc=(gg == 3))
            S.op("dve", lambda eng: eng.tensor_tensor(out=self.cbm[:], in0=ps2[:].rearrange("p (a b) -> p a b", a=4), in1=triU.unsqueeze(1).broadcast_to([P, 4, P]), op=ALU.mult),
                 reads=[p2r, "cstt"], writes=["cbm"])
            for gg in range(4):
                po, por = self.bank()
                S.op("pe", lambda eng, gg=gg, po=po: eng.matmul(po[:, 0:384], lhsT=self.xc[:, 16 + gg, :], rhs=self.STb[:, gg * 384:(gg + 1) * 384], start=True, stop=True),
                     reads=["xc", "STb"], writes=[por])
                S.op("dve", lambda eng, gg=gg, po=po: eng.tensor_tensor(out=self.y[:, gg * 384:(gg + 1) * 384].rearrange("p (h e) -> p h e", h=6),
                                                                         in0=po[:, 0:384].rearrange("p (h e) -> p h e", h=6),
                                                                         in1=self.bc(sm[:, 5, gg * 6:(gg + 1) * 6], 64), op=ALU.mult),
                     reads=[por, "sm5"], writes=["y"])
            for hh in range(2):
                for hb in range(3):
                    pd, pdr = self.bank()
                    for jj in range(4):
                        h = hh * 12 + hb * 4 + jj
                        S.op("pe", lambda eng, h=h, jj=jj, pd=pd: eng.matmul(pd[:, jj * P:(jj + 1) * P], lhsT=sm[:, 2, h:h + 1].broadcast_to([P, P]), rhs=triU, start=True, stop=True),
                             reads=["sm2", "cstt"], writes=[pdr], inc=(jj == 3))
                    for jj in range(4):
                        h = hh * 12 + hb * 4 + jj
                        S.op("dve", lambda eng, h=h, jj=jj, pd=pd: eng.tensor_scalar(out=self.Dt[:, jj, :], in0=pd[:, jj * P:(jj + 1) * P], scalar1=sm[:, 4, h:h + 1], scalar2=0.0,
                                                                                   op0=ALU.add, op1=ALU.min), reads=[pdr, "sm4"], writes=["Dt"])
                    S.op("act", lambda eng: eng.activation(out=self.Lt[:], in_=self.Dt[:], func=AF.Exp), reads=["Dt"], writes=["Lt"])
                    for jj in range(4):
                        h = hh * 12 + hb * 4 + jj
                        S.op("pool", lambda eng, h=h, jj=jj, hb=hb: eng.tensor_tensor(out=self.Mt[:, hb * 4 + jj, :], in0=self.Lt[:, jj, :], in1=self.cbm[:, h // 6, :], op=ALU.mult),
                             reads=["Lt", "cbm"], writes=["Mt"])
                py, pyr = [], []
                for k in range(2):
                    a, b2 = self.bank()
                    py.append(a); pyr.append(b2)
                for j12 in range(12):
                    h = hh * 12 + j12
                    S.op("pe", lambda eng, h=h, j12=j12, py=py: eng.matmul(py[j12 // 8][:, (j12 % 8) * 64:(j12 % 8) * 64 + 64], lhsT=self.Mt[:, j12, :], rhs=self.xd[:, h, :], start=True, stop=True),
                         reads=["Mt", "xd"], writes=[pyr[j12 // 8]], inc=(j12 in (7, 11)))
                S.op("dve", lambda eng, hh=hh, py=py: eng.tensor_tensor(out=self.y[:, hh * 768:hh * 768 + 512], in0=self.y[:, hh * 768:hh * 768 + 512], in1=py[0][:, 0:512], op=ALU.add),
                     reads=["y", pyr[0]], writes=["y"])
                S.op("dve", lambda eng, hh=hh, py=py: eng.tensor_tensor(out=self.y[:, hh * 768 + 512:hh * 768 + 768], in0=self.y[:, hh * 768 + 512:hh * 768 + 768], in1=py[1][:, 0:256], op=ALU.add),
                     reads=["y", pyr[1]], writes=["y"])
        for gg in range(4):
            pS, pSr = self.bank()
            S.op("pe", lambda eng, gg=gg, pS=pS: eng.matmul(pS[:, 0:384], lhsT=self.xstm[:, 12 + gg, :], rhs=self.xdp[:, gg * 6:(gg + 1) * 6, :].rearrange("p h e -> p (h e)"), start=True, stop=True),
                 reads=["xstm", "xdp"], writes=[pSr])
            stv = self.ST[:, gg * 384:(gg + 1) * 384].rearrange("p (h e) -> p h e", h=6)
            S.op("pool", lambda eng, gg=gg, stv=stv: eng.tensor_tensor(out=stv, in0=stv, in1=self.bc(sm[:, 7, gg * 6:(gg + 1) * 6], 64), op=ALU.mult), reads=["ST", "sm7"], writes=["ST"])
            S.op("dve", lambda eng, gg=gg, pS=pS: eng.tensor_tensor(out=self.ST[:, gg * 384:(gg + 1) * 384], in0=self.ST[:, gg * 384:(gg + 1) * 384], in1=pS[:, 0:384], op=ALU.add),
                 reads=["ST", pSr], writes=["ST"])
        S.op("act", lambda eng: eng.copy(out=self.STb[:], in_=self.ST[:]), reads=["ST"], writes=["STb"])
        if kind == "smp":
            self.store_conv(1 + bi, 1)
            self.store_state(1 + bi)
        elif kind == "own" and oi == self.n_own - 1:
            self.store_conv(0, 125)
            self.store_state(0)
        self.dump("sm", self.sm[:], "sm7", g)
        self.dump("ST", self.ST[:], "ST", g)
        if not full:
            return
        self.dump("y0", self.y[:], "y", g)
        if STOP <= 2:
            return
        t3 = self.ytmp[:].rearrange("p (h e) -> p h e", h=24)
        S.op("pool", lambda eng: eng.tensor_tensor(out=t3, in0=xs3, in1=self.bc(self.hv[:, 2, :], 64), op=ALU.mult), reads=["xstm", "hv"], writes=["ytmp"])
        S.op("dve", lambda eng: eng.tensor_tensor(out=self.y[:], in0=self.y[:], in1=self.ytmp[:], op=ALU.add), reads=["y", "ytmp"], writes=["y"])
        S.op("dve", lambda eng: eng.tensor_tensor(out=self.y[:], in0=self.y[:], in1=self.zs[:], op=ALU.mult), reads=["y", "zs"], writes=["y"])
        for gg in range(4):
            S.op("act", lambda eng, gg=gg: eng.activation(out=self.ytmp[:, gg * 384:(gg + 1) * 384], in_=self.y[:, gg * 384:(gg + 1) * 384], func=AF.Square, accum_out=sm[:, 8, gg:gg + 1]),
                 reads=["y"], writes=["ytmp", "sm8"])
        S.op("dve", lambda eng: eng.tensor_scalar(out=sm[:, 8, 0:4], in0=sm[:, 8, 0:4], scalar1=1.0 / 384, scalar2=EPS, op0=ALU.mult, op1=ALU.add), reads=["sm8"], writes=["sm8"])
        S.op("act", lambda eng: eng.sqrt(out=sm[:, 8, 0:4], in_=sm[:, 8, 0:4]), reads=["sm8"], writes=["sm8"])
        S.op("dve", lambda eng: eng.reciprocal(out=sm[:, 8, 0:4], in_=sm[:, 8, 0:4]), reads=["sm8"], writes=["sm8"])
        S.op("dve", lambda eng: eng.tensor_tensor(out=self.yn[:].rearrange("p (g e) -> p g e", g=4), in0=self.y[:].rearrange("p (g e) -> p g e", g=4), in1=self.bc(sm[:, 8, 0:4], 384), op=ALU.mult),
             reads=["y", "sm8"], writes=["yn"])
        self.dump("yn", self.yn[:], "yn", g)
        self.transpose_to(self.yn, "yn", 12, self.ynT, "ynT", scale=self.nw, scale_res="nw")
        def ev_gate(off):
            def ev(ps, psr, cb, n):
                S.op("act", lambda eng: eng.activation(out=gates[:, cb:cb + n], in_=ps[:, 0:n], func=AF.Sigmoid), reads=[psr], writes=[gr])
            return ev
        self.linear(self.hnT, "hnT", 8, "w_in", OFF_GATE + 1024, 1024, ev_gate(1024))
        def ev_b(ps, psr, cb, n):
            S.op("dve", lambda eng: eng.tensor_tensor(out=mrg[:, cb:cb + n], in0=ps[:, 0:n], in1=gates[:, cb:cb + n], op=ALU.mult), reads=[psr, gr], writes=[mr])
        self.linear(self.ynT, "ynT", 12, "w_branch_b", 0, 1024, ev_b)
        self.dump(mr, mrg[:], mr, g)
        if STOP <= 3:
            return
        mbase = [0, 2, 7]
        if kind == "smp":
            tiles = [(dg, o, mbase[dg] + o, o) for dg in range(3) for o in range(self.NT[dg])]
        else:
            first = self.n_pre - self.n_kvpre
            tiles = [(dg, o, mbase[dg] + o, (g - o) % self.NT[dg]) for dg in range(3) for o in range(self.NT[dg]) if g - o >= first]
        self.attention(tiles)
        self.dump("oa", self.oa[:], "oa", g)
        self.transpose_to(self.oa, "oa", 4, self.oaT, "oaT")
        self.linear(self.hnT, "hnT", 8, "w_in", OFF_GATE, 1024, ev_gate(0))
        def ev_a(ps, psr, cb, n):
            S.op("dve", lambda eng: eng.tensor_tensor(out=self.tmp[:, 0:n], in0=ps[:, 0:n], in1=gates[:, cb:cb + n], op=ALU.mult), reads=[psr, gr], writes=["tmp"])
            S.op("dve", lambda eng: eng.tensor_tensor(out=self.mrgb[:, cb:cb + n], in0=mrg[:, cb:cb + n], in1=self.tmp[:, 0:n], op=ALU.add), reads=[mr, "tmp"], writes=["mrgb"])
        self.linear(self.oaT, "oaT", 4, "w_branch_a", 0, 1024, ev_a)
        self.dump("mrgb", self.mrgb[:], "mrgb", g)
        self.transpose_to(self.mrgb, "mrgb", 8, self.mrgT, "mrgT")
        evac, fin = self.postnorm_evac(x, xr, 1, 1.0)
        self.linear(self.mrgT, "mrgT", 8, "w_out", 0, D, evac)
        fin()

    def attention(self, tiles):
        S = self.S
        outb = [self.bank() for _ in range(2)]
        scb = [self.bank() for _ in range(4)]
        PT = self.PT
        r = 0
        for h in range(8):
            c, pb = h // 2, 64 * (h % 2)
            ob, obr = outb[h // 4]
            oc = (h % 4) * 65
            for b0 in range(0, len(tiles), 8):
                batch = tiles[b0:b0 + 8]
                nb = len(batch)
                banks = scb[2 * (r % 2):2 * (r % 2) + 2]
                r += 1
                for j, (dg, o, m, sl) in enumerate(batch):
                    ps, psr = banks[j // 4]
                    S.op("pe", lambda eng, j=j, dg=dg, sl=sl, ps=ps, pb=pb, c=c: eng.matmul(ps[:, (j % 4) * P:(j % 4 + 1) * P], lhsT=self.kTh[dg][pb:pb + 64, sl, c, :], rhs=self.qT[pb:pb + 64, dg * 4 + c, :], start=True, stop=True),
                         reads=[f"kTh{dg}_{sl}", "qT"], writes=[psr], inc=(j % 4 == 3 or j == nb - 1))
                for k in range((nb + 3) // 4):
                    ps, psr = banks[k]
                    n4 = min(4, nb - 4 * k)
                    pr = f"PT{k}"
                    S.op("act", lambda eng, ps=ps, k=k, n4=n4: eng.activation(out=PT[:, 4 * k:4 * k + n4, :], in_=ps[:, 0:n4 * P].rearrange("p (a b) -> p a b", a=n4), func=AF.Exp), reads=[psr], writes=[pr])
                    ms = [t[2] for t in batch[4 * k:4 * k + n4]]
                    if ms == list(range(ms[0], ms[0] + n4)):
                        S.op("dve", lambda eng, k=k, n4=n4, m0=ms[0]: eng.tensor_tensor(out=PT[:, 4 * k:4 * k + n4, :], in0=PT[:, 4 * k:4 * k + n4, :], in1=self.mask[:, m0:m0 + n4, :], op=ALU.mult), reads=[pr, "mask"], writes=[pr])
                    else:
                        for j, m in enumerate(ms):
                            S.op("dve", lambda eng, j=4 * k + j, m=m: eng.tensor_tensor(out=PT[:, j, :], in0=PT[:, j, :], in1=self.mask[:, m, :], op=ALU.mult), reads=[pr, "mask"], writes=[pr])
                for j, (dg, o, m, sl) in enumerate(batch):
                    first = (b0 + j == 0)
                    last = (b0 + j == len(tiles) - 1)
                    S.op("pe", lambda eng, j=j, dg=dg, sl=sl, first=first, last=last, ob=ob, oc=oc, h=h: eng.matmul(ob[:, oc:oc + 65], lhsT=PT[:, j, :], rhs=self.Vh[dg][:, sl, h, :], start=first, stop=last),
                         reads=[f"PT{j // 4}", f"Vh{dg}_{sl}"], writes=[obr], inc=(j % 4 == 3 or j == nb - 1))
        for k in range(2):
            ob, obr = outb[k]
            ob3 = ob[:, 0:260].rearrange("p (h e) -> p h e", e=65)
            S.op("dve", lambda eng, ob3=ob3, k=k: eng.reciprocal(out=self.sm[:, 9, 4 * k:4 * k + 4].unsqueeze(2), in_=ob3[:, :, 64:65]), reads=[obr], writes=["sm9"])
            S.op("dve", lambda eng, ob3=ob3, k=k: eng.tensor_tensor(out=self.oa[:, 256 * k:256 * k + 256].rearrange("p (h e) -> p h e", e=64), in0=ob3[:, :, 0:64],
                                                                   in1=self.bc(self.sm[:, 9, 4 * k:4 * k + 4], 64), op=ALU.mult), reads=[obr, "sm9"], writes=["oa"])

    def store_conv(self, idx, r0):
        self.S.dma("pool", "cvo", self.convo[idx], self.xraw[r0:r0 + 3, :], reads=["xraw"], writes=[])

    def store_state(self, idx):
        S = self.S
        id32 = self.cstt[:, 0:P]
        for c0 in range(0, 12, 4):
            ps, psr = self.bank()
            for j in range(4):
                S.op("pe", lambda eng, ps=ps, j=j, c=c0 + j: eng.transpose(ps[:, j * P:(j + 1) * P], self.ST[:, c * P:(c + 1) * P], id32),
                     reads=["ST", "cstt"], writes=[psr], inc=(j == 3))
            S.op("act", lambda eng, ps=ps, c0=c0: eng.copy(out=self.ytmp[:, c0 * P:(c0 + 4) * P], in_=ps[:, 0:512]), reads=[psr], writes=["ytmp"])
        S.dma("pool", "sso", self.ssmo[idx].rearrange("(c p) n -> p c n", p=P), self.ytmp[:].rearrange("p (c n) -> p c n", c=12), reads=["ytmp"], writes=[])

    def load_sample(self, bi):
        S = self.S
        id32 = self.cstt[:, 0:P]
        for dg in range(3):
            nt = self.NT[dg]
            for j in range(nt - 1):
                sl = nt - 1 - j
                src = self.cache[dg][bi, j * P:(j + 1) * P, :]
                kb, kr = (self.qb, "qb") if j % 2 == 0 else (self.qb2, "qb2")
                S.dma("pool", f"ck{j % 2}", kb[:], src[:, 0:512], reads=[], writes=[kr])
                self.transpose_to(kb, kr, 4, self.kTh[dg][:, sl, :, :], f"kTh{dg}_{sl}")
                S.dma("pool", f"cv{dg}_{sl}", self.Vh[dg][:, sl, :, 0:64], src[:, 512:1024].rearrange("p (h e) -> p h e", h=8),
                      reads=[], writes=[f"Vh{dg}_{sl}"])
        S.dma("sp", "ssi", self.ytmp[:].rearrange("p (c n) -> p c n", c=12), self.sssm[bi].rearrange("(c p) n -> p c n", p=P), reads=[], writes=["ytmp"])
        for c0 in range(0, 12, 4):
            ps, psr = self.bank()
            for j in range(4):
                S.op("pe", lambda eng, ps=ps, j=j, c=c0 + j: eng.transpose(ps[:, j * P:(j + 1) * P], self.ytmp[:, c * P:(c + 1) * P], id32),
                     reads=["ytmp", "cstt"], writes=[psr], inc=(j == 3))
            S.op("act", lambda eng, ps=ps, c0=c0: eng.copy(out=self.ST[:, c0 * P:(c0 + 4) * P], in_=ps[:, 0:512]), reads=[psr], writes=["ST"])
        S.op("act", lambda eng: eng.copy(out=self.STb[:], in_=self.ST[:]), reads=["ST"], writes=["STb"])
        for hf in range(2):
            S.dma("sp", "sci", self.cacc[0][0:3, :, :].rearrange("p a b -> p (a b)"), self.sconv[bi][:, hf * 1280:(hf + 1) * 1280], reads=[], writes=["cacc0"])
            ps, psr = self.bank()
            for c in range(10):
                S.op("pe", lambda eng, ps=ps, c=c: eng.matmul(ps[:, c * 3:(c + 1) * 3], lhsT=self.cacc[0][0:3, c, :], rhs=self.cstt[0:3, 0:3], start=True, stop=True),
                     reads=["cacc0", "cstt"], writes=[psr], inc=(c == 9))
            S.op("act", lambda eng, ps=ps, hf=hf: eng.copy(out=self.carry[:, hf * 10:(hf + 1) * 10, :], in_=ps[:, 0:30].rearrange("p (c t) -> p c t", t=3)),
                 reads=[psr], writes=["carry"])

    def setup(self):
        S = self.S
        S.dma("sp", "c0", self.cstt[:], self.cst, reads=[], writes=["cstt"])
        S.dma("sp", "c1", self.gpost[:], self.gvec[3:6, :].partition_broadcast(P), reads=[], writes=["gpost"])
        S.dma("sp", "c2", self.gpre[:], self.gpre_d.rearrange("p (g c) -> p g c", g=3), reads=[], writes=["gpre"])
        S.op("dve", lambda eng: eng.tensor_copy(out=self.identb[:], in_=self.cstt[:, 0:P]), reads=["cstt"], writes=["identb"])
        S.dma("sp", "c3", self.cw[:], self.cw_d.rearrange("p (c t) -> p c t", t=5), reads=[], writes=["cw"])
        S.dma("sp", "c4", self.hv[:], self.hv_d.partition_broadcast(P), reads=[], writes=["hv"])
        S.dma("sp", "c5", self.nw[:], self.nw_d, reads=[], writes=["nw"])
        S.op("act", lambda eng: eng.activation(out=self.hv[:, 1, :], in_=self.hv[:, 1, :], func=AF.Exp), reads=["hv"], writes=["hv"])
        S.op("dve", lambda eng: eng.tensor_scalar(out=self.hv[:, 1, :], in0=self.hv[:, 1, :], scalar1=-1.0, scalar2=None, op0=ALU.mult), reads=["hv"], writes=["hv"])
        for i in range(6):
            S.dma("sp", "c6", self.maskf[:], self.mask_d[:, i * 512:(i + 1) * 512], reads=[], writes=["maskf"])
            S.op("dve", lambda eng, i=i: eng.tensor_copy(out=self.mask[:, i * 4:(i + 1) * 4, :], in_=self.maskf[:].rearrange("p (a b) -> p a b", a=4)), reads=["maskf"], writes=["mask"])
        S.op("dve", lambda eng: eng.memset(self.ST[:], 0.0), writes=["ST"])
        S.op("dve", lambda eng: eng.memset(self.STb[:], 0.0), writes=["STb"])
        S.op("dve", lambda eng: eng.memset(self.carry[:], 0.0), writes=["carry"])
        for dg in range(3):
            S.op("dve", lambda eng, dg=dg: eng.memset(self.Vh[dg][:], 1.0), writes=[f"Vh{dg}_{sl}" for sl in range(self.NT[dg])])
        S.dma("sp", "c7", self.flg[:], self.flg_d, reads=[], writes=["flg"])
        for dg in range(3):
            W = self.WIN[dg]
            for b in range(self.n_smp):
                for r0 in range(4, W, 256):
                    r1 = min(W, r0 + 256)
                    S.dma("act", "kvcp", self.kvs[dg][b, r0 - 4:r1 - 4, :], self.cache[dg][b, r0:r1, :], reads=[], writes=[])

    def load_x(self, gs, bufs):
        for t, g in enumerate(gs):
            self.S.dma("sp", f"xin{t}", bufs[t][0][:], self.xs[g * P:(g + 1) * P, :], reads=[], writes=[bufs[t][1]])

    def pair(self, gs, nxt):
        S = self.S
        kind = self.kinds[gs[0]]
        tiles = self.cur_x[:len(gs)]
        self.ffn(tiles, 1)
        if kind == "pre" and nxt:
            self.load_x(nxt, self.cur_g)
        for (x, xr), g in zip(tiles, gs):
            if kind == "smp":
                bi = g - self.n_pre - self.n_own
                if bi == 0:
                    for dg in range(3):
                        S.op("dve", lambda eng, dg=dg: eng.memset(self.Vh[dg][:, :, :, 64:65], 1.0), writes=[f"Vh{dg}_{sl}" for sl in range(self.NT[dg])])
                self.load_sample(bi)
            self.dump("h1", x[:], xr, g, DBG_G)
            self.mixer(x, xr, g, kind)
            self.dump("h2", x[:], xr, g, DBG_G)
        if kind != "pre":
            if nxt:
                self.load_x(nxt, self.cur_g)
            self.ffn(tiles, 2)
            for (x, xr), g in zip(tiles, gs):
                o = g - self.n_pre
                S.dma("pool", f"yout{o % 2}", self.ys[o * P:(o + 1) * P, :], x[:], reads=[xr], writes=[])
        self.cur_x, self.cur_g = self.cur_g, self.cur_x

    def build(self):
        self.setup()
        pairs = []
        g = 0
        while g < len(self.kinds):
            gs = [g]
            if g + 1 < len(self.kinds) and self.kinds[g + 1] == self.kinds[g]:
                gs.append(g + 1)
            pairs.append(gs)
            g += len(gs)
        self.load_x(pairs[0], self.cur_x)
        for i, gs in enumerate(pairs):
            self.pair(gs, pairs[i + 1] if i + 1 < len(pairs) else None)
        self.S.finish()
        self.S.emit()
        return self.nc


PARAMS = ["w_in", "conv_w", "conv_b", "dt_bias", "a_log", "d_skip", "ssd_norm_w", "w_branch_a", "w_branch_b", "w_out",
          "ffn1_gate", "ffn1_up", "ffn1_down", "ffn2_gate", "ffn2_up", "ffn2_down",
          "g_pre_ffn1", "g_post_ffn1", "g_pre_mix", "g_post_mix", "g_pre_ffn2", "g_post_ffn2"]


def _common_inputs(p):
    f = lambda a: np.ascontiguousarray(np.asarray(a, dtype=np.float32))
    ins = {n: f(p[n]) for n in WEIGHTS}
    gvec = np.stack([f(p[k]) for k in ("g_pre_ffn1", "g_pre_mix", "g_pre_ffn2", "g_post_ffn1", "g_post_mix", "g_post_ffn2")])
    ins["gvec"] = gvec
    ins["gpre_d"] = f(gvec[0:3].reshape(3, 8, P).transpose(2, 0, 1).reshape(P, 24))
    ins["cst"] = f(np.concatenate([np.eye(P), np.triu(np.ones((P, P))), np.ones((P, P))], 1))
    cw = np.concatenate([f(p["conv_w"]), f(p["conv_b"])[None]], 0)
    ins["cw_d"] = f(cw.reshape(5, 20, P).transpose(2, 1, 0).reshape(P, 100))
    ins["hv_d"] = f(np.stack([f(p["dt_bias"]), f(p["a_log"]), f(p["d_skip"]), np.zeros(24, np.float32)]))
    ins["nw_d"] = f(f(p["ssd_norm_w"]).reshape(12, P).T)
    k = np.arange(P)[:, None]
    q = np.arange(P)[None, :]
    ms = []
    for (W, dil), nt in zip(((128, 1), (512, 4), (2048, 16)), (2, 5, 17)):
        for o in range(nt):
            d = q + P * o - k
            ms.append(((d >= 0) & (d <= W) & (d % dil == 0)).astype(np.float32))
    ins["mask_d"] = f(np.stack(ms, 1).reshape(P, 24 * P))
    return ins


def _run(inp, ncores):
    f = lambda a: np.ascontiguousarray(np.asarray(a, dtype=np.float32))
    xp = f(inp["x_prompt"])
    xsm = f(inp["x_sample"])
    NB, SEQ, _ = xp.shape
    DB, DL, _ = xsm.shape
    halves = ncores // NB
    assert halves == 2 and DL == 4
    L = SEQ // 2
    n_own = L // P
    n_pre = n_own
    n_smp = DB // ncores
    caches = [f(inp[f"cache_kv_w{w}"])[0].reshape(DB, w, 1024) for w in (128, 512, 2048)]
    sconv = f(inp["state_conv"])[0]
    sssm = f(inp["state_ssm"])[0].reshape(DB, 1536, P)
    p = {k: np.asarray(inp[k])[0] for k in PARAMS}
    common = _common_inputs(p)
    nc = Builder(n_pre, n_own, n_smp, n_kvpre=min(16, n_pre)).build()
    in_maps = []
    for c in range(ncores):
        b, half = c // 2, c % 2
        xs = np.zeros(((n_pre + n_own + n_smp) * P, D), np.float32)
        if half:
            xs[0:L] = xp[b, 0:L]
        xs[L:2 * L] = xp[b, half * L:(half + 1) * L]
        for j in range(n_smp):
            xs[2 * L + j * P:2 * L + j * P + 4] = xsm[c * n_smp + j]
        flg = np.zeros((P, 2), np.float32)
        flg[:, 0] = half
        flg[0:4, 1] = 1.0
        m = dict(common)
        m["xs"] = xs
        m["flg_d"] = flg
        sl = slice(c * n_smp, (c + 1) * n_smp)
        for g in range(3):
            m[f"cache{g}"] = f(caches[g][sl])
        m["sconv"] = f(sconv[sl])
        m["sssm"] = f(sssm[sl])
        in_maps.append(m)
    res = run_bass_kernel_spmd(nc, in_maps, core_ids=list(range(ncores))).results
    y_p = np.zeros((NB, SEQ, D), np.float32)
    y_s = np.zeros((DB, DL, D), np.float32)
    WIN = (128, 512, 2048)
    kv_p = [np.zeros((1, NB, min(w, SEQ), 2, 8, 64), np.float32) for w in WIN]
    kv_s = [np.zeros((1, DB, w, 2, 8, 64), np.float32) for w in WIN]
    conv_p = np.zeros((1, NB, 3, 2560), np.float32)
    ssm_p = np.zeros((1, NB, 24, 64, 128), np.float32)
    conv_s = np.zeros((1, DB, 3, 2560), np.float32)
    ssm_s = np.zeros((1, DB, 24, 64, 128), np.float32)
    for c in range(ncores):
        r = res[c]
        b, half = c // 2, c % 2
        y_p[b, half * L:(half + 1) * L] = r["ys"][0:L]
        for j in range(n_smp):
            bb = c * n_smp + j
            y_s[bb] = r["ys"][L + j * P:L + j * P + 4]
            conv_s[0, bb] = r["convo"][1 + j]
            ssm_s[0, bb] = r["ssmo"][1 + j].reshape(24, 64, 128)
            for g in range(3):
                kv_s[g][0, bb] = r[f"kvs{g}"][j].reshape(WIN[g], 2, 8, 64)
        if half:
            conv_p[0, b] = r["convo"][0]
            ssm_p[0, b] = r["ssmo"][0].reshape(24, 64, 128)
            for g in range(3):
                n = min(WIN[g], SEQ)
                kv_p[g][0, b] = r[f"kvo{g}"][WIN[g] - n:].reshape(n, 2, 8, 64)
    return (y_p, y_s, kv_p[0], kv_p[1], kv_p[2], conv_p, ssm_p, kv_s[0], kv_s[1], kv_s[2], conv_s, ssm_s)


def kernel(**inp):
    return _run(inp, NCORES)
```

```python
import numpy as np
import os
STOP = int(os.environ.get('MIX_STOP', '9'))
DBG = int(os.environ.get('MIX_DBG', '0'))
DBG_G = int(os.environ.get('MIX_DBG_G', '0'))
ATT_PIPE = int(os.environ.get('ATT_PIPE', '0'))
import concourse.bass as bass
import concourse.mybir as mybir
from concourse.bass_utils import run_bass_kernel_spmd

F32 = mybir.dt.float32
BF16 = mybir.dt.bfloat16
AF = mybir.ActivationFunctionType
ALU = mybir.AluOpType

D = 1024
DFF = 2816
NIN = 10776
OFF_Z, OFF_XBC, OFF_DT, OFF_GATE = 4608, 6144, 8704, 8728
EPS = 1e-6
NCORES = 8
P = 128


class Sched:
    def __init__(self, nc):
        self.nc = nc
        self.engs = {"pe": nc.tensor, "act": nc.scalar, "dve": nc.vector, "pool": nc.gpsimd, "sp": nc.sync}
        self.stream = {e: [] for e in self.engs}
        self.sems, self.cnt = {}, {}
        self.waited = {e: {} for e in self.engs}
        self.lastw, self.readers = {}, {}
        for e in self.engs:
            self.sem(e)

    def sem(self, key):
        if key not in self.sems:
            self.sems[key] = self.nc.alloc_semaphore("s_" + key)
            self.cnt[key] = 0
        return self.sems[key]

    def _deps(self, e, reads, writes):
        need = {}

        def add(kv):
            if kv is not None and need.get(kv[0], 0) < kv[1]:
                need[kv[0]] = kv[1]

        for r in reads:
            add(self.lastw.get(r))
            if r.startswith("pb"):
                for kv in self.readers.get(r, {}).items():
                    if kv[0] != e:
                        add(kv)
        for w in writes:
            add(self.lastw.get(w))
            for kv in self.readers.get(w, {}).items():
                add(kv)
        out = []
        for k, v in need.items():
            if k == "pe" and e == "pe":
                continue
            if self.waited[e].get(k, 0) >= v:
                continue
            self.waited[e][k] = v
            out.append((k, v))
        return out

    def _mark(self, key, val, reads, writes):
        for r in reads:
            d = self.readers.setdefault(r, {})
            d[key] = max(d.get(key, 0), val)
        for w in writes:
            self.lastw[w] = (key, val)
            self.readers[w] = {}

    def op(self, e, fn, reads=(), writes=(), inc=True):
        waits = self._deps(e, reads, writes)
        if inc:
            self.cnt[e] += 1
            val = self.cnt[e]
        else:
            val = self.cnt[e] + 1
        self._mark(e, val, reads, writes)
        self.stream[e].append((waits, fn, (e, 1) if inc else None))

    def dma(self, e, semkey, out, in_, reads, writes):
        waits = self._deps(e, reads, writes)
        self.sem(semkey)
        self.cnt[semkey] += 16
        self._mark(semkey, self.cnt[semkey], reads, writes)
        self.stream[e].append((waits, lambda eng: eng.dma_start(out=out, in_=in_), (semkey, 16)))

    def finish(self):
        for k, v in self.cnt.items():
            if k not in self.engs and v > 0 and self.waited["sp"].get(k, 0) < v:
                self.stream["sp"].append(([(k, v)], None, None))

    def emit(self):
        nc = self.nc
        with nc.Block() as block:
            def mk(e):
                def body(eng):
                    for waits, fn, inc in self.stream[e]:
                        for k, v in waits:
                            eng.wait_ge(self.sems[k], v)
                        if fn is None:
                            continue
                        ins = fn(eng)
                        if inc is not None:
                            ins.then_inc(self.sems[inc[0]], inc[1])
                return body
            block.sync(mk("sp"))
            block.tensor(mk("pe"))
            block.scalar(mk("act"))
            block.vector(mk("dve"))
            block.gpsimd(mk("pool"))


WEIGHTS = {
    "w_in": (D, NIN), "w_branch_a": (512, D), "w_branch_b": (1536, D), "w_out": (D, D),
    "ffn1_gate": (D, DFF), "ffn1_up": (D, DFF), "ffn1_down": (DFF, D),
    "ffn2_gate": (D, DFF), "ffn2_up": (D, DFF), "ffn2_down": (DFF, D),
}
NSLOT = 2


class Builder:
    def __init__(self, n_pre, n_own, n_smp, n_kvpre=16):
        nc = self.nc = bass.Bass("TRN2", target_bir_lowering=False)
        self.S = Sched(nc)
        self.n_pre, self.n_own, self.n_smp, self.n_kvpre = n_pre, n_own, n_smp, n_kvpre
        self.kinds = ["pre"] * n_pre + ["own"] * n_own + ["smp"] * n_smp
        ng = len(self.kinds)
        nout = n_own + n_smp
        self.w32 = {n: nc.dram_tensor(n, [k, m], F32, kind="ExternalInput").ap() for n, (k, m) in WEIGHTS.items()}
        self.conv = {}
        self.xs = nc.dram_tensor("xs", [ng * P, D], F32, kind="ExternalInput").ap()
        self.gvec = nc.dram_tensor("gvec", [6, D], F32, kind="ExternalInput").ap()
        self.gpre_d = nc.dram_tensor("gpre_d", [P, 24], F32, kind="ExternalInput").ap()
        self.cst = nc.dram_tensor("cst", [P, 3 * P], F32, kind="ExternalInput").ap()
        self.ys = nc.dram_tensor("ys", [nout * P, D], F32, kind="ExternalOutput").ap()
        A = nc.alloc_sbuf_tensor
        self.ws = [A(f"ws{i}", [P, 8, 512], BF16) for i in range(NSLOT)]
        self.slot_i = 0
        self.nring = NSLOT + 1
        self.x = [A(f"x{i}", [P, D], F32) for i in range(2)]
        self.xn = A("xn", [P, D], BF16)
        self.junk = self.xn
        self.hnT = A("hnT", [P, 8, P], BF16)
        self.sg = A("sg", [P, 512], F32)
        self.gpre = A("gpre", [P, 3, 8], F32)
        self.gpost = A("gpost", [P, 3, D], F32)
        self.cstt = A("cstt", [P, 3 * P], F32)
        self.identb = A("identb", [P, P], BF16)
        self.st = A("st", [P, 16], F32)
        self.tmp = A("tmp", [P, 512], F32)
        DI = nc.dram_tensor
        self.cw_d = DI("cw_d", [P, 20 * 5], F32, kind="ExternalInput").ap()
        self.hv_d = DI("hv_d", [4, 24], F32, kind="ExternalInput").ap()
        self.nw_d = DI("nw_d", [P, 12], F32, kind="ExternalInput").ap()
        self.mask_d = DI("mask_d", [P, 24 * P], F32, kind="ExternalInput").ap()
        self.WIN = [128, 512, 2048]
        self.flg_d = DI("flg_d", [P, 2], F32, kind="ExternalInput").ap()
        self.kvo = [DI(f"kvo{g}", [self.WIN[g], 1024], F32, kind="ExternalOutput").ap() for g in range(3)]
        self.convo = DI("convo", [1 + n_smp, 3, 2560], F32, kind="ExternalOutput").ap()
        self.ssmo = DI("ssmo", [1 + n_smp, 1536, P], F32, kind="ExternalOutput").ap()
        if n_smp:
            self.cache = [DI(f"cache{g}", [n_smp, self.WIN[g], 1024], F32, kind="ExternalInput").ap() for g in range(3)]
            self.kvs = [DI(f"kvs{g}", [n_smp, self.WIN[g], 1024], F32, kind="ExternalOutput").ap() for g in range(3)]
            self.sconv = DI("sconv", [n_smp, 3, 2560], F32, kind="ExternalInput").ap()
            self.sssm = DI("sssm", [n_smp, 1536, P], F32, kind="ExternalInput").ap()
        self.flg = A("flg", [P, 2], F32)
        self.cw = A("cw", [P, 20, 5], F32)
        self.hv = A("hv", [P, 4, 24], F32)
        self.nw = A("nw", [P, 12], F32)
        self.maskf = A("maskf", [P, 512], F32)
        self.mask = A("mask", [P, 24, P], BF16)
        self.qb = A("qb", [P, 512], BF16)
        self.qb2 = A("qb2", [P, 512], BF16)
        self.qT = A("qT", [P, 12, P], BF16)
        self.NT = [2, 5, 17]
        self.kTh = [A(f"kTh{g}", [P, self.NT[g], 4, P], BF16) for g in range(3)]
        self.Vh = [A(f"Vh{g}", [P, self.NT[g], 8, 65], BF16) for g in range(3)]
        self.zs = A("zs", [P, 1536], F32)
        self.xraw = A("xraw", [P, 2560], BF16)
        self.stg = [A(f"stg{i}", [P, 10, 131], F32) for i in range(2)]
        self.carry = A("carry", [P, 20, 3], F32)
        self.caccT = A("cacc", [P, 2, 10, P], F32)
        self.cacc = [self.caccT[:, 0, :, :], self.caccT[:, 1, :, :]]
        self.xc = A("xc", [P, 20, P], BF16)
        self.xstm = A("xstm", [P, 16, P], BF16)
        self.sm = A("sm", [P, 10, 24], F32)
        self.ssdbuf = A("ssdbuf", [P, 3 * 1536], BF16)
        self.xd = self.ssdbuf[:, 0:1536].rearrange("p (h e) -> p h e", h=24)
        self.xdp = self.ssdbuf[:, 1536:3072].rearrange("p (h e) -> p h e", h=24)
        self.cbm = A("cbm", [P, 4, P], F32)
        self.Dt = A("Dt", [P, 4, P], F32)
        self.Lt = A("Lt", [P, 4, P], F32)
        self.Mt = self.ssdbuf[:, 3072:4608].rearrange("p (h e) -> p h e", h=12)
        self.y = A("y", [P, 1536], F32)
        self.ytmp = A("ytmp", [P, 1536], F32)
        self.act = self.zs[:].bitcast(BF16)[:, 0:DFF]
        self.actT = self.y[:].bitcast(BF16)[:, 0:DFF].rearrange("p (c t) -> p c t", c=22)
        self.ctmp = self.ytmp[:, 0:1280].rearrange("p (a b) -> p a b", a=10)
        self.ctmp1 = self.y[:, 0:1280].rearrange("p (a b) -> p a b", a=10)
        self.xraw2 = self.zs[:].bitcast(BF16)[:, 0:2560]
        self.act2 = self.ytmp[:].bitcast(BF16)[:, 0:DFF]
        self.actT2 = self.caccT[:].rearrange("p a b c -> p (a b c)").bitcast(BF16)[:, 0:DFF].rearrange("p (c t) -> p c t", c=22)
        self.yn = A("yn", [P, 1536], BF16)
        self.ynT = A("ynT", [P, 12, P], BF16)
        self.ST = A("ST", [P, 1536], F32)
        self.STb = A("STb", [P, 1536], BF16)
        self.gates = A("gates", [P, 1024], F32)
        self.mrg = A("mrg", [P, 1024], F32)
        self.cur_x = [(self.x[0], "x0"), (self.x[1], "x1")]
        self.cur_g = [(self.gates, "gates"), (self.mrg, "mrg")]
        self.mrgb = A("mrgb", [P, 1024], BF16)
        self.mrgT = A("mrgT", [P, 8, P], BF16)
        self.PT = A("PT", [P, 8, P], BF16)
        self.oa = A("oa", [P, 512], BF16)
        self.oaT = A("oaT", [P, 4, P], BF16)
        self.pb = [nc.alloc_psum_tensor(f"pb{i}", [P, 512], F32) for i in range(8)]
        self.pb_i = 0

    def dump(self, name, ap, res, g=0, only_g=None):
        if not DBG or g != DBG_G:
            return
        shp = [int(v) for v in ap.shape]
        d = self.nc.dram_tensor("dbg_" + name, shp, ap.dtype, kind="ExternalOutput").ap()
        self.S.dma("sp", "dbg_" + name, d, ap, reads=[res], writes=[])

    @staticmethod
    def L(r):
        return [r] if isinstance(r, str) else list(r)

    def bank(self):
        i = self.pb_i
        self.pb_i = (i + 1) % 8
        return self.pb[i], f"pb{i}"

    def slab(self, wname, k0, nk, c0, ncols):
        S = self.S
        i = self.slot_i % self.nring
        self.slot_i = (i + 1) % self.nring
        if i < NSLOT:
            slot, wres = self.ws[i], [f"ws{i}"]
        else:
            slot, wres = self.ssdbuf[:, 0:4096].rearrange("p (k n) -> p k n", k=8), ["xd", "xdp", "Mt"]
        dst = slot[:, 0:nk, 0:ncols]
        key = (wname, k0, c0)
        res_scr = f"scr_{wname}_{k0}_{c0}"
        if key not in self.conv:
            scr = self.nc.dram_tensor(res_scr, [P, nk * ncols], BF16).ap().rearrange("p (k n) -> p k n", k=nk)
            self.conv[key] = scr
            src = self.w32[wname][k0 * P:(k0 + nk) * P, c0:c0 + ncols].rearrange("(kc p) n -> p kc n", p=P)
            S.dma("pool", f"lc{i}", dst, src, reads=[], writes=wres)
            S.dma("sp", f"sv{i}", scr, dst, reads=wres, writes=[res_scr])
        else:
            S.dma("sp", f"ld{i}", dst, self.conv[key], reads=[res_scr], writes=wres)
        return slot, wres

    def linear(self, actT, actT_res, KC, wname, c0, ncols, evac):
        self.linear_multi([(actT, actT_res)], KC, wname, c0, ncols, [evac])

    def linear_multi(self, acts, KC, wname, c0, ncols, evacs):
        S = self.S
        cb = 0
        while cb < ncols:
            n = min(512, ncols - cb)
            banks = [self.bank() for _ in acts]
            k0 = 0
            while k0 < KC:
                nk = min(8, KC - k0)
                slot, sr = self.slab(wname, k0, nk, c0 + cb, n)
                for (actT, ares), (ps, psr) in zip(acts, banks):
                    for j in range(nk):
                        first = (k0 + j == 0)
                        last = (k0 + j == KC - 1)
                        S.op("pe", (lambda eng, ps=ps, a=actT[:, k0 + j, :], r=slot[:, j, 0:n], f=first, l=last, n=n:
                                    eng.matmul(ps[:, 0:n], lhsT=a, rhs=r, start=f, stop=l)),
                             reads=self.L(ares) + sr, writes=[psr], inc=last or j == nk - 1)
                k0 += nk
            for (ps, psr), evac in zip(banks, evacs):
                evac(ps, psr, cb, n)
            cb += n

    def transpose_to(self, src, src_res, nchunks, dstT, dst_res, scale=None, scale_res="gpre"):
        S = self.S
        c = 0
        while c < nchunks:
            n = min(8, nchunks - c)
            ps, psr = self.bank()
            psb = ps[:].bitcast(BF16)
            for j in range(n):
                S.op("pe", (lambda eng, o=psb[:, j * P:(j + 1) * P], i=src[:, (c + j) * P:(c + j + 1) * P]:
                            eng.transpose(o, i, self.identb[:])),
                     reads=self.L(src_res) + ["identb"], writes=[psr], inc=(j == n - 1))
            src3 = psb[:, 0:n * P].rearrange("p (a b) -> p a b", a=n)
            if scale is None:
                S.op("act", (lambda eng, o=dstT[:, c:c + n, :], i=src3: eng.copy(out=o, in_=i)), reads=[psr], writes=self.L(dst_res))
            else:
                S.op("dve", (lambda eng, o=dstT[:, c:c + n, :], i=src3, sc=scale[:, c:c + n].unsqueeze(2).broadcast_to([P, n, P]):
                             eng.tensor_tensor(out=o, in0=i, in1=sc, op=ALU.mult)), reads=[psr, scale_res], writes=self.L(dst_res))
            c += n

    def rstd(self, ss_ap, out_ap, factor, res_in, res_out):
        S = self.S
        f2 = factor * factor
        S.op("dve", lambda eng: eng.tensor_scalar(out=out_ap, in0=ss_ap, scalar1=1.0 / (D * f2), scalar2=EPS / f2,
                                                  op0=ALU.mult, op1=ALU.add), reads=[res_in], writes=[res_out])
        S.op("act", lambda eng: eng.sqrt(out=out_ap, in_=out_ap), reads=[res_out], writes=[res_out])
        S.op("dve", lambda eng: eng.reciprocal(out=out_ap, in_=out_ap), reads=[res_out], writes=[res_out])

    def prenorm(self, x, xr, gi, hnT=None, hres="hnT"):
        S = self.S
        hnT = self.hnT if hnT is None else hnT
        S.op("act", lambda eng: eng.activation(out=self.junk[:], in_=x[:], func=AF.Square, accum_out=self.st[:, 0:1]),
             reads=[xr], writes=["xn", "st0"])
        self.rstd(self.st[:, 0:1], self.st[:, 1:2], 1.0, "st0", "st1")
        S.op("dve", lambda eng: eng.tensor_scalar(out=self.xn[:], in0=x[:], scalar1=self.st[:, 1:2], scalar2=None,
                                                  op0=ALU.mult), reads=[xr, "st1"], writes=["xn"])
        self.transpose_to(self.xn, "xn", 8, hnT, hres, scale=self.gpre[:, gi, :])

    def postnorm_evac(self, x, xr, gi, factor, sc=2):
        S = self.S
        held = []

        def evac(ps, psr, cb, n):
            k = len(held)
            S.op("act", lambda eng: eng.activation(out=self.junk[:, cb:cb + n], in_=ps[:, 0:n], func=AF.Square,
                                                   accum_out=self.st[:, sc + k:sc + 1 + k]),
                 reads=[psr], writes=["xn", f"st{sc + k}"])
            held.append((ps, psr, cb, n))

        def fin():
            S.op("dve", lambda eng: eng.tensor_tensor(out=self.st[:, sc + 2:sc + 3], in0=self.st[:, sc:sc + 1], in1=self.st[:, sc + 1:sc + 2],
                                                      op=ALU.add), reads=[f"st{sc}", f"st{sc + 1}"], writes=[f"st{sc + 2}"])
            self.rstd(self.st[:, sc + 2:sc + 3], self.st[:, sc + 3:sc + 4], factor, f"st{sc + 2}", f"st{sc + 3}")
            for ps, psr, cb, n in held:
                S.op("dve", lambda eng, ps=ps, cb=cb, n=n: eng.scalar_tensor_tensor(
                    out=self.tmp[:, 0:n], in0=ps[:, 0:n], scalar=self.st[:, sc + 3:sc + 4], in1=self.gpost[:, gi, cb:cb + n],
                    op0=ALU.mult, op1=ALU.mult), reads=[psr, f"st{sc + 3}", "gpost"], writes=["tmp"])
                S.op("dve", lambda eng, cb=cb, n=n: eng.tensor_tensor(out=x[:, cb:cb + n], in0=x[:, cb:cb + n],
                                                                     in1=self.tmp[:, 0:n], op=ALU.add),
                     reads=["tmp", xr], writes=[xr])
        return evac, fin

    def ffn(self, tiles, which):
        S = self.S
        pre, post = (0, 0) if which == 1 else (2, 2)
        gname, uname, dname = f"ffn{which}_gate", f"ffn{which}_up", f"ffn{which}_down"
        self.nring = NSLOT + 1
        bufs = [(self.hnT, "hnT", self.sg, "sg", self.act, "zs", self.actT, "y"),
                (self.mrgT, "mrgT", self.tmp, "tmp", self.act2, "ytmp", self.actT2, ("cacc0", "cacc1"))][:len(tiles)]
        for (x, xr), b in zip(tiles, bufs):
            self.prenorm(x, xr, pre, b[0], b[1])

        def mk_gate(b):
            def ev(ps, psr, cb, n):
                S.op("act", lambda eng, ps=ps: eng.activation(out=b[2][:, 0:n], in_=ps[:, 0:n], func=AF.Silu), reads=[psr], writes=[b[3]])
            return ev

        def mk_up(b, cb0):
            def ev(ps, psr, cb, n):
                S.op("dve", lambda eng, ps=ps: eng.tensor_tensor(out=b[4][:, cb0:cb0 + n], in0=ps[:, 0:n], in1=b[2][:, 0:n], op=ALU.mult),
                     reads=[psr, b[3]], writes=[b[5]])
            return ev
        acts = [(b[0], b[1]) for b in bufs]
        cb = 0
        while cb < DFF:
            n = min(512, DFF - cb)
            self.linear_multi(acts, 8, gname, cb, n, [mk_gate(b) for b in bufs])
            self.linear_multi(acts, 8, uname, cb, n, [mk_up(b, cb) for b in bufs])
            cb += n
        for b in bufs:
            self.transpose_to(b[4], b[5], 22, b[6], b[7])
        pn = [self.postnorm_evac(x, xr, post, 0.5, sc=2 + 6 * t) for t, (x, xr) in enumerate(tiles)]
        self.linear_multi([(b[6], b[7]) for b in bufs], 22, dname, 0, D, [p[0] for p in pn])
        for p in pn:
            p[1]()

    def bc(self, ap24, n):
        return ap24.unsqueeze(2).broadcast_to([P, ap24.shape[1], n])

    def mixer(self, x, xr, g, kind, shared=None):
        S = self.S
        full = kind != "pre"
        do_kv = full or g >= self.n_pre - self.n_kvpre
        oi = g - self.n_pre if kind == "own" else -1
        bi = g - self.n_pre - self.n_own if kind == "smp" else -1
        (gates, gr), (mrg, mr) = self.cur_g
        V = lambda t, a, b2: t[:].rearrange("p (a b) -> p a b", a=a, b=b2)
        if shared is None:
            self.prenorm(x, xr, 1)
            xrawb, xrawr, d0 = self.xraw, "xraw", 0
        else:
            xrawb, xrawr, d0 = shared
        triU = self.cstt[:, P:2 * P]
        ones = self.cstt[:, 2 * P:3 * P]
        slots = [0 if kind == "smp" else g % nt for nt in self.NT]
        if STOP <= 0:
            return
        def ev_x(ps, psr, cb, n):
            S.op("act", lambda eng, ps=ps: eng.copy(out=self.xraw[:, cb:cb + n], in_=ps[:, 0:n]), reads=[psr], writes=["xraw"])
        if shared is None:
            self.linear(self.hnT, "hnT", 8, "w_in", OFF_XBC, 2560, ev_x)
        for hf in range(2):
            c0 = hf * 10
            CE = "dve" if hf == 0 else "pool"
            ctmp, ctr = (self.ctmp, "ytmp") if hf == 0 else (self.ctmp1, "y")
            stg, cacc = self.stg[hf], self.cacc[hf]
            sr, ar = f"stg{hf}", f"cacc{hf}"
            S.op(CE, lambda eng, c0=c0, stg=stg: eng.tensor_copy(out=stg[:, :, 0:3], in_=self.carry[:, c0:c0 + 10, :]), reads=["carry"], writes=[sr])
            self.transpose_to(xrawb[:, c0 * P:(c0 + 10) * P], xrawr, 10, stg[:, :, 3:131], sr)
            S.op(CE, lambda eng, c0=c0, stg=stg: eng.tensor_copy(out=self.carry[:, c0:c0 + 10, :], in_=stg[:, :, 128:131]), reads=[sr], writes=["carry"])
            for tap in range(4):
                wv = self.cw[:, c0:c0 + 10, tap:tap + 1].broadcast_to([P, 10, P])
                if tap == 0:
                    S.op(CE, lambda eng, wv=wv, stg=stg, cacc=cacc: eng.tensor_tensor(out=cacc[:], in0=stg[:, :, 0:P], in1=wv, op=ALU.mult), reads=[sr, "cw"], writes=[ar])
                else:
                    S.op(CE, lambda eng, wv=wv, stg=stg, tap=tap, ctmp=ctmp: eng.tensor_tensor(out=ctmp, in0=stg[:, :, tap:tap + P], in1=wv, op=ALU.mult), reads=[sr, "cw"], writes=[ctr])
                    S.op(CE, lambda eng, cacc=cacc, ctmp=ctmp: eng.tensor_tensor(out=cacc[:], in0=cacc[:], in1=ctmp, op=ALU.add), reads=[ar, ctr], writes=[ar])
            S.op(CE, lambda eng, c0=c0, cacc=cacc: eng.tensor_tensor(out=cacc[:], in0=cacc[:], in1=self.cw[:, c0:c0 + 10, 4:5].broadcast_to([P, 10, P]), op=ALU.add),
                 reads=[ar, "cw"], writes=[ar])
        for dg in range(3):
            if not do_kv or shared is not None:
                break
            base = dg * 1536
            sl = slots[dg]
            need_rows = kind == "smp" or (kind == "own" and self.WIN[dg] - (self.n_own - oi) * P >= 0)
            if full:
                def ev_q(ps, psr, cb, n):
                    S.op("act", lambda eng, ps=ps: eng.mul(out=self.qb[:], in_=ps[:, 0:512], mul=0.125), reads=[psr], writes=["qb"])
                self.linear(self.hnT, "hnT", 8, "w_in", base, 512, ev_q)
                self.transpose_to(self.qb, "qb", 4, self.qT[:, dg * 4:(dg + 1) * 4, :], "qT")

            def ev_k(ps, psr, cb, n):
                S.op("act", lambda eng, ps=ps: eng.copy(out=self.qb2[:], in_=ps[:, 0:512]), reads=[psr], writes=["qb2"])
                if need_rows:
                    S.op("dve", lambda eng, ps=ps: eng.tensor_copy(out=mrg[:, 0:512], in_=ps[:, 0:512]), reads=[psr], writes=[mr])
            self.linear(self.hnT, "hnT", 8, "w_in", base + 512, 512, ev_k)
            self.transpose_to(self.qb2, "qb2", 4, self.kTh[dg][:, sl, :, :], f"kTh{dg}_{sl}")

            def ev_v(ps, psr, cb, n, dg=dg, sl=sl):
                S.op("act", lambda eng, ps=ps: eng.copy(out=self.Vh[dg][:, sl, :, 0:64], in_=ps[:, 0:512].rearrange("p (h e) -> p h e", h=8)),
                     reads=[psr], writes=[f"Vh{dg}_{sl}"])
                if need_rows:
                    S.op("dve", lambda eng, ps=ps: eng.tensor_copy(out=mrg[:, 512:1024], in_=ps[:, 0:512]), reads=[psr], writes=[mr])
            self.linear(self.hnT, "hnT", 8, "w_in", base + 1024, 512, ev_v)
            if kind == "pre":
                S.op("dve", lambda eng, dg=dg, sl=sl: eng.tensor_copy(out=self.Vh[dg][:, sl, :, 64:65], in_=self.flg[:, 0:1].unsqueeze(1).broadcast_to([P, 8, 1])),
                     reads=["flg"], writes=[f"Vh{dg}_{sl}"])
            elif kind == "own":
                S.op("dve", lambda eng, dg=dg, sl=sl: eng.memset(self.Vh[dg][:, sl, :, 64:65], 1.0), writes=[f"Vh{dg}_{sl}"])
                r0 = self.WIN[dg] - (self.n_own - oi) * P
                if r0 >= 0:
                    S.dma("pool", f"kvo{dg}", self.kvo[dg][r0:r0 + P, :], mrg[:], reads=[mr], writes=[])
            else:
                W = self.WIN[dg]
                S.dma("pool", f"kvo{dg}", self.kvs[dg][bi, W - 4:W, :], mrg[0:4, :], reads=[mr], writes=[])
        if STOP == 10:
            return
        if full:
            def ev_z(ps, psr, cb, n):
                S.op("act", lambda eng: eng.activation(out=self.zs[:, cb:cb + n], in_=ps[:, 0:n], func=AF.Silu), reads=[psr], writes=["zs"])
            self.linear(self.hnT, "hnT", 8, "w_in", OFF_Z, 1536, ev_z)
        for hf in range(2):
            S.op("act", lambda eng, hf=hf: eng.activation(out=self.xc[:, hf * 10:(hf + 1) * 10, :], in_=self.cacc[hf][:], func=AF.Silu), reads=[f"cacc{hf}"], writes=["xc"])
        self.transpose_to(self.xc[:].rearrange("p a b -> p (a b)"), "xc", 16, self.xstm, "xstm")
        if STOP == 11:
            return
        sm = self.sm
        def ev_dt(ps, psr, cb, n):
            S.op("dve", lambda eng: eng.tensor_tensor(out=sm[:, 0, :], in0=ps[:, 0:24], in1=self.hv[:, 0, :], op=ALU.add), reads=[psr, "hv"], writes=["sm0"])
        if shared is None:
            self.linear(self.hnT, "hnT", 8, "w_in", OFF_DT, 24, ev_dt)
        S.op("act", lambda eng: eng.activation(out=sm[:, d0, :], in_=sm[:, d0, :], func=AF.Exp), reads=[f"sm{d0}"], writes=[f"sm{d0}"])
        S.op("act", lambda eng: eng.activation(out=sm[:, 1, :], in_=sm[:, d0, :], func=AF.Ln, bias=1.0), reads=[f"sm{d0}"], writes=["sm1"])
        if kind != "own":
            fc = 0 if kind == "pre" else 1
            S.op("dve", lambda eng: eng.tensor_scalar(out=sm[:, 1, :], in0=sm[:, 1, :], scalar1=self.flg[:, fc:fc + 1], scalar2=None, op0=ALU.mult), reads=["sm1", "flg"], writes=["sm1"])
        S.op("dve", lambda eng: eng.tensor_tensor(out=sm[:, 2, :], in0=sm[:, 1, :], in1=self.hv[:, 1, :], op=ALU.mult), reads=["sm1", "hv"], writes=["sm2"])
        self.dump("zs", self.zs[:], "zs", g)
        self.dump("xc", self.xc[:], "xc", g)
        self.dump("xstm", self.xstm[:], "xstm", g)
        if STOP <= 1:
            return
        ps1, p1r = self.bank()
        S.op("pe", lambda eng: eng.matmul(ps1[:, 0:24], lhsT=triU, rhs=sm[:, 2, :], start=True, stop=True), reads=["sm2", "cstt"], writes=[p1r], inc=False)
        S.op("pe", lambda eng: eng.matmul(ps1[:, 32:56], lhsT=ones, rhs=sm[:, 2, :], start=True, stop=True), reads=["sm2", "cstt"], writes=[p1r])
        S.op("act", lambda eng: eng.copy(out=sm[:, 3, :], in_=ps1[:, 0:24]), reads=[p1r], writes=["sm3"])
        S.op("dve", lambda eng: eng.tensor_scalar(out=sm[:, 4, :], in0=ps1[:, 0:24], scalar1=-1.0, scalar2=None, op0=ALU.mult), reads=[p1r], writes=["sm4"])
        S.op("act", lambda eng: eng.activation(out=sm[:, 5, :], in_=ps1[:, 0:24], func=AF.Exp), reads=[p1r], writes=["sm5"])
        S.op("dve", lambda eng: eng.tensor_tensor(out=sm[:, 6, :], in0=ps1[:, 32:56], in1=sm[:, 3, :], op=ALU.subtract), reads=[p1r, "sm3"], writes=["sm6"])
        S.op("act", lambda eng: eng.activation(out=sm[:, 6, :], in_=sm[:, 6, :], func=AF.Exp), reads=["sm6"], writes=["sm6"])
        S.op("dve", lambda eng: eng.tensor_tensor(out=sm[:, 6, :], in0=sm[:, 6, :], in1=sm[:, 1, :], op=ALU.mult), reads=["sm6", "sm1"], writes=["sm6"])
        S.op("act", lambda eng: eng.activation(out=sm[:, 7, :], in_=ps1[:, 32:56], func=AF.Exp), reads=[p1r], writes=["sm7"])
        xs3 = self.xstm[:, 0:12, :].rearrange("p a (h e) -> p (a h) e", h=2)
        S.op("dve", lambda eng: eng.tensor_tensor(out=self.xd, in0=xs3, in1=self.bc(sm[:, 1, :], 64), op=ALU.mult), reads=["xstm", "sm1"], writes=["xd"])
        S.op("pool", lambda eng: eng.tensor_tensor(out=self.xdp, in0=xs3, in1=self.bc(sm[:, 6, :], 64), op=ALU.mult), reads=["xstm", "sm6"], writes=["xdp"])
        if full:
            ps2, p2r = self.bank()
            for gg in range(4):
                S.op("pe", lambda eng, gg=gg: eng.matmul(ps2[:, gg * P:(gg + 1) * P], lhsT=self.xc[:, 12 + gg, :], rhs=self.xc[:, 16 + gg, :], start=True, stop=True),
                     reads=["xc"], writes=[p2r], inc=(gg == 3))
            S.op("dve", lambda eng: eng.tensor_tensor(out=self.cbm[:], in0=ps2[:].rearrange("p (a b) -> p a b", a=4), in1=triU.unsqueeze(1).broadcast_to([P, 4, P]), op=ALU.mult),
                 reads=[p2r, "cstt"], writes=["cbm"])
            for gg in range(4):
                po, por = self.bank()
                S.op("pe", lambda eng, gg=gg, po=po: eng.matmul(po[:, 0:384], lhsT=self.xc[:, 16 + gg, :], rhs=self.STb[:, gg * 384:(gg + 1) * 384], start=True, stop=True),
                     reads=["xc", "STb"], writes=[por])
                S.op("dve", lambda eng, gg=gg, po=po: eng.tensor_tensor(out=self.y[:, gg * 384:(gg + 1) * 384].rearrange("p (h e) -> p h e", h=6),
                                                                         in0=po[:, 0:384].rearrange("p (h e) -> p h e", h=6),
                                                                         in1=self.bc(sm[:, 5, gg * 6:(gg + 1) * 6], 64), op=ALU.mult),
                     reads=[por, "sm5"], writes=["y"])
            for hh in range(2):
                for hb in range(3):
                    pd, pdr = self.bank()
                    for jj in range(4):
                        h = hh * 12 + hb * 4 + jj
                        S.op("pe", lambda eng, h=h, jj=jj, pd=pd: eng.matmul(pd[:, jj * P:(jj + 1) * P], lhsT=sm[:, 2, h:h + 1].broadcast_to([P, P]), rhs=triU, start=True, stop=True),
                             reads=["sm2", "cstt"], writes=[pdr], inc=(jj == 3))
                    for jj in range(4):
                        h = hh * 12 + hb * 4 + jj
                        S.op("dve", lambda eng, h=h, jj=jj, pd=pd: eng.tensor_scalar(out=self.Dt[:, jj, :], in0=pd[:, jj * P:(jj + 1) * P], scalar1=sm[:, 4, h:h + 1], scalar2=0.0,
                                                                                   op0=ALU.add, op1=ALU.min), reads=[pdr, "sm4"], writes=["Dt"])
                    S.op("act", lambda eng: eng.activation(out=self.Lt[:], in_=self.Dt[:], func=AF.Exp), reads=["Dt"], writes=["Lt"])
                    for jj in range(4):
                        h = hh * 12 + hb * 4 + jj
                        S.op("pool", lambda eng, h=h, jj=jj, hb=hb: eng.tensor_tensor(out=self.Mt[:, hb * 4 + jj, :], in0=self.Lt[:, jj, :], in1=self.cbm[:, h // 6, :], op=ALU.mult),
                             reads=["Lt", "cbm"], writes=["Mt"])
                py, pyr = [], []
                for k in range(2):
                    a, b2 = self.bank()
                    py.append(a); pyr.append(b2)
                for j12 in range(12):
                    h = hh * 12 + j12
                    S.op("pe", lambda eng, h=h, j12=j12, py=py: eng.matmul(py[j12 // 8][:, (j12 % 8) * 64:(j12 % 8) * 64 + 64], lhsT=self.Mt[:, j12, :], rhs=self.xd[:, h, :], start=True, stop=True),
                         reads=["Mt", "xd"], writes=[pyr[j12 // 8]], inc=(j12 in (7, 11)))
                S.op("dve", lambda eng, hh=hh, py=py: eng.tensor_tensor(out=self.y[:, hh * 768:hh * 768 + 512], in0=self.y[:, hh * 768:hh * 768 + 512], in1=py[0][:, 0:512], op=ALU.add),
                     reads=["y", pyr[0]], writes=["y"])
                S.op("dve", lambda eng, hh=hh, py=py: eng.tensor_tensor(out=self.y[:, hh * 768 + 512:hh * 768 + 768], in0=self.y[:, hh * 768 + 512:hh * 768 + 768], in1=py[1][:, 0:256], op=ALU.add),
                     reads=["y", pyr[1]], writes=["y"])
        for gg in range(4):
            pS, pSr = self.bank()
            S.op("pe", lambda eng, gg=gg, pS=pS: eng.matmul(pS[:, 0:384], lhsT=self.xstm[:, 12 + gg, :], rhs=self.xdp[:, gg * 6:(gg + 1) * 6, :].rearrange("p h e -> p (h e)"), start=True, stop=True),
                 reads=["xstm", "xdp"], writes=[pSr])
            stv = self.ST[:, gg * 384:(gg + 1) * 384].rearrange("p (h e) -> p h e", h=6)
            S.op("pool", lambda eng, gg=gg, stv=stv: eng.tensor_tensor(out=stv, in0=stv, in1=self.bc(sm[:, 7, gg * 6:(gg + 1) * 6], 64), op=ALU.mult), reads=["ST", "sm7"], writes=["ST"])
            S.op("dve", lambda eng, gg=gg, pS=pS: eng.tensor_tensor(out=self.ST[:, gg * 384:(gg + 1) * 384], in0=self.ST[:, gg * 384:(gg + 1) * 384], in1=pS[:, 0:384], op=ALU.add),
                 reads=["ST", pSr], writes=["ST"])
        S.op("act", lambda eng: eng.copy(out=self.STb[:], in_=self.ST[:]), reads=["ST"], writes=["STb"])
        if kind == "smp":
            self.store_conv(1 + bi, 1)
            self.store_state(1 + bi)
        elif kind == "own" and oi == self.n_own - 1:
            self.store_conv(0, 125)
            self.store_state(0)
        self.dump("sm", self.sm[:], "sm7", g)
        self.dump("ST", self.ST[:], "ST", g)
        if not full:
            return
        self.dump("y0", self.y[:], "y", g)
        if STOP <= 2:
            return
        t3 = self.ytmp[:].rearrange("p (h e) -> p h e", h=24)
        S.op("pool", lambda eng: eng.tensor_tensor(out=t3, in0=xs3, in1=self.bc(self.hv[:, 2, :], 64), op=ALU.mult), reads=["xstm", "hv"], writes=["ytmp"])
        S.op("dve", lambda eng: eng.tensor_tensor(out=self.y[:], in0=self.y[:], in1=self.ytmp[:], op=ALU.add), reads=["y", "ytmp"], writes=["y"])
        S.op("dve", lambda eng: eng.tensor_tensor(out=self.y[:], in0=self.y[:], in1=self.zs[:], op=ALU.mult), reads=["y", "zs"], writes=["y"])
        for gg in range(4):
            S.op("act", lambda eng, gg=gg: eng.activation(out=self.ytmp[:, gg * 384:(gg + 1) * 384], in_=self.y[:, gg * 384:(gg + 1) * 384], func=AF.Square, accum_out=sm[:, 8, gg:gg + 1]),
                 reads=["y"], writes=["ytmp", "sm8"])
        S.op("dve", lambda eng: eng.tensor_scalar(out=sm[:, 8, 0:4], in0=sm[:, 8, 0:4], scalar1=1.0 / 384, scalar2=EPS, op0=ALU.mult, op1=ALU.add), reads=["sm8"], writes=["sm8"])
        S.op("act", lambda eng: eng.sqrt(out=sm[:, 8, 0:4], in_=sm[:, 8, 0:4]), reads=["sm8"], writes=["sm8"])
        S.op("dve", lambda eng: eng.reciprocal(out=sm[:, 8, 0:4], in_=sm[:, 8, 0:4]), reads=["sm8"], writes=["sm8"])
        S.op("dve", lambda eng: eng.tensor_tensor(out=self.yn[:].rearrange("p (g e) -> p g e", g=4), in0=self.y[:].rearrange("p (g e) -> p g e", g=4), in1=self.bc(sm[:, 8, 0:4], 384), op=ALU.mult),
             reads=["y", "sm8"], writes=["yn"])
        self.dump("yn", self.yn[:], "yn", g)
        self.transpose_to(self.yn, "yn", 12, self.ynT, "ynT", scale=self.nw, scale_res="nw")
        def ev_gate(off):
            def ev(ps, psr, cb, n):
                S.op("act", lambda eng: eng.activation(out=gates[:, cb:cb + n], in_=ps[:, 0:n], func=AF.Sigmoid), reads=[psr], writes=[gr])
            return ev
        self.linear(self.hnT, "hnT", 8, "w_in", OFF_GATE + 1024, 1024, ev_gate(1024))
        def ev_b(ps, psr, cb, n):
            S.op("dve", lambda eng: eng.tensor_tensor(out=mrg[:, cb:cb + n], in0=ps[:, 0:n], in1=gates[:, cb:cb + n], op=ALU.mult), reads=[psr, gr], writes=[mr])
        self.linear(self.ynT, "ynT", 12, "w_branch_b", 0, 1024, ev_b)
        self.dump(mr, mrg[:], mr, g)
        if STOP <= 3:
            return
        mbase = [0, 2, 7]
        if kind == "smp":
            tiles = [(dg, o, mbase[dg] + o, o) for dg in range(3) for o in range(self.NT[dg])]
        else:
            first = self.n_pre - self.n_kvpre
            tiles = [(dg, o, mbase[dg] + o, (g - o) % self.NT[dg]) for dg in range(3) for o in range(self.NT[dg]) if g - o >= first]
        self.attention(tiles)
        self.dump("oa", self.oa[:], "oa", g)
        self.transpose_to(self.oa, "oa", 4, self.oaT, "oaT")
        self.linear(self.hnT, "hnT", 8, "w_in", OFF_GATE, 1024, ev_gate(0))
        def ev_a(ps, psr, cb, n):
            S.op("dve", lambda eng: eng.tensor_tensor(out=self.tmp[:, 0:n], in0=ps[:, 0:n], in1=gates[:, cb:cb + n], op=ALU.mult), reads=[psr, gr], writes=["tmp"])
            S.op("dve", lambda eng: eng.tensor_tensor(out=self.mrgb[:, cb:cb + n], in0=mrg[:, cb:cb + n], in1=self.tmp[:, 0:n], op=ALU.add), reads=[mr, "tmp"], writes=["mrgb"])
        self.linear(self.oaT, "oaT", 4, "w_branch_a", 0, 1024, ev_a)
        self.dump("mrgb", self.mrgb[:], "mrgb", g)
        self.transpose_to(self.mrgb, "mrgb", 8, self.mrgT, "mrgT")
        evac, fin = self.postnorm_evac(x, xr, 1, 1.0)
        self.linear(self.mrgT, "mrgT", 8, "w_out", 0, D, evac)
        fin()

    def attention(self, tiles):
        S = self.S
        outb = [self.bank() for _ in range(2)]
        scb = [self.bank() for _ in range(4)]
        PT = self.PT
        r = 0
        for h in range(8):
            c, pb = h // 2, 64 * (h % 2)
            ob, obr = outb[h // 4]
            oc = (h % 4) * 65
            for b0 in range(0, len(tiles), 8):
                batch = tiles[b0:b0 + 8]
                nb = len(batch)
                banks = scb[2 * (r % 2):2 * (r % 2) + 2]
                r += 1
                for j, (dg, o, m, sl) in enumerate(batch):
                    ps, psr = banks[j // 4]
                    S.op("pe", lambda eng, j=j, dg=dg, sl=sl, ps=ps, pb=pb, c=c: eng.matmul(ps[:, (j % 4) * P:(j % 4 + 1) * P], lhsT=self.kTh[dg][pb:pb + 64, sl, c, :], rhs=self.qT[pb:pb + 64, dg * 4 + c, :], start=True, stop=True),
                         reads=[f"kTh{dg}_{sl}", "qT"], writes=[psr], inc=(j % 4 == 3 or j == nb - 1))
                for k in range((nb + 3) // 4):
                    ps, psr = banks[k]
                    n4 = min(4, nb - 4 * k)
                    pr = f"PT{k}"
                    S.op("act", lambda eng, ps=ps, k=k, n4=n4: eng.activation(out=PT[:, 4 * k:4 * k + n4, :], in_=ps[:, 0:n4 * P].rearrange("p (a b) -> p a b", a=n4), func=AF.Exp), reads=[psr], writes=[pr])
                    ms = [t[2] for t in batch[4 * k:4 * k + n4]]
                    if ms == list(range(ms[0], ms[0] + n4)):
                        S.op("dve", lambda eng, k=k, n4=n4, m0=ms[0]: eng.tensor_tensor(out=PT[:, 4 * k:4 * k + n4, :], in0=PT[:, 4 * k:4 * k + n4, :], in1=self.mask[:, m0:m0 + n4, :], op=ALU.mult), reads=[pr, "mask"], writes=[pr])
                    else:
                        for j, m in enumerate(ms):
                            S.op("dve", lambda eng, j=4 * k + j, m=m: eng.tensor_tensor(out=PT[:, j, :], in0=PT[:, j, :], in1=self.mask[:, m, :], op=ALU.mult), reads=[pr, "mask"], writes=[pr])
                for j, (dg, o, m, sl) in enumerate(batch):
                    first = (b0 + j == 0)
                    last = (b0 + j == len(tiles) - 1)
                    S.op("pe", lambda eng, j=j, dg=dg, sl=sl, first=first, last=last, ob=ob, oc=oc, h=h: eng.matmul(ob[:, oc:oc + 65], lhsT=PT[:, j, :], rhs=self.Vh[dg][:, sl, h, :], start=first, stop=last),
                         reads=[f"PT{j // 4}", f"Vh{dg}_{sl}"], writes=[obr], inc=(j % 4 == 3 or j == nb - 1))
        for k in range(2):
            ob, obr = outb[k]
            ob3 = ob[:, 0:260].rearrange("p (h e) -> p h e", e=65)
            S.op("dve", lambda eng, ob3=ob3, k=k: eng.reciprocal(out=self.sm[:, 9, 4 * k:4 * k + 4].unsqueeze(2), in_=ob3[:, :, 64:65]), reads=[obr], writes=["sm9"])
            S.op("dve", lambda eng, ob3=ob3, k=k: eng.tensor_tensor(out=self.oa[:, 256 * k:256 * k + 256].rearrange("p (h e) -> p h e", e=64), in0=ob3[:, :, 0:64],
                                                                   in1=self.bc(self.sm[:, 9, 4 * k:4 * k + 4], 64), op=ALU.mult), reads=[obr, "sm9"], writes=["oa"])

    def store_conv(self, idx, r0):
        self.S.dma("pool", "cvo", self.convo[idx], self.xraw[r0:r0 + 3, :], reads=["xraw"], writes=[])

    def store_state(self, idx):
        S = self.S
        id32 = self.cstt[:, 0:P]
        for c0 in range(0, 12, 4):
            ps, psr = self.bank()
            for j in range(4):
                S.op("pe", lambda eng, ps=ps, j=j, c=c0 + j: eng.transpose(ps[:, j * P:(j + 1) * P], self.ST[:, c * P:(c + 1) * P], id32),
                     reads=["ST", "cstt"], writes=[psr], inc=(j == 3))
            S.op("act", lambda eng, ps=ps, c0=c0: eng.copy(out=self.ytmp[:, c0 * P:(c0 + 4) * P], in_=ps[:, 0:512]), reads=[psr], writes=["ytmp"])
        S.dma("pool", "sso", self.ssmo[idx].rearrange("(c p) n -> p c n", p=P), self.ytmp[:].rearrange("p (c n) -> p c n", c=12), reads=["ytmp"], writes=[])

    def load_sample(self, bi):
        S = self.S
        id32 = self.cstt[:, 0:P]
        for dg in range(3):
            nt = self.NT[dg]
            for j in range(nt - 1):
                sl = nt - 1 - j
                src = self.cache[dg][bi, j * P:(j + 1) * P, :]
                kb, kr = (self.qb, "qb") if j % 2 == 0 else (self.qb2, "qb2")
                S.dma("pool", f"ck{j % 2}", kb[:], src[:, 0:512], reads=[], writes=[kr])
                self.transpose_to(kb, kr, 4, self.kTh[dg][:, sl, :, :], f"kTh{dg}_{sl}")
                S.dma("pool", f"cv{dg}_{sl}", self.Vh[dg][:, sl, :, 0:64], src[:, 512:1024].rearrange("p (h e) -> p h e", h=8),
                      reads=[], writes=[f"Vh{dg}_{sl}"])
        S.dma("sp", "ssi", self.ytmp[:].rearrange("p (c n) -> p c n", c=12), self.sssm[bi].rearrange("(c p) n -> p c n", p=P), reads=[], writes=["ytmp"])
        for c0 in range(0, 12, 4):
            ps, psr = self.bank()
            for j in range(4):
                S.op("pe", lambda eng, ps=ps, j=j, c=c0 + j: eng.transpose(ps[:, j * P:(j + 1) * P], self.ytmp[:, c * P:(c + 1) * P], id32),
                     reads=["ytmp", "cstt"], writes=[psr], inc=(j == 3))
            S.op("act", lambda eng, ps=ps, c0=c0: eng.copy(out=self.ST[:, c0 * P:(c0 + 4) * P], in_=ps[:, 0:512]), reads=[psr], writes=["ST"])
        S.op("act", lambda eng: eng.copy(out=self.STb[:], in_=self.ST[:]), reads=["ST"], writes=["STb"])
        for hf in range(2):
            S.dma("sp", "sci", self.cacc[0][0:3, :, :].rearrange("p a b -> p (a b)"), self.sconv[bi][:, hf * 1280:(hf + 1) * 1280], reads=[], writes=["cacc0"])
            ps, psr = self.bank()
            for c in range(10):
                S.op("pe", lambda eng, ps=ps, c=c: eng.matmul(ps[:, c * 3:(c + 1) * 3], lhsT=self.cacc[0][0:3, c, :], rhs=self.cstt[0:3, 0:3], start=True, stop=True),
                     reads=["cacc0", "cstt"], writes=[psr], inc=(c == 9))
            S.op("act", lambda eng, ps=ps, hf=hf: eng.copy(out=self.carry[:, hf * 10:(hf + 1) * 10, :], in_=ps[:, 0:30].rearrange("p (c t) -> p c t", t=3)),
                 reads=[psr], writes=["carry"])

    def setup(self):
        S = self.S
        S.dma("sp", "c0", self.cstt[:], self.cst, reads=[], writes=["cstt"])
        S.dma("sp", "c1", self.gpost[:], self.gvec[3:6, :].partition_broadcast(P), reads=[], writes=["gpost"])
        S.dma("sp", "c2", self.gpre[:], self.gpre_d.rearrange("p (g c) -> p g c", g=3), reads=[], writes=["gpre"])
        S.op("dve", lambda eng: eng.tensor_copy(out=self.identb[:], in_=self.cstt[:, 0:P]), reads=["cstt"], writes=["identb"])
        S.dma("sp", "c3", self.cw[:], self.cw_d.rearrange("p (c t) -> p c t", t=5), reads=[], writes=["cw"])
        S.dma("sp", "c4", self.hv[:], self.hv_d.partition_broadcast(P), reads=[], writes=["hv"])
        S.dma("sp", "c5", self.nw[:], self.nw_d, reads=[], writes=["nw"])
        S.op("act", lambda eng: eng.activation(out=self.hv[:, 1, :], in_=self.hv[:, 1, :], func=AF.Exp), reads=["hv"], writes=["hv"])
        S.op("dve", lambda eng: eng.tensor_scalar(out=self.hv[:, 1, :], in0=self.hv[:, 1, :], scalar1=-1.0, scalar2=None, op0=ALU.mult), reads=["hv"], writes=["hv"])
        for i in range(6):
            S.dma("sp", "c6", self.maskf[:], self.mask_d[:, i * 512:(i + 1) * 512], reads=[], writes=["maskf"])
            S.op("dve", lambda eng, i=i: eng.tensor_copy(out=self.mask[:, i * 4:(i + 1) * 4, :], in_=self.maskf[:].rearrange("p (a b) -> p a b", a=4)), reads=["maskf"], writes=["mask"])
        S.op("dve", lambda eng: eng.memset(self.ST[:], 0.0), writes=["ST"])
        S.op("dve", lambda eng: eng.memset(self.STb[:], 0.0), writes=["STb"])
        S.op("dve", lambda eng: eng.memset(self.carry[:], 0.0), writes=["carry"])
        for dg in range(3):
            S.op("dve", lambda eng, dg=dg: eng.memset(self.Vh[dg][:], 1.0), writes=[f"Vh{dg}_{sl}" for sl in range(self.NT[dg])])
        S.dma("sp", "c7", self.flg[:], self.flg_d, reads=[], writes=["flg"])
        for dg in range(3):
            W = self.WIN[dg]
            for b in range(self.n_smp):
                for r0 in range(4, W, 256):
                    r1 = min(W, r0 + 256)
                    S.dma("act", "kvcp", self.kvs[dg][b, r0 - 4:r1 - 4, :], self.cache[dg][b, r0:r1, :], reads=[], writes=[])

    def pre_proj(self, tiles, gs):
        S = self.S
        nt_ = len(tiles)
        hb = [(self.hnT, "hnT"), (self.mrgT, "mrgT")][:nt_]
        xb = [(self.xraw, "xraw", 0), (self.xraw2, "zs", 8)][:nt_]
        for (x, xr), (h, hr) in zip(tiles, hb):
            self.prenorm(x, xr, 1, h, hr)

        def mk_x(t):
            buf, br, _ = xb[t]
            def ev(ps, psr, cb, n):
                S.op("act", lambda eng, ps=ps: eng.copy(out=buf[:, cb:cb + n], in_=ps[:, 0:n]), reads=[psr], writes=[br])
            return ev
        self.linear_multi(hb, 8, "w_in", OFF_XBC, 2560, [mk_x(t) for t in range(nt_)])

        def mk_dt(t):
            d0 = xb[t][2]
            def ev(ps, psr, cb, n):
                S.op("dve", lambda eng, ps=ps: eng.tensor_tensor(out=self.sm[:, d0, :], in0=ps[:, 0:24], in1=self.hv[:, 0, :], op=ALU.add), reads=[psr, "hv"], writes=[f"sm{d0}"])
            return ev
        self.linear_multi(hb, 8, "w_in", OFF_DT, 24, [mk_dt(t) for t in range(nt_)])
        if gs[0] >= self.n_pre - self.n_kvpre:
            kst = [(self.qb, "qb"), (self.qb2, "qb2")][:nt_]
            for dg in range(3):
                base = dg * 1536
                sls = [g % self.NT[dg] for g in gs]

                def mk_k(t):
                    kb, kr = kst[t]
                    def ev(ps, psr, cb, n):
                        S.op("act", lambda eng, ps=ps: eng.copy(out=kb[:], in_=ps[:, 0:512]), reads=[psr], writes=[kr])
                    return ev
                self.linear_multi(hb, 8, "w_in", base + 512, 512, [mk_k(t) for t in range(nt_)])
                for t in range(nt_):
                    self.transpose_to(kst[t][0], kst[t][1], 4, self.kTh[dg][:, sls[t], :, :], f"kTh{dg}_{sls[t]}")

                def mk_v(t, dg=dg):
                    sl = sls[t]
                    def ev(ps, psr, cb, n):
                        S.op("act", lambda eng, ps=ps: eng.copy(out=self.Vh[dg][:, sl, :, 0:64], in_=ps[:, 0:512].rearrange("p (h e) -> p h e", h=8)),
                             reads=[psr], writes=[f"Vh{dg}_{sl}"])
                        S.op("dve", lambda eng: eng.tensor_copy(out=self.Vh[dg][:, sl, :, 64:65], in_=self.flg[:, 0:1].unsqueeze(1).broadcast_to([P, 8, 1])),
                             reads=["flg"], writes=[f"Vh{dg}_{sl}"])
                    return ev
                self.linear_multi(hb, 8, "w_in", base + 1024, 512, [mk_v(t) for t in range(nt_)])
        return xb

    def load_x(self, gs, bufs):
        for t, g in enumerate(gs):
            self.S.dma("sp", f"xin{t}", bufs[t][0][:], self.xs[g * P:(g + 1) * P, :], reads=[], writes=[bufs[t][1]])

    def pair(self, gs, nxt):
        S = self.S
        kind = self.kinds[gs[0]]
        tiles = self.cur_x[:len(gs)]
        self.ffn(tiles, 1)
        if kind == "pre" and nxt:
            self.load_x(nxt, self.cur_g)
        shared = self.pre_proj(tiles, gs) if kind == "pre" else [None] * len(gs)
        for ti, ((x, xr), g) in enumerate(zip(tiles, gs)):
            if kind == "smp":
                bi = g - self.n_pre - self.n_own
                if bi == 0:
                    for dg in range(3):
                        S.op("dve", lambda eng, dg=dg: eng.memset(self.Vh[dg][:, :, :, 64:65], 1.0), writes=[f"Vh{dg}_{sl}" for sl in range(self.NT[dg])])
                self.load_sample(bi)
            self.dump("h1", x[:], xr, g, DBG_G)
            self.mixer(x, xr, g, kind, shared[ti])
            self.dump("h2", x[:], xr, g, DBG_G)
        if kind != "pre":
            if nxt:
                self.load_x(nxt, self.cur_g)
            self.ffn(tiles, 2)
            for (x, xr), g in zip(tiles, gs):
                o = g - self.n_pre
                S.dma("pool", f"yout{o % 2}", self.ys[o * P:(o + 1) * P, :], x[:], reads=[xr], writes=[])
        self.cur_x, self.cur_g = self.cur_g, self.cur_x

    def build(self):
        self.setup()
        pairs = []
        g = 0
        while g < len(self.kinds):
            gs = [g]
            if g + 1 < len(self.kinds) and self.kinds[g + 1] == self.kinds[g]:
                gs.append(g + 1)
            pairs.append(gs)
            g += len(gs)
        self.load_x(pairs[0], self.cur_x)
        for i, gs in enumerate(pairs):
            self.pair(gs, pairs[i + 1] if i + 1 < len(pairs) else None)
        self.S.finish()
        self.S.emit()
        return self.nc


PARAMS = ["w_in", "conv_w", "conv_b", "dt_bias", "a_log", "d_skip", "ssd_norm_w", "w_branch_a", "w_branch_b", "w_out",
          "ffn1_gate", "ffn1_up", "ffn1_down", "ffn2_gate", "ffn2_up", "ffn2_down",
          "g_pre_ffn1", "g_post_ffn1", "g_pre_mix", "g_post_mix", "g_pre_ffn2", "g_post_ffn2"]


def _common_inputs(p):
    f = lambda a: np.ascontiguousarray(np.asarray(a, dtype=np.float32))
    ins = {n: f(p[n]) for n in WEIGHTS}
    gvec = np.stack([f(p[k]) for k in ("g_pre_ffn1", "g_pre_mix", "g_pre_ffn2", "g_post_ffn1", "g_post_mix", "g_post_ffn2")])
    ins["gvec"] = gvec
    ins["gpre_d"] = f(gvec[0:3].reshape(3, 8, P).transpose(2, 0, 1).reshape(P, 24))
    ins["cst"] = f(np.concatenate([np.eye(P), np.triu(np.ones((P, P))), np.ones((P, P))], 1))
    cw = np.concatenate([f(p["conv_w"]), f(p["conv_b"])[None]], 0)
    ins["cw_d"] = f(cw.reshape(5, 20, P).transpose(2, 1, 0).reshape(P, 100))
    ins["hv_d"] = f(np.stack([f(p["dt_bias"]), f(p["a_log"]), f(p["d_skip"]), np.zeros(24, np.float32)]))
    ins["nw_d"] = f(f(p["ssd_norm_w"]).reshape(12, P).T)
    k = np.arange(P)[:, None]
    q = np.arange(P)[None, :]
    ms = []
    for (W, dil), nt in zip(((128, 1), (512, 4), (2048, 16)), (2, 5, 17)):
        for o in range(nt):
            d = q + P * o - k
            ms.append(((d >= 0) & (d <= W) & (d % dil == 0)).astype(np.float32))
    ins["mask_d"] = f(np.stack(ms, 1).reshape(P, 24 * P))
    return ins


def _run(inp, ncores):
    f = lambda a: np.ascontiguousarray(np.asarray(a, dtype=np.float32))
    xp = f(inp["x_prompt"])
    xsm = f(inp["x_sample"])
    NB, SEQ, _ = xp.shape
    DB, DL, _ = xsm.shape
    halves = ncores // NB
    assert halves == 2 and DL == 4
    L = SEQ // 2
    n_own = L // P
    n_pre = n_own
    n_smp = DB // ncores
    caches = [f(inp[f"cache_kv_w{w}"])[0].reshape(DB, w, 1024) for w in (128, 512, 2048)]
    sconv = f(inp["state_conv"])[0]
    sssm = f(inp["state_ssm"])[0].reshape(DB, 1536, P)
    p = {k: np.asarray(inp[k])[0] for k in PARAMS}
    common = _common_inputs(p)
    nc = Builder(n_pre, n_own, n_smp, n_kvpre=min(16, n_pre)).build()
    in_maps = []
    for c in range(ncores):
        b, half = c // 2, c % 2
        xs = np.zeros(((n_pre + n_own + n_smp) * P, D), np.float32)
        if half:
            xs[0:L] = xp[b, 0:L]
        xs[L:2 * L] = xp[b, half * L:(half + 1) * L]
        for j in range(n_smp):
            xs[2 * L + j * P:2 * L + j * P + 4] = xsm[c * n_smp + j]
        flg = np.zeros((P, 2), np.float32)
        flg[:, 0] = half
        flg[0:4, 1] = 1.0
        m = dict(common)
        m["xs"] = xs
        m["flg_d"] = flg
        sl = slice(c * n_smp, (c + 1) * n_smp)
        for g in range(3):
            m[f"cache{g}"] = f(caches[g][sl])
        m["sconv"] = f(sconv[sl])
        m["sssm"] = f(sssm[sl])
        in_maps.append(m)
    res = run_bass_kernel_spmd(nc, in_maps, core_ids=list(range(ncores))).results
    y_p = np.zeros((NB, SEQ, D), np.float32)
    y_s = np.zeros((DB, DL, D), np.float32)
    WIN = (128, 512, 2048)
    kv_p = [np.zeros((1, NB, min(w, SEQ), 2, 8, 64), np.float32) for w in WIN]
    kv_s = [np.zeros((1, DB, w, 2, 8, 64), np.float32) for w in WIN]
    conv_p = np.zeros((1, NB, 3, 2560), np.float32)
    ssm_p = np.zeros((1, NB, 24, 64, 128), np.float32)
    conv_s = np.zeros((1, DB, 3, 2560), np.float32)
    ssm_s = np.zeros((1, DB, 24, 64, 128), np.float32)
    for c in range(ncores):
        r = res[c]
        b, half = c // 2, c % 2
        y_p[b, half * L:(half + 1) * L] = r["ys"][0:L]
        for j in range(n_smp):
            bb = c * n_smp + j
            y_s[bb] = r["ys"][L + j * P:L + j * P + 4]
            conv_s[0, bb] = r["convo"][1 + j]
            ssm_s[0, bb] = r["ssmo"][1 + j].reshape(24, 64, 128)
            for g in range(3):
                kv_s[g][0, bb] = r[f"kvs{g}"][j].reshape(WIN[g], 2, 8, 64)
        if half:
            conv_p[0, b] = r["convo"][0]
            ssm_p[0, b] = r["ssmo"][0].reshape(24, 64, 128)
            for g in range(3):
                n = min(WIN[g], SEQ)
                kv_p[g][0, b] = r[f"kvo{g}"][WIN[g] - n:].reshape(n, 2, 8, 64)
    return (y_p, y_s, kv_p[0], kv_p[1], kv_p[2], conv_p, ssm_p, kv_s[0], kv_s[1], kv_s[2], conv_s, ssm_s)


def kernel(**inp):
    return _run(inp, NCORES)
```

```python
import numpy as np
import os
STOP = int(os.environ.get('MIX_STOP', '9'))
DBG = int(os.environ.get('MIX_DBG', '0'))
DBG_G = int(os.environ.get('MIX_DBG_G', '0'))
ATT_PIPE = int(os.environ.get('ATT_PIPE', '0'))
import concourse.bass as bass
import concourse.mybir as mybir
from concourse.bass_utils import run_bass_kernel_spmd

F32 = mybir.dt.float32
BF16 = mybir.dt.bfloat16
AF = mybir.ActivationFunctionType
ALU = mybir.AluOpType

D = 1024
DFF = 2816
NIN = 10776
OFF_Z, OFF_XBC, OFF_DT, OFF_GATE = 4608, 6144, 8704, 8728
EPS = 1e-6
NCORES = 8
P = 128


class Sched:
    def __init__(self, nc):
        self.nc = nc
        self.engs = {"pe": nc.tensor, "act": nc.scalar, "dve": nc.vector, "pool": nc.gpsimd, "sp": nc.sync}
        self.stream = {e: [] for e in self.engs}
        self.sems, self.cnt = {}, {}
        self.waited = {e: {} for e in self.engs}
        self.lastw, self.readers = {}, {}
        for e in self.engs:
            self.sem(e)

    def sem(self, key):
        if key not in self.sems:
            self.sems[key] = self.nc.alloc_semaphore("s_" + key)
            self.cnt[key] = 0
        return self.sems[key]

    def _deps(self, e, reads, writes):
        need = {}

        def add(kv):
            if kv is not None and need.get(kv[0], 0) < kv[1]:
                need[kv[0]] = kv[1]

        for r in reads:
            add(self.lastw.get(r))
            if r.startswith("pb"):
                for kv in self.readers.get(r, {}).items():
                    if kv[0] != e:
                        add(kv)
        for w in writes:
            add(self.lastw.get(w))
            for kv in self.readers.get(w, {}).items():
                add(kv)
        out = []
        for k, v in need.items():
            if k == "pe" and e == "pe":
                continue
            if self.waited[e].get(k, 0) >= v:
                continue
            self.waited[e][k] = v
            out.append((k, v))
        return out

    def _mark(self, key, val, reads, writes):
        for r in reads:
            d = self.readers.setdefault(r, {})
            d[key] = max(d.get(key, 0), val)
        for w in writes:
            self.lastw[w] = (key, val)
            self.readers[w] = {}

    def op(self, e, fn, reads=(), writes=(), inc=True):
        waits = self._deps(e, reads, writes)
        if inc:
            self.cnt[e] += 1
            val = self.cnt[e]
        else:
            val = self.cnt[e] + 1
        self._mark(e, val, reads, writes)
        self.stream[e].append((waits, fn, (e, 1) if inc else None))

    def dma(self, e, semkey, out, in_, reads, writes):
        waits = self._deps(e, reads, writes)
        self.sem(semkey)
        self.cnt[semkey] += 16
        self._mark(semkey, self.cnt[semkey], reads, writes)
        self.stream[e].append((waits, lambda eng: eng.dma_start(out=out, in_=in_), (semkey, 16)))

    def finish(self):
        for k, v in self.cnt.items():
            if k not in self.engs and v > 0 and self.waited["sp"].get(k, 0) < v:
                self.stream["sp"].append(([(k, v)], None, None))

    def emit(self):
        nc = self.nc
        with nc.Block() as block:
            def mk(e):
                def body(eng):
                    for waits, fn, inc in self.stream[e]:
                        for k, v in waits:
                            eng.wait_ge(self.sems[k], v)
                        if fn is None:
                            continue
                        ins = fn(eng)
                        if inc is not None:
                            ins.then_inc(self.sems[inc[0]], inc[1])
                return body
            block.sync(mk("sp"))
            block.tensor(mk("pe"))
            block.scalar(mk("act"))
            block.vector(mk("dve"))
            block.gpsimd(mk("pool"))


WEIGHTS = {
    "w_in": (D, NIN), "w_branch_a": (512, D), "w_branch_b": (1536, D), "w_out": (D, D),
    "ffn1_gate": (D, DFF), "ffn1_up": (D, DFF), "ffn1_down": (DFF, D),
    "ffn2_gate": (D, DFF), "ffn2_up": (D, DFF), "ffn2_down": (DFF, D),
}
NSLOT = 2


class Builder:
    def __init__(self, n_pre, n_own, n_smp, n_kvpre=16):
        nc = self.nc = bass.Bass("TRN2", target_bir_lowering=False)
        self.S = Sched(nc)
        self.n_pre, self.n_own, self.n_smp, self.n_kvpre = n_pre, n_own, n_smp, n_kvpre
        self.kinds = ["pre"] * n_pre + ["own"] * n_own + ["smp"] * n_smp
        ng = len(self.kinds)
        nout = n_own + n_smp
        self.w32 = {n: nc.dram_tensor(n, [k, m], F32, kind="ExternalInput").ap() for n, (k, m) in WEIGHTS.items()}
        self.conv = {}
        self.xs = nc.dram_tensor("xs", [ng * P, D], F32, kind="ExternalInput").ap()
        self.gvec = nc.dram_tensor("gvec", [6, D], F32, kind="ExternalInput").ap()
        self.gpre_d = nc.dram_tensor("gpre_d", [P, 24], F32, kind="ExternalInput").ap()
        self.cst = nc.dram_tensor("cst", [P, 3 * P], F32, kind="ExternalInput").ap()
        self.ys = nc.dram_tensor("ys", [nout * P, D], F32, kind="ExternalOutput").ap()
        A = nc.alloc_sbuf_tensor
        self.ws = [A(f"ws{i}", [P, 8, 512], BF16) for i in range(NSLOT)]
        self.slot_i = 0
        self.nring = NSLOT + 1
        self.x = [A(f"x{i}", [P, D], F32) for i in range(2)]
        self.xn = A("xn", [P, D], BF16)
        self.junk = self.xn
        self.hnT = A("hnT", [P, 8, P], BF16)
        self.sg = A("sg", [P, 512], F32)
        self.gpre = A("gpre", [P, 3, 8], F32)
        self.gpost = A("gpost", [P, 3, D], F32)
        self.cstt = A("cstt", [P, 3 * P], F32)
        self.identb = A("identb", [P, P], BF16)
        self.st = A("st", [P, 16], F32)
        self.tmp = A("tmp", [P, 512], F32)
        DI = nc.dram_tensor
        self.cw_d = DI("cw_d", [P, 20 * 5], F32, kind="ExternalInput").ap()
        self.hv_d = DI("hv_d", [4, 24], F32, kind="ExternalInput").ap()
        self.nw_d = DI("nw_d", [P, 12], F32, kind="ExternalInput").ap()
        self.mask_d = DI("mask_d", [P, 24 * P], F32, kind="ExternalInput").ap()
        self.WIN = [128, 512, 2048]
        self.flg_d = DI("flg_d", [P, 2], F32, kind="ExternalInput").ap()
        self.kvo = [DI(f"kvo{g}", [self.WIN[g], 1024], F32, kind="ExternalOutput").ap() for g in range(3)]
        self.convo = DI("convo", [1 + n_smp, 3, 2560], F32, kind="ExternalOutput").ap()
        self.ssmo = DI("ssmo", [1 + n_smp, 1536, P], F32, kind="ExternalOutput").ap()
        if n_smp:
            self.cache = [DI(f"cache{g}", [n_smp, self.WIN[g], 1024], F32, kind="ExternalInput").ap() for g in range(3)]
            self.kvs = [DI(f"kvs{g}", [n_smp, self.WIN[g], 1024], F32, kind="ExternalOutput").ap() for g in range(3)]
            self.sconv = DI("sconv", [n_smp, 3, 2560], F32, kind="ExternalInput").ap()
            self.sssm = DI("sssm", [n_smp, 1536, P], F32, kind="ExternalInput").ap()
        self.flg = A("flg", [P, 2], F32)
        self.cw = A("cw", [P, 20, 5], F32)
        self.hv = A("hv", [P, 4, 24], F32)
        self.nw = A("nw", [P, 12], F32)
        self.maskf = A("maskf", [P, 512], F32)
        self.mask = A("mask", [P, 24, P], BF16)
        self.qb = A("qb", [P, 512], BF16)
        self.qb2 = A("qb2", [P, 512], BF16)
        self.qT = A("qT", [P, 12, P], BF16)
        self.NT = [2, 5, 17]
        self.kTh = [A(f"kTh{g}", [P, self.NT[g], 4, P], BF16) for g in range(3)]
        self.Vh = [A(f"Vh{g}", [P, self.NT[g], 8, 65], BF16) for g in range(3)]
        self.zs = A("zs", [P, 1536], F32)
        self.xraw = A("xraw", [P, 2560], BF16)
        self.NCH = [14, 6]
        self.stg = [A(f"stg{i}", [P, self.NCH[i], 131], F32) for i in range(2)]
        self.carry = A("carry", [P, 20, 3], F32)
        self.caccT = A("cacc", [P, 20 * P], F32)
        self.cacc = [self.caccT[:, 0:14 * P].rearrange("p (a b) -> p a b", a=14), self.caccT[:, 14 * P:20 * P].rearrange("p (a b) -> p a b", a=6)]
        self.CR = [[f"cacc0_{c}" for c in range(14)], ["cacc1"]]
        self.xc = A("xc", [P, 20, P], BF16)
        self.xstm = A("xstm", [P, 16, P], BF16)
        self.sm = A("sm", [P, 10, 24], F32)
        self.ssdbuf = A("ssdbuf", [P, 3 * 1536], BF16)
        self.xd = self.ssdbuf[:, 0:1536].rearrange("p (h e) -> p h e", h=24)
        self.xdp = self.ssdbuf[:, 1536:3072].rearrange("p (h e) -> p h e", h=24)
        self.cbm = A("cbm", [P, 4, P], F32)
        self.Dt = A("Dt", [P, 4, P], F32)
        self.Lt = A("Lt", [P, 4, P], F32)
        self.Mt = self.ssdbuf[:, 3072:4608].rearrange("p (h e) -> p h e", h=12)
        self.y = A("y", [P, 1536], F32)
        self.ytmp = A("ytmp", [P, 1536], F32)
        self.act = self.zs[:].bitcast(BF16)[:, 0:DFF]
        self.actT = self.y[:].bitcast(BF16)[:, 0:DFF].rearrange("p (c t) -> p c t", c=22)
        self.ctmp = self.ytmp[:, 0:1280].rearrange("p (a b) -> p a b", a=10)
        self.ctmp1 = self.y[:, 0:768].rearrange("p (a b) -> p a b", a=6)
        self.act2 = self.ytmp[:].bitcast(BF16)[:, 0:DFF]
        self.actT2 = self.caccT[:].bitcast(BF16)[:, 0:DFF].rearrange("p (c t) -> p c t", c=22)
        self.yn = A("yn", [P, 1536], BF16)
        self.ynT = A("ynT", [P, 12, P], BF16)
        self.ST = A("ST", [P, 1536], F32)
        self.STb = A("STb", [P, 1536], BF16)
        self.gates = A("gates", [P, 1024], F32)
        self.mrg = A("mrg", [P, 1024], F32)
        self.cur_x = [(self.x[0], "x0"), (self.x[1], "x1")]
        self.cur_g = [(self.gates, "gates"), (self.mrg, "mrg")]
        self.mrgb = A("mrgb", [P, 1024], BF16)
        self.mrgT = A("mrgT", [P, 8, P], BF16)
        self.PT = A("PT", [P, 8, P], BF16)
        self.oa = A("oa", [P, 512], BF16)
        self.oaT = A("oaT", [P, 4, P], BF16)
        self.pb = [nc.alloc_psum_tensor(f"pb{i}", [P, 512], F32) for i in range(8)]
        self.pb_i = 0

    def dump(self, name, ap, res, g=0, only_g=None):
        if not DBG or g != DBG_G:
            return
        shp = [int(v) for v in ap.shape]
        d = self.nc.dram_tensor("dbg_" + name, shp, ap.dtype, kind="ExternalOutput").ap()
        self.S.dma("sp", "dbg_" + name, d, ap, reads=[res], writes=[])

    @staticmethod
    def L(r):
        return [r] if isinstance(r, str) else list(r)

    def bank(self):
        i = self.pb_i
        self.pb_i = (i + 1) % 8
        return self.pb[i], f"pb{i}"

    def slab(self, wname, k0, nk, c0, ncols):
        S = self.S
        i = self.slot_i % self.nring
        self.slot_i = (i + 1) % self.nring
        if i < NSLOT:
            slot, wres = self.ws[i], [f"ws{i}"]
        else:
            slot, wres = self.ssdbuf[:, 0:4096].rearrange("p (k n) -> p k n", k=8), ["xd", "xdp", "Mt"]
        dst = slot[:, 0:nk, 0:ncols]
        key = (wname, k0, c0)
        res_scr = f"scr_{wname}_{k0}_{c0}"
        if key not in self.conv:
            scr = self.nc.dram_tensor(res_scr, [P, nk * ncols], BF16).ap().rearrange("p (k n) -> p k n", k=nk)
            self.conv[key] = scr
            src = self.w32[wname][k0 * P:(k0 + nk) * P, c0:c0 + ncols].rearrange("(kc p) n -> p kc n", p=P)
            S.dma("pool", f"lc{i}", dst, src, reads=[], writes=wres)
            S.dma("sp", f"sv{i}", scr, dst, reads=wres, writes=[res_scr])
        else:
            S.dma("sp", f"ld{i}", dst, self.conv[key], reads=[res_scr], writes=wres)
        return slot, wres

    def linear(self, actT, actT_res, KC, wname, c0, ncols, evac):
        self.linear_multi([(actT, actT_res)], KC, wname, c0, ncols, [evac])

    def linear_multi(self, acts, KC, wname, c0, ncols, evacs):
        S = self.S
        cb = 0
        while cb < ncols:
            n = min(512, ncols - cb)
            banks = [self.bank() for _ in acts]
            k0 = 0
            while k0 < KC:
                nk = min(8, KC - k0)
                slot, sr = self.slab(wname, k0, nk, c0 + cb, n)
                for (actT, ares), (ps, psr) in zip(acts, banks):
                    for j in range(nk):
                        first = (k0 + j == 0)
                        last = (k0 + j == KC - 1)
                        S.op("pe", (lambda eng, ps=ps, a=actT[:, k0 + j, :], r=slot[:, j, 0:n], f=first, l=last, n=n:
                                    eng.matmul(ps[:, 0:n], lhsT=a, rhs=r, start=f, stop=l)),
                             reads=self.L(ares) + sr, writes=[psr], inc=last or j == nk - 1)
                k0 += nk
            for (ps, psr), evac in zip(banks, evacs):
                evac(ps, psr, cb, n)
            cb += n

    def transpose_to(self, src, src_res, nchunks, dstT, dst_res, scale=None, scale_res="gpre"):
        S = self.S
        c = 0
        while c < nchunks:
            n = min(8, nchunks - c)
            ps, psr = self.bank()
            psb = ps[:].bitcast(BF16)
            for j in range(n):
                S.op("pe", (lambda eng, o=psb[:, j * P:(j + 1) * P], i=src[:, (c + j) * P:(c + j + 1) * P]:
                            eng.transpose(o, i, self.identb[:])),
                     reads=self.L(src_res) + ["identb"], writes=[psr], inc=(j == n - 1))
            src3 = psb[:, 0:n * P].rearrange("p (a b) -> p a b", a=n)
            if scale is None:
                S.op("act", (lambda eng, o=dstT[:, c:c + n, :], i=src3: eng.copy(out=o, in_=i)), reads=[psr], writes=self.L(dst_res))
            else:
                S.op("dve", (lambda eng, o=dstT[:, c:c + n, :], i=src3, sc=scale[:, c:c + n].unsqueeze(2).broadcast_to([P, n, P]):
                             eng.tensor_tensor(out=o, in0=i, in1=sc, op=ALU.mult)), reads=[psr, scale_res], writes=self.L(dst_res))
            c += n

    def rstd(self, ss_ap, out_ap, factor, res_in, res_out):
        S = self.S
        f2 = factor * factor
        S.op("dve", lambda eng: eng.tensor_scalar(out=out_ap, in0=ss_ap, scalar1=1.0 / (D * f2), scalar2=EPS / f2,
                                                  op0=ALU.mult, op1=ALU.add), reads=[res_in], writes=[res_out])
        S.op("act", lambda eng: eng.sqrt(out=out_ap, in_=out_ap), reads=[res_out], writes=[res_out])
        S.op("dve", lambda eng: eng.reciprocal(out=out_ap, in_=out_ap), reads=[res_out], writes=[res_out])

    def prenorm(self, x, xr, gi, hnT=None, hres="hnT"):
        S = self.S
        hnT = self.hnT if hnT is None else hnT
        S.op("act", lambda eng: eng.activation(out=self.junk[:], in_=x[:], func=AF.Square, accum_out=self.st[:, 0:1]),
             reads=[xr], writes=["xn", "st0"])
        self.rstd(self.st[:, 0:1], self.st[:, 1:2], 1.0, "st0", "st1")
        S.op("dve", lambda eng: eng.tensor_scalar(out=self.xn[:], in0=x[:], scalar1=self.st[:, 1:2], scalar2=None,
                                                  op0=ALU.mult), reads=[xr, "st1"], writes=["xn"])
        self.transpose_to(self.xn, "xn", 8, hnT, hres, scale=self.gpre[:, gi, :])

    def postnorm_evac(self, x, xr, gi, factor, sc=2):
        S = self.S
        held = []

        def evac(ps, psr, cb, n):
            k = len(held)
            S.op("act", lambda eng: eng.activation(out=self.junk[:, cb:cb + n], in_=ps[:, 0:n], func=AF.Square,
                                                   accum_out=self.st[:, sc + k:sc + 1 + k]),
                 reads=[psr], writes=["xn", f"st{sc + k}"])
            held.append((ps, psr, cb, n))

        def fin():
            S.op("dve", lambda eng: eng.tensor_tensor(out=self.st[:, sc + 2:sc + 3], in0=self.st[:, sc:sc + 1], in1=self.st[:, sc + 1:sc + 2],
                                                      op=ALU.add), reads=[f"st{sc}", f"st{sc + 1}"], writes=[f"st{sc + 2}"])
            self.rstd(self.st[:, sc + 2:sc + 3], self.st[:, sc + 3:sc + 4], factor, f"st{sc + 2}", f"st{sc + 3}")
            for ps, psr, cb, n in held:
                S.op("dve", lambda eng, ps=ps, cb=cb, n=n: eng.scalar_tensor_tensor(
                    out=self.tmp[:, 0:n], in0=ps[:, 0:n], scalar=self.st[:, sc + 3:sc + 4], in1=self.gpost[:, gi, cb:cb + n],
                    op0=ALU.mult, op1=ALU.mult), reads=[psr, f"st{sc + 3}", "gpost"], writes=["tmp"])
                S.op("dve", lambda eng, cb=cb, n=n: eng.tensor_tensor(out=x[:, cb:cb + n], in0=x[:, cb:cb + n],
                                                                     in1=self.tmp[:, 0:n], op=ALU.add),
                     reads=["tmp", xr], writes=[xr])
        return evac, fin

    def ffn(self, tiles, which):
        S = self.S
        pre, post = (0, 0) if which == 1 else (2, 2)
        gname, uname, dname = f"ffn{which}_gate", f"ffn{which}_up", f"ffn{which}_down"
        self.nring = NSLOT + 1
        bufs = [(self.hnT, "hnT", self.sg, "sg", self.act, "zs", self.actT, "y"),
                (self.mrgT, "mrgT", self.tmp, "tmp", self.act2, "ytmp", self.actT2, tuple(self.CR[0] + self.CR[1]))][:len(tiles)]
        for (x, xr), b in zip(tiles, bufs):
            self.prenorm(x, xr, pre, b[0], b[1])

        def mk_gate(b):
            def ev(ps, psr, cb, n):
                S.op("act", lambda eng, ps=ps: eng.activation(out=b[2][:, 0:n], in_=ps[:, 0:n], func=AF.Silu), reads=[psr], writes=[b[3]])
            return ev

        def mk_up(b, cb0):
            def ev(ps, psr, cb, n):
                S.op("dve", lambda eng, ps=ps: eng.tensor_tensor(out=b[4][:, cb0:cb0 + n], in0=ps[:, 0:n], in1=b[2][:, 0:n], op=ALU.mult),
                     reads=[psr, b[3]], writes=[b[5]])
            return ev
        acts = [(b[0], b[1]) for b in bufs]
        cb = 0
        while cb < DFF:
            n = min(512, DFF - cb)
            self.linear_multi(acts, 8, gname, cb, n, [mk_gate(b) for b in bufs])
            self.linear_multi(acts, 8, uname, cb, n, [mk_up(b, cb) for b in bufs])
            cb += n
        for b in bufs:
            self.transpose_to(b[4], b[5], 22, b[6], b[7])
        pn = [self.postnorm_evac(x, xr, post, 0.5, sc=2 + 6 * t) for t, (x, xr) in enumerate(tiles)]
        self.linear_multi([(b[6], b[7]) for b in bufs], 22, dname, 0, D, [p[0] for p in pn])
        for p in pn:
            p[1]()

    def bc(self, ap24, n):
        return ap24.unsqueeze(2).broadcast_to([P, ap24.shape[1], n])

    def mixer(self, x, xr, g, kind):
        S = self.S
        full = kind != "pre"
        do_kv = full or g >= self.n_pre - self.n_kvpre
        oi = g - self.n_pre if kind == "own" else -1
        bi = g - self.n_pre - self.n_own if kind == "smp" else -1
        (gates, gr), (mrg, mr) = self.cur_g
        V = lambda t, a, b2: t[:].rearrange("p (a b) -> p a b", a=a, b=b2)
        self.prenorm(x, xr, 1)
        triU = self.cstt[:, P:2 * P]
        ones = self.cstt[:, 2 * P:3 * P]
        slots = [0 if kind == "smp" else g % nt for nt in self.NT]
        if STOP <= 0:
            return
        def ev_x(ps, psr, cb, n):
            S.op("act", lambda eng, ps=ps: eng.copy(out=self.xraw[:, cb:cb + n], in_=ps[:, 0:n]), reads=[psr], writes=["xraw"])
        self.linear(self.hnT, "hnT", 8, "w_in", OFF_XBC, 2560, ev_x)
        for hf in range(2):
            c0, nch = (0, 14) if hf == 0 else (14, 6)
            CE = "dve" if hf == 0 else "pool"
            stg, cacc = self.stg[hf], self.cacc[hf]
            sr = f"stg{hf}"
            S.op(CE, lambda eng, c0=c0, nch=nch, stg=stg: eng.tensor_copy(out=stg[:, :, 0:3], in_=self.carry[:, c0:c0 + nch, :]), reads=["carry"], writes=[sr])
            self.transpose_to(self.xraw[:, c0 * P:(c0 + nch) * P], "xraw", nch, stg[:, :, 3:131], sr)
            S.op(CE, lambda eng, c0=c0, nch=nch, stg=stg: eng.tensor_copy(out=self.carry[:, c0:c0 + nch, :], in_=stg[:, :, 128:131]), reads=[sr], writes=["carry"])
            if hf == 0:
                for tap in range(4):
                    for ci in range(nch):
                        w = self.cw[:, c0 + ci, tap:tap + 1]
                        cr = self.CR[0][ci]
                        if tap == 0:
                            S.op(CE, lambda eng, ci=ci, w=w, stg=stg, cacc=cacc, b=self.cw[:, c0 + ci, 4:5]: eng.tensor_scalar(
                                out=cacc[:, ci, :], in0=stg[:, ci, 0:P], scalar1=w, scalar2=b, op0=ALU.mult, op1=ALU.add), reads=[sr, "cw"], writes=[cr])
                        else:
                            S.op(CE, lambda eng, ci=ci, w=w, stg=stg, cacc=cacc, tap=tap: eng.scalar_tensor_tensor(
                                out=cacc[:, ci, :], in0=stg[:, ci, tap:tap + P], scalar=w, in1=cacc[:, ci, :], op0=ALU.mult, op1=ALU.add), reads=[sr, "cw", cr], writes=[cr])
            else:
                ctmp, ctr, ar = self.ctmp1, "y", "cacc1"
                for tap in range(4):
                    wv = self.cw[:, c0:c0 + nch, tap:tap + 1].broadcast_to([P, nch, P])
                    if tap == 0:
                        S.op(CE, lambda eng, wv=wv, stg=stg, cacc=cacc: eng.tensor_tensor(out=cacc, in0=stg[:, :, 0:P], in1=wv, op=ALU.mult), reads=[sr, "cw"], writes=[ar])
                    else:
                        S.op(CE, lambda eng, wv=wv, stg=stg, tap=tap, ctmp=ctmp: eng.tensor_tensor(out=ctmp, in0=stg[:, :, tap:tap + P], in1=wv, op=ALU.mult), reads=[sr, "cw"], writes=[ctr])
                        S.op(CE, lambda eng, cacc=cacc, ctmp=ctmp: eng.tensor_tensor(out=cacc, in0=cacc, in1=ctmp, op=ALU.add), reads=[ar, ctr], writes=[ar])
                S.op(CE, lambda eng, c0=c0, nch=nch, cacc=cacc: eng.tensor_tensor(out=cacc, in0=cacc, in1=self.cw[:, c0:c0 + nch, 4:5].broadcast_to([P, nch, P]), op=ALU.add),
                     reads=[ar, "cw"], writes=[ar])
        for dg in range(3):
            if not do_kv:
                break
            base = dg * 1536
            sl = slots[dg]
            need_rows = kind == "smp" or (kind == "own" and self.WIN[dg] - (self.n_own - oi) * P >= 0)
            if full:
                def ev_q(ps, psr, cb, n):
                    S.op("act", lambda eng, ps=ps: eng.mul(out=self.qb[:], in_=ps[:, 0:512], mul=0.125), reads=[psr], writes=["qb"])
                self.linear(self.hnT, "hnT", 8, "w_in", base, 512, ev_q)
                self.transpose_to(self.qb, "qb", 4, self.qT[:, dg * 4:(dg + 1) * 4, :], "qT")

            def ev_k(ps, psr, cb, n):
                S.op("act", lambda eng, ps=ps: eng.copy(out=self.qb2[:], in_=ps[:, 0:512]), reads=[psr], writes=["qb2"])
                if need_rows:
                    S.op("dve", lambda eng, ps=ps: eng.tensor_copy(out=mrg[:, 0:512], in_=ps[:, 0:512]), reads=[psr], writes=[mr])
            self.linear(self.hnT, "hnT", 8, "w_in", base + 512, 512, ev_k)
            self.transpose_to(self.qb2, "qb2", 4, self.kTh[dg][:, sl, :, :], f"kTh{dg}_{sl}")

            def ev_v(ps, psr, cb, n, dg=dg, sl=sl):
                S.op("act", lambda eng, ps=ps: eng.copy(out=self.Vh[dg][:, sl, :, 0:64], in_=ps[:, 0:512].rearrange("p (h e) -> p h e", h=8)),
                     reads=[psr], writes=[f"Vh{dg}_{sl}"])
                if need_rows:
                    S.op("dve", lambda eng, ps=ps: eng.tensor_copy(out=mrg[:, 512:1024], in_=ps[:, 0:512]), reads=[psr], writes=[mr])
            self.linear(self.hnT, "hnT", 8, "w_in", base + 1024, 512, ev_v)
            if kind == "pre":
                S.op("dve", lambda eng, dg=dg, sl=sl: eng.tensor_copy(out=self.Vh[dg][:, sl, :, 64:65], in_=self.flg[:, 0:1].unsqueeze(1).broadcast_to([P, 8, 1])),
                     reads=["flg"], writes=[f"Vh{dg}_{sl}"])
            elif kind == "own":
                S.op("dve", lambda eng, dg=dg, sl=sl: eng.memset(self.Vh[dg][:, sl, :, 64:65], 1.0), writes=[f"Vh{dg}_{sl}"])
                r0 = self.WIN[dg] - (self.n_own - oi) * P
                if r0 >= 0:
                    S.dma("pool", f"kvo{dg}", self.kvo[dg][r0:r0 + P, :], mrg[:], reads=[mr], writes=[])
            else:
                W = self.WIN[dg]
                S.dma("pool", f"kvo{dg}", self.kvs[dg][bi, W - 4:W, :], mrg[0:4, :], reads=[mr], writes=[])
        if STOP == 10:
            return
        if full:
            def ev_z(ps, psr, cb, n):
                S.op("act", lambda eng: eng.activation(out=self.zs[:, cb:cb + n], in_=ps[:, 0:n], func=AF.Silu), reads=[psr], writes=["zs"])
            self.linear(self.hnT, "hnT", 8, "w_in", OFF_Z, 1536, ev_z)
        for hf, (c0, nch) in enumerate(((0, 14), (14, 6))):
            S.op("act", lambda eng, hf=hf, c0=c0, nch=nch: eng.activation(out=self.xc[:, c0:c0 + nch, :], in_=self.cacc[hf], func=AF.Silu), reads=self.CR[hf], writes=["xc"])
        self.transpose_to(self.xc[:].rearrange("p a b -> p (a b)"), "xc", 16, self.xstm, "xstm")
        if STOP == 11:
            return
        sm = self.sm
        def ev_dt(ps, psr, cb, n):
            S.op("dve", lambda eng: eng.tensor_tensor(out=sm[:, 0, :], in0=ps[:, 0:24], in1=self.hv[:, 0, :], op=ALU.add), reads=[psr, "hv"], writes=["sm0"])
        self.linear(self.hnT, "hnT", 8, "w_in", OFF_DT, 24, ev_dt)
        S.op("act", lambda eng: eng.activation(out=sm[:, 0, :], in_=sm[:, 0, :], func=AF.Exp), reads=["sm0"], writes=["sm0"])
        S.op("act", lambda eng: eng.activation(out=sm[:, 1, :], in_=sm[:, 0, :], func=AF.Ln, bias=1.0), reads=["sm0"], writes=["sm1"])
        if kind != "own":
            fc = 0 if kind == "pre" else 1
            S.op("dve", lambda eng: eng.tensor_scalar(out=sm[:, 1, :], in0=sm[:, 1, :], scalar1=self.flg[:, fc:fc + 1], scalar2=None, op0=ALU.mult), reads=["sm1", "flg"], writes=["sm1"])
        S.op("dve", lambda eng: eng.tensor_tensor(out=sm[:, 2, :], in0=sm[:, 1, :], in1=self.hv[:, 1, :], op=ALU.mult), reads=["sm1", "hv"], writes=["sm2"])
        self.dump("zs", self.zs[:], "zs", g)
        self.dump("xc", self.xc[:], "xc", g)
        self.dump("xstm", self.xstm[:], "xstm", g)
        if STOP <= 1:
            return
        ps1, p1r = self.bank()
        S.op("pe", lambda eng: eng.matmul(ps1[:, 0:24], lhsT=triU, rhs=sm[:, 2, :], start=True, stop=True), reads=["sm2", "cstt"], writes=[p1r], inc=False)
        S.op("pe", lambda eng: eng.matmul(ps1[:, 32:56], lhsT=ones, rhs=sm[:, 2, :], start=True, stop=True), reads=["sm2", "cstt"], writes=[p1r])
        S.op("act", lambda eng: eng.copy(out=sm[:, 3, :], in_=ps1[:, 0:24]), reads=[p1r], writes=["sm3"])
        S.op("dve", lambda eng: eng.tensor_scalar(out=sm[:, 4, :], in0=ps1[:, 0:24], scalar1=-1.0, scalar2=None, op0=ALU.mult), reads=[p1r], writes=["sm4"])
        S.op("act", lambda eng: eng.activation(out=sm[:, 5, :], in_=ps1[:, 0:24], func=AF.Exp), reads=[p1r], writes=["sm5"])
        S.op("dve", lambda eng: eng.tensor_tensor(out=sm[:, 6, :], in0=ps1[:, 32:56], in1=sm[:, 3, :], op=ALU.subtract), reads=[p1r, "sm3"], writes=["sm6"])
        S.op("act", lambda eng: eng.activation(out=sm[:, 6, :], in_=sm[:, 6, :], func=AF.Exp), reads=["sm6"], writes=["sm6"])
        S.op("dve", lambda eng: eng.tensor_tensor(out=sm[:, 6, :], in0=sm[:, 6, :], in1=sm[:, 1, :], op=ALU.mult), reads=["sm6", "sm1"], writes=["sm6"])
        S.op("act", lambda eng: eng.activation(out=sm[:, 7, :], in_=ps1[:, 32:56], func=AF.Exp), reads=[p1r], writes=["sm7"])
        xs3 = self.xstm[:, 0:12, :].rearrange("p a (h e) -> p (a h) e", h=2)
        S.op("dve", lambda eng: eng.tensor_tensor(out=self.xd, in0=xs3, in1=self.bc(sm[:, 1, :], 64), op=ALU.mult), reads=["xstm", "sm1"], writes=["xd"])
        S.op("pool", lambda eng: eng.tensor_tensor(out=self.xdp, in0=xs3, in1=self.bc(sm[:, 6, :], 64), op=ALU.mult), reads=["xstm", "sm6"], writes=["xdp"])
        if full:
            ps2, p2r = self.bank()
            for gg in range(4):
                S.op("pe", lambda eng, gg=gg: eng.matmul(ps2[:, gg * P:(gg + 1) * P], lhsT=self.xc[:, 12 + gg, :], rhs=self.xc[:, 16 + gg, :], start=True, stop=True),
                     reads=["xc"], writes=[p2r], inc=(gg == 3))
            S.op("dve", lambda eng: eng.tensor_tensor(out=self.cbm[:], in0=ps2[:].rearrange("p (a b) -> p a b", a=4), in1=triU.unsqueeze(1).broadcast_to([P, 4, P]), op=ALU.mult),
                 reads=[p2r, "cstt"], writes=["cbm"])
            for gg in range(4):
                po, por = self.bank()
                S.op("pe", lambda eng, gg=gg, po=po: eng.matmul(po[:, 0:384], lhsT=self.xc[:, 16 + gg, :], rhs=self.STb[:, gg * 384:(gg + 1) * 384], start=True, stop=True),
                     reads=["xc", "STb"], writes=[por])
                S.op("dve", lambda eng, gg=gg, po=po: eng.tensor_tensor(out=self.y[:, gg * 384:(gg + 1) * 384].rearrange("p (h e) -> p h e", h=6),
                                                                         in0=po[:, 0:384].rearrange("p (h e) -> p h e", h=6),
                                                                         in1=self.bc(sm[:, 5, gg * 6:(gg + 1) * 6], 64), op=ALU.mult),
                     reads=[por, "sm5"], writes=["y"])
            for hh in range(2):
                for hb in range(3):
                    pd, pdr = self.bank()
                    for jj in range(4):
                        h = hh * 12 + hb * 4 + jj
                        S.op("pe", lambda eng, h=h, jj=jj, pd=pd: eng.matmul(pd[:, jj * P:(jj + 1) * P], lhsT=sm[:, 2, h:h + 1].broadcast_to([P, P]), rhs=triU, start=True, stop=True),
                             reads=["sm2", "cstt"], writes=[pdr], inc=(jj == 3))
                    for jj in range(4):
                        h = hh * 12 + hb * 4 + jj
                        S.op("dve", lambda eng, h=h, jj=jj, pd=pd: eng.tensor_scalar(out=self.Dt[:, jj, :], in0=pd[:, jj * P:(jj + 1) * P], scalar1=sm[:, 4, h:h + 1], scalar2=0.0,
                                                                                   op0=ALU.add, op1=ALU.min), reads=[pdr, "sm4"], writes=["Dt"])
                    S.op("act", lambda eng: eng.activation(out=self.Lt[:], in_=self.Dt[:], func=AF.Exp), reads=["Dt"], writes=["Lt"])
                    for jj in range(4):
                        h = hh * 12 + hb * 4 + jj
                        S.op("pool", lambda eng, h=h, jj=jj, hb=hb: eng.tensor_tensor(out=self.Mt[:, hb * 4 + jj, :], in0=self.Lt[:, jj, :], in1=self.cbm[:, h // 6, :], op=ALU.mult),
                             reads=["Lt", "cbm"], writes=["Mt"])
                py, pyr = [], []
                for k in range(2):
                    a, b2 = self.bank()
                    py.append(a); pyr.append(b2)
                for j12 in range(12):
                    h = hh * 12 + j12
                    S.op("pe", lambda eng, h=h, j12=j12, py=py: eng.matmul(py[j12 // 8][:, (j12 % 8) * 64:(j12 % 8) * 64 + 64], lhsT=self.Mt[:, j12, :], rhs=self.xd[:, h, :], start=True, stop=True),
                         reads=["Mt", "xd"], writes=[pyr[j12 // 8]], inc=(j12 in (7, 11)))
                S.op("dve", lambda eng, hh=hh, py=py: eng.tensor_tensor(out=self.y[:, hh * 768:hh * 768 + 512], in0=self.y[:, hh * 768:hh * 768 + 512], in1=py[0][:, 0:512], op=ALU.add),
                     reads=["y", pyr[0]], writes=["y"])
                S.op("dve", lambda eng, hh=hh, py=py: eng.tensor_tensor(out=self.y[:, hh * 768 + 512:hh * 768 + 768], in0=self.y[:, hh * 768 + 512:hh * 768 + 768], in1=py[1][:, 0:256], op=ALU.add),
                     reads=["y", pyr[1]], writes=["y"])
        for gg in range(4):
            pS, pSr = self.bank()
            S.op("pe", lambda eng, gg=gg, pS=pS: eng.matmul(pS[:, 0:384], lhsT=self.xstm[:, 12 + gg, :], rhs=self.xdp[:, gg * 6:(gg + 1) * 6, :].rearrange("p h e -> p (h e)"), start=True, stop=True),
                 reads=["xstm", "xdp"], writes=[pSr])
            stv = self.ST[:, gg * 384:(gg + 1) * 384].rearrange("p (h e) -> p h e", h=6)
            S.op("pool", lambda eng, gg=gg, stv=stv: eng.tensor_tensor(out=stv, in0=stv, in1=self.bc(sm[:, 7, gg * 6:(gg + 1) * 6], 64), op=ALU.mult), reads=["ST", "sm7"], writes=["ST"])
            S.op("dve", lambda eng, gg=gg, pS=pS: eng.tensor_tensor(out=self.ST[:, gg * 384:(gg + 1) * 384], in0=self.ST[:, gg * 384:(gg + 1) * 384], in1=pS[:, 0:384], op=ALU.add),
                 reads=["ST", pSr], writes=["ST"])
        S.op("act", lambda eng: eng.copy(out=self.STb[:], in_=self.ST[:]), reads=["ST"], writes=["STb"])
        if kind == "smp":
            self.store_conv(1 + bi, 1)
            self.store_state(1 + bi)
        elif kind == "own" and oi == self.n_own - 1:
            self.store_conv(0, 125)
            self.store_state(0)
        self.dump("sm", self.sm[:], "sm7", g)
        self.dump("ST", self.ST[:], "ST", g)
        if not full:
            return
        self.dump("y0", self.y[:], "y", g)
        if STOP <= 2:
            return
        t3 = self.ytmp[:].rearrange("p (h e) -> p h e", h=24)
        S.op("pool", lambda eng: eng.tensor_tensor(out=t3, in0=xs3, in1=self.bc(self.hv[:, 2, :], 64), op=ALU.mult), reads=["xstm", "hv"], writes=["ytmp"])
        S.op("dve", lambda eng: eng.tensor_tensor(out=self.y[:], in0=self.y[:], in1=self.ytmp[:], op=ALU.add), reads=["y", "ytmp"], writes=["y"])
        S.op("dve", lambda eng: eng.tensor_tensor(out=self.y[:], in0=self.y[:], in1=self.zs[:], op=ALU.mult), reads=["y", "zs"], writes=["y"])
        for gg in range(4):
            S.op("act", lambda eng, gg=gg: eng.activation(out=self.ytmp[:, gg * 384:(gg + 1) * 384], in_=self.y[:, gg * 384:(gg + 1) * 384], func=AF.Square, accum_out=sm[:, 8, gg:gg + 1]),
                 reads=["y"], writes=["ytmp", "sm8"])
        S.op("dve", lambda eng: eng.tensor_scalar(out=sm[:, 8, 0:4], in0=sm[:, 8, 0:4], scalar1=1.0 / 384, scalar2=EPS, op0=ALU.mult, op1=ALU.add), reads=["sm8"], writes=["sm8"])
        S.op("act", lambda eng: eng.sqrt(out=sm[:, 8, 0:4], in_=sm[:, 8, 0:4]), reads=["sm8"], writes=["sm8"])
        S.op("dve", lambda eng: eng.reciprocal(out=sm[:, 8, 0:4], in_=sm[:, 8, 0:4]), reads=["sm8"], writes=["sm8"])
        S.op("dve", lambda eng: eng.tensor_tensor(out=self.yn[:].rearrange("p (g e) -> p g e", g=4), in0=self.y[:].rearrange("p (g e) -> p g e", g=4), in1=self.bc(sm[:, 8, 0:4], 384), op=ALU.mult),
             reads=["y", "sm8"], writes=["yn"])
        self.dump("yn", self.yn[:], "yn", g)
        self.transpose_to(self.yn, "yn", 12, self.ynT, "ynT", scale=self.nw, scale_res="nw")
        def ev_gate(off):
            def ev(ps, psr, cb, n):
                S.op("act", lambda eng: eng.activation(out=gates[:, cb:cb + n], in_=ps[:, 0:n], func=AF.Sigmoid), reads=[psr], writes=[gr])
            return ev
        self.linear(self.hnT, "hnT", 8, "w_in", OFF_GATE + 1024, 1024, ev_gate(1024))
        def ev_b(ps, psr, cb, n):
            S.op("dve", lambda eng: eng.tensor_tensor(out=mrg[:, cb:cb + n], in0=ps[:, 0:n], in1=gates[:, cb:cb + n], op=ALU.mult), reads=[psr, gr], writes=[mr])
        self.linear(self.ynT, "ynT", 12, "w_branch_b", 0, 1024, ev_b)
        self.dump(mr, mrg[:], mr, g)
        if STOP <= 3:
            return
        mbase = [0, 2, 7]
        if kind == "smp":
            tiles = [(dg, o, mbase[dg] + o, o) for dg in range(3) for o in range(self.NT[dg])]
        else:
            first = self.n_pre - self.n_kvpre
            tiles = [(dg, o, mbase[dg] + o, (g - o) % self.NT[dg]) for dg in range(3) for o in range(self.NT[dg]) if g - o >= first]
        self.attention(tiles)
        self.dump("oa", self.oa[:], "oa", g)
        self.transpose_to(self.oa, "oa", 4, self.oaT, "oaT")
        self.linear(self.hnT, "hnT", 8, "w_in", OFF_GATE, 1024, ev_gate(0))
        def ev_a(ps, psr, cb, n):
            S.op("dve", lambda eng: eng.tensor_tensor(out=self.tmp[:, 0:n], in0=ps[:, 0:n], in1=gates[:, cb:cb + n], op=ALU.mult), reads=[psr, gr], writes=["tmp"])
            S.op("dve", lambda eng: eng.tensor_tensor(out=self.mrgb[:, cb:cb + n], in0=mrg[:, cb:cb + n], in1=self.tmp[:, 0:n], op=ALU.add), reads=[mr, "tmp"], writes=["mrgb"])
        self.linear(self.oaT, "oaT", 4, "w_branch_a", 0, 1024, ev_a)
        self.dump("mrgb", self.mrgb[:], "mrgb", g)
        self.transpose_to(self.mrgb, "mrgb", 8, self.mrgT, "mrgT")
        evac, fin = self.postnorm_evac(x, xr, 1, 1.0)
        self.linear(self.mrgT, "mrgT", 8, "w_out", 0, D, evac)
        fin()

    def attention(self, tiles):
        S = self.S
        outb = [self.bank() for _ in range(2)]
        scb = [self.bank() for _ in range(4)]
        PT = self.PT
        r = 0
        for h in range(8):
            c, pb = h // 2, 64 * (h % 2)
            ob, obr = outb[h // 4]
            oc = (h % 4) * 65
            for b0 in range(0, len(tiles), 8):
                batch = tiles[b0:b0 + 8]
                nb = len(batch)
                banks = scb[2 * (r % 2):2 * (r % 2) + 2]
                r += 1
                for j, (dg, o, m, sl) in enumerate(batch):
                    ps, psr = banks[j // 4]
                    S.op("pe", lambda eng, j=j, dg=dg, sl=sl, ps=ps, pb=pb, c=c: eng.matmul(ps[:, (j % 4) * P:(j % 4 + 1) * P], lhsT=self.kTh[dg][pb:pb + 64, sl, c, :], rhs=self.qT[pb:pb + 64, dg * 4 + c, :], start=True, stop=True),
                         reads=[f"kTh{dg}_{sl}", "qT"], writes=[psr], inc=(j % 4 == 3 or j == nb - 1))
                for k in range((nb + 3) // 4):
                    ps, psr = banks[k]
                    n4 = min(4, nb - 4 * k)
                    pr = f"PT{k}"
                    S.op("act", lambda eng, ps=ps, k=k, n4=n4: eng.activation(out=PT[:, 4 * k:4 * k + n4, :], in_=ps[:, 0:n4 * P].rearrange("p (a b) -> p a b", a=n4), func=AF.Exp), reads=[psr], writes=[pr])
                    ms = [t[2] for t in batch[4 * k:4 * k + n4]]
                    if ms == list(range(ms[0], ms[0] + n4)):
                        S.op("dve", lambda eng, k=k, n4=n4, m0=ms[0]: eng.tensor_tensor(out=PT[:, 4 * k:4 * k + n4, :], in0=PT[:, 4 * k:4 * k + n4, :], in1=self.mask[:, m0:m0 + n4, :], op=ALU.mult), reads=[pr, "mask"], writes=[pr])
                    else:
                        for j, m in enumerate(ms):
                            S.op("dve", lambda eng, j=4 * k + j, m=m: eng.tensor_tensor(out=PT[:, j, :], in0=PT[:, j, :], in1=self.mask[:, m, :], op=ALU.mult), reads=[pr, "mask"], writes=[pr])
                for j, (dg, o, m, sl) in enumerate(batch):
                    first = (b0 + j == 0)
                    last = (b0 + j == len(tiles) - 1)
                    S.op("pe", lambda eng, j=j, dg=dg, sl=sl, first=first, last=last, ob=ob, oc=oc, h=h: eng.matmul(ob[:, oc:oc + 65], lhsT=PT[:, j, :], rhs=self.Vh[dg][:, sl, h, :], start=first, stop=last),
                         reads=[f"PT{j // 4}", f"Vh{dg}_{sl}"], writes=[obr], inc=(j % 4 == 3 or j == nb - 1))
        for k in range(2):
            ob, obr = outb[k]
            ob3 = ob[:, 0:260].rearrange("p (h e) -> p h e", e=65)
            S.op("dve", lambda eng, ob3=ob3, k=k: eng.reciprocal(out=self.sm[:, 9, 4 * k:4 * k + 4].unsqueeze(2), in_=ob3[:, :, 64:65]), reads=[obr], writes=["sm9"])
            S.op("dve", lambda eng, ob3=ob3, k=k: eng.tensor_tensor(out=self.oa[:, 256 * k:256 * k + 256].rearrange("p (h e) -> p h e", e=64), in0=ob3[:, :, 0:64],
                                                                   in1=self.bc(self.sm[:, 9, 4 * k:4 * k + 4], 64), op=ALU.mult), reads=[obr, "sm9"], writes=["oa"])

    def store_conv(self, idx, r0):
        self.S.dma("pool", "cvo", self.convo[idx], self.xraw[r0:r0 + 3, :], reads=["xraw"], writes=[])

    def store_state(self, idx):
        S = self.S
        id32 = self.cstt[:, 0:P]
        for c0 in range(0, 12, 4):
            ps, psr = self.bank()
            for j in range(4):
                S.op("pe", lambda eng, ps=ps, j=j, c=c0 + j: eng.transpose(ps[:, j * P:(j + 1) * P], self.ST[:, c * P:(c + 1) * P], id32),
                     reads=["ST", "cstt"], writes=[psr], inc=(j == 3))
            S.op("act", lambda eng, ps=ps, c0=c0: eng.copy(out=self.ytmp[:, c0 * P:(c0 + 4) * P], in_=ps[:, 0:512]), reads=[psr], writes=["ytmp"])
        S.dma("pool", "sso", self.ssmo[idx].rearrange("(c p) n -> p c n", p=P), self.ytmp[:].rearrange("p (c n) -> p c n", c=12), reads=["ytmp"], writes=[])

    def load_sample(self, bi):
        S = self.S
        id32 = self.cstt[:, 0:P]
        for dg in range(3):
            nt = self.NT[dg]
            for j in range(nt - 1):
                sl = nt - 1 - j
                src = self.cache[dg][bi, j * P:(j + 1) * P, :]
                kb, kr = (self.qb, "qb") if j % 2 == 0 else (self.qb2, "qb2")
                S.dma("pool", f"ck{j % 2}", kb[:], src[:, 0:512], reads=[], writes=[kr])
                self.transpose_to(kb, kr, 4, self.kTh[dg][:, sl, :, :], f"kTh{dg}_{sl}")
                S.dma("pool", f"cv{dg}_{sl}", self.Vh[dg][:, sl, :, 0:64], src[:, 512:1024].rearrange("p (h e) -> p h e", h=8),
                      reads=[], writes=[f"Vh{dg}_{sl}"])
        S.dma("sp", "ssi", self.ytmp[:].rearrange("p (c n) -> p c n", c=12), self.sssm[bi].rearrange("(c p) n -> p c n", p=P), reads=[], writes=["ytmp"])
        for c0 in range(0, 12, 4):
            ps, psr = self.bank()
            for j in range(4):
                S.op("pe", lambda eng, ps=ps, j=j, c=c0 + j: eng.transpose(ps[:, j * P:(j + 1) * P], self.ytmp[:, c * P:(c + 1) * P], id32),
                     reads=["ytmp", "cstt"], writes=[psr], inc=(j == 3))
            S.op("act", lambda eng, ps=ps, c0=c0: eng.copy(out=self.ST[:, c0 * P:(c0 + 4) * P], in_=ps[:, 0:512]), reads=[psr], writes=["ST"])
        S.op("act", lambda eng: eng.copy(out=self.STb[:], in_=self.ST[:]), reads=["ST"], writes=["STb"])
        for hf in range(2):
            S.dma("sp", "sci", self.cacc[0][0:3, 0:10, :].rearrange("p a b -> p (a b)"), self.sconv[bi][:, hf * 1280:(hf + 1) * 1280], reads=[], writes=self.CR[0][0:10])
            ps, psr = self.bank()
            for c in range(10):
                S.op("pe", lambda eng, ps=ps, c=c: eng.matmul(ps[:, c * 3:(c + 1) * 3], lhsT=self.cacc[0][0:3, c, :], rhs=self.cstt[0:3, 0:3], start=True, stop=True),
                     reads=[self.CR[0][c], "cstt"], writes=[psr], inc=(c == 9))
            S.op("act", lambda eng, ps=ps, hf=hf: eng.copy(out=self.carry[:, hf * 10:(hf + 1) * 10, :], in_=ps[:, 0:30].rearrange("p (c t) -> p c t", t=3)),
                 reads=[psr], writes=["carry"])

    def setup(self):
        S = self.S
        S.dma("sp", "c0", self.cstt[:], self.cst, reads=[], writes=["cstt"])
        S.dma("sp", "c1", self.gpost[:], self.gvec[3:6, :].partition_broadcast(P), reads=[], writes=["gpost"])
        S.dma("sp", "c2", self.gpre[:], self.gpre_d.rearrange("p (g c) -> p g c", g=3), reads=[], writes=["gpre"])
        S.op("dve", lambda eng: eng.tensor_copy(out=self.identb[:], in_=self.cstt[:, 0:P]), reads=["cstt"], writes=["identb"])
        S.dma("sp", "c3", self.cw[:], self.cw_d.rearrange("p (c t) -> p c t", t=5), reads=[], writes=["cw"])
        S.dma("sp", "c4", self.hv[:], self.hv_d.partition_broadcast(P), reads=[], writes=["hv"])
        S.dma("sp", "c5", self.nw[:], self.nw_d, reads=[], writes=["nw"])
        S.op("act", lambda eng: eng.activation(out=self.hv[:, 1, :], in_=self.hv[:, 1, :], func=AF.Exp), reads=["hv"], writes=["hv"])
        S.op("dve", lambda eng: eng.tensor_scalar(out=self.hv[:, 1, :], in0=self.hv[:, 1, :], scalar1=-1.0, scalar2=None, op0=ALU.mult), reads=["hv"], writes=["hv"])
        for i in range(6):
            S.dma("sp", "c6", self.maskf[:], self.mask_d[:, i * 512:(i + 1) * 512], reads=[], writes=["maskf"])
            S.op("dve", lambda eng, i=i: eng.tensor_copy(out=self.mask[:, i * 4:(i + 1) * 4, :], in_=self.maskf[:].rearrange("p (a b) -> p a b", a=4)), reads=["maskf"], writes=["mask"])
        S.op("dve", lambda eng: eng.memset(self.ST[:], 0.0), writes=["ST"])
        S.op("dve", lambda eng: eng.memset(self.STb[:], 0.0), writes=["STb"])
        S.op("dve", lambda eng: eng.memset(self.carry[:], 0.0), writes=["carry"])
        for dg in range(3):
            S.op("dve", lambda eng, dg=dg: eng.memset(self.Vh[dg][:], 1.0), writes=[f"Vh{dg}_{sl}" for sl in range(self.NT[dg])])
        S.dma("sp", "c7", self.flg[:], self.flg_d, reads=[], writes=["flg"])
        for dg in range(3):
            W = self.WIN[dg]
            for b in range(self.n_smp):
                for r0 in range(4, W, 256):
                    r1 = min(W, r0 + 256)
                    S.dma("act", "kvcp", self.kvs[dg][b, r0 - 4:r1 - 4, :], self.cache[dg][b, r0:r1, :], reads=[], writes=[])

    def load_x(self, gs, bufs):
        for t, g in enumerate(gs):
            self.S.dma("sp", f"xin{t}", bufs[t][0][:], self.xs[g * P:(g + 1) * P, :], reads=[], writes=[bufs[t][1]])

    def pair(self, gs, nxt):
        S = self.S
        kind = self.kinds[gs[0]]
        tiles = self.cur_x[:len(gs)]
        self.ffn(tiles, 1)
        if kind == "pre" and nxt:
            self.load_x(nxt, self.cur_g)
        for (x, xr), g in zip(tiles, gs):
            if kind == "smp":
                bi = g - self.n_pre - self.n_own
                if bi == 0:
                    for dg in range(3):
                        S.op("dve", lambda eng, dg=dg: eng.memset(self.Vh[dg][:, :, :, 64:65], 1.0), writes=[f"Vh{dg}_{sl}" for sl in range(self.NT[dg])])
                self.load_sample(bi)
            self.dump("h1", x[:], xr, g, DBG_G)
            self.mixer(x, xr, g, kind)
            self.dump("h2", x[:], xr, g, DBG_G)
        if kind != "pre":
            if nxt:
                self.load_x(nxt, self.cur_g)
            self.ffn(tiles, 2)
            for (x, xr), g in zip(tiles, gs):
                o = g - self.n_pre
                S.dma("pool", f"yout{o % 2}", self.ys[o * P:(o + 1) * P, :], x[:], reads=[xr], writes=[])
        self.cur_x, self.cur_g = self.cur_g, self.cur_x

    def build(self):
        self.setup()
        pairs = []
        g = 0
        while g < len(self.kinds):
            gs = [g]
            if g + 1 < len(self.kinds) and self.kinds[g + 1] == self.kinds[g]:
                gs.append(g + 1)
            pairs.append(gs)
            g += len(gs)
        self.load_x(pairs[0], self.cur_x)
        for i, gs in enumerate(pairs):
            self.pair(gs, pairs[i + 1] if i + 1 < len(pairs) else None)
        self.S.finish()
        self.S.emit()
        return self.nc


PARAMS = ["w_in", "conv_w", "conv_b", "dt_bias", "a_log", "d_skip", "ssd_norm_w", "w_branch_a", "w_branch_b", "w_out",
          "ffn1_gate", "ffn1_up", "ffn1_down", "ffn2_gate", "ffn2_up", "ffn2_down",
          "g_pre_ffn1", "g_post_ffn1", "g_pre_mix", "g_post_mix", "g_pre_ffn2", "g_post_ffn2"]


def _common_inputs(p):
    f = lambda a: np.ascontiguousarray(np.asarray(a, dtype=np.float32))
    ins = {n: f(p[n]) for n in WEIGHTS}
    gvec = np.stack([f(p[k]) for k in ("g_pre_ffn1", "g_pre_mix", "g_pre_ffn2", "g_post_ffn1", "g_post_mix", "g_post_ffn2")])
    ins["gvec"] = gvec
    ins["gpre_d"] = f(gvec[0:3].reshape(3, 8, P).transpose(2, 0, 1).reshape(P, 24))
    ins["cst"] = f(np.concatenate([np.eye(P), np.triu(np.ones((P, P))), np.ones((P, P))], 1))
    cw = np.concatenate([f(p["conv_w"]), f(p["conv_b"])[None]], 0)
    ins["cw_d"] = f(cw.reshape(5, 20, P).transpose(2, 1, 0).reshape(P, 100))
    ins["hv_d"] = f(np.stack([f(p["dt_bias"]), f(p["a_log"]), f(p["d_skip"]), np.zeros(24, np.float32)]))
    ins["nw_d"] = f(f(p["ssd_norm_w"]).reshape(12, P).T)
    k = np.arange(P)[:, None]
    q = np.arange(P)[None, :]
    ms = []
    for (W, dil), nt in zip(((128, 1), (512, 4), (2048, 16)), (2, 5, 17)):
        for o in range(nt):
            d = q + P * o - k
            ms.append(((d >= 0) & (d <= W) & (d % dil == 0)).astype(np.float32))
    ins["mask_d"] = f(np.stack(ms, 1).reshape(P, 24 * P))
    return ins


def _run(inp, ncores):
    f = lambda a: np.ascontiguousarray(np.asarray(a, dtype=np.float32))
    xp = f(inp["x_prompt"])
    xsm = f(inp["x_sample"])
    NB, SEQ, _ = xp.shape
    DB, DL, _ = xsm.shape
    halves = ncores // NB
    assert halves == 2 and DL == 4
    L = SEQ // 2
    n_own = L // P
    n_pre = n_own
    n_smp = DB // ncores
    caches = [f(inp[f"cache_kv_w{w}"])[0].reshape(DB, w, 1024) for w in (128, 512, 2048)]
    sconv = f(inp["state_conv"])[0]
    sssm = f(inp["state_ssm"])[0].reshape(DB, 1536, P)
    p = {k: np.asarray(inp[k])[0] for k in PARAMS}
    common = _common_inputs(p)
    nc = Builder(n_pre, n_own, n_smp, n_kvpre=min(16, n_pre)).build()
    in_maps = []
    for c in range(ncores):
        b, half = c // 2, c % 2
        xs = np.zeros(((n_pre + n_own + n_smp) * P, D), np.float32)
        if half:
            xs[0:L] = xp[b, 0:L]
        xs[L:2 * L] = xp[b, half * L:(half + 1) * L]
        for j in range(n_smp):
            xs[2 * L + j * P:2 * L + j * P + 4] = xsm[c * n_smp + j]
        flg = np.zeros((P, 2), np.float32)
        flg[:, 0] = half
        flg[0:4, 1] = 1.0
        m = dict(common)
        m["xs"] = xs
        m["flg_d"] = flg
        sl = slice(c * n_smp, (c + 1) * n_smp)
        for g in range(3):
            m[f"cache{g}"] = f(caches[g][sl])
        m["sconv"] = f(sconv[sl])
        m["sssm"] = f(sssm[sl])
        in_maps.append(m)
    res = run_bass_kernel_spmd(nc, in_maps, core_ids=list(range(ncores))).results
    y_p = np.zeros((NB, SEQ, D), np.float32)
    y_s = np.zeros((DB, DL, D), np.float32)
    WIN = (128, 512, 2048)
    kv_p = [np.zeros((1, NB, min(w, SEQ), 2, 8, 64), np.float32) for w in WIN]
    kv_s = [np.zeros((1, DB, w, 2, 8, 64), np.float32) for w in WIN]
    conv_p = np.zeros((1, NB, 3, 2560), np.float32)
    ssm_p = np.zeros((1, NB, 24, 64, 128), np.float32)
    conv_s = np.zeros((1, DB, 3, 2560), np.float32)
    ssm_s = np.zeros((1, DB, 24, 64, 128), np.float32)
    for c in range(ncores):
        r = res[c]
        b, half = c // 2, c % 2
        y_p[b, half * L:(half + 1) * L] = r["ys"][0:L]
        for j in range(n_smp):
            bb = c * n_smp + j
            y_s[bb] = r["ys"][L + j * P:L + j * P + 4]
            conv_s[0, bb] = r["convo"][1 + j]
            ssm_s[0, bb] = r["ssmo"][1 + j].reshape(24, 64, 128)
            for g in range(3):
                kv_s[g][0, bb] = r[f"kvs{g}"][j].reshape(WIN[g], 2, 8, 64)
        if half:
            conv_p[0, b] = r["convo"][0]
            ssm_p[0, b] = r["ssmo"][0].reshape(24, 64, 128)
            for g in range(3):
                n = min(WIN[g], SEQ)
                kv_p[g][0, b] = r[f"kvo{g}"][WIN[g] - n:].reshape(n, 2, 8, 64)
    return (y_p, y_s, kv_p[0], kv_p[1], kv_p[2], conv_p, ssm_p, kv_s[0], kv_s[1], kv_s[2], conv_s, ssm_s)


def kernel(**inp):
    return _run(inp, NCORES)
```

```python
import numpy as np
import os
STOP = int(os.environ.get('MIX_STOP', '9'))
DBG = int(os.environ.get('MIX_DBG', '0'))
DBG_G = int(os.environ.get('MIX_DBG_G', '0'))
ATT_PIPE = int(os.environ.get('ATT_PIPE', '0'))
import concourse.bass as bass
import concourse.mybir as mybir
from concourse.bass_utils import run_bass_kernel_spmd

F32 = mybir.dt.float32
BF16 = mybir.dt.bfloat16
AF = mybir.ActivationFunctionType
ALU = mybir.AluOpType

D = 1024
DFF = 2816
NIN = 10776
OFF_Z, OFF_XBC, OFF_DT, OFF_GATE = 4608, 6144, 8704, 8728
EPS = 1e-6
NCORES = 8
P = 128


class Sched:
    def __init__(self, nc):
        self.nc = nc
        self.engs = {"pe": nc.tensor, "act": nc.scalar, "dve": nc.vector, "pool": nc.gpsimd, "sp": nc.sync}
        self.stream = {e: [] for e in self.engs}
        self.sems, self.cnt = {}, {}
        self.waited = {e: {} for e in self.engs}
        self.lastw, self.readers = {}, {}
        for e in self.engs:
            self.sem(e)

    def sem(self, key):
        if key not in self.sems:
            self.sems[key] = self.nc.alloc_semaphore("s_" + key)
            self.cnt[key] = 0
        return self.sems[key]

    def _deps(self, e, reads, writes):
        need = {}

        def add(kv):
            if kv is not None and need.get(kv[0], 0) < kv[1]:
                need[kv[0]] = kv[1]

        for r in reads:
            add(self.lastw.get(r))
            if r.startswith("pb"):
                for kv in self.readers.get(r, {}).items():
                    if kv[0] != e:
                        add(kv)
        for w in writes:
            add(self.lastw.get(w))
            for kv in self.readers.get(w, {}).items():
                add(kv)
        out = []
        for k, v in need.items():
            if k == "pe" and e == "pe":
                continue
            if self.waited[e].get(k, 0) >= v:
                continue
            self.waited[e][k] = v
            out.append((k, v))
        return out

    def _mark(self, key, val, reads, writes):
        for r in reads:
            d = self.readers.setdefault(r, {})
            d[key] = max(d.get(key, 0), val)
        for w in writes:
            self.lastw[w] = (key, val)
            self.readers[w] = {}

    def op(self, e, fn, reads=(), writes=(), inc=True):
        waits = self._deps(e, reads, writes)
        if inc:
            self.cnt[e] += 1
            val = self.cnt[e]
        else:
            val = self.cnt[e] + 1
        self._mark(e, val, reads, writes)
        self.stream[e].append((waits, fn, (e, 1) if inc else None))

    def dma(self, e, semkey, out, in_, reads, writes):
        waits = self._deps(e, reads, writes)
        self.sem(semkey)
        self.cnt[semkey] += 16
        self._mark(semkey, self.cnt[semkey], reads, writes)
        self.stream[e].append((waits, lambda eng: eng.dma_start(out=out, in_=in_), (semkey, 16)))

    def finish(self):
        for k, v in self.cnt.items():
            if k not in self.engs and v > 0 and self.waited["sp"].get(k, 0) < v:
                self.stream["sp"].append(([(k, v)], None, None))

    def emit(self):
        nc = self.nc
        with nc.Block() as block:
            def mk(e):
                def body(eng):
                    for waits, fn, inc in self.stream[e]:
                        if fn is None:
                            for k, v in waits:
                                eng.wait_ge(self.sems[k], v)
                            continue
                        for k, v in waits[:-1]:
                            eng.wait_ge(self.sems[k], v)
                        ins = fn(eng)
                        if waits:
                            ins._wait_ge(self.sems[waits[-1][0]], waits[-1][1])
                        if inc is not None:
                            ins.then_inc(self.sems[inc[0]], inc[1])
                return body
            block.sync(mk("sp"))
            block.tensor(mk("pe"))
            block.scalar(mk("act"))
            block.vector(mk("dve"))
            block.gpsimd(mk("pool"))


WEIGHTS = {
    "w_in": (D, NIN), "w_branch_a": (512, D), "w_branch_b": (1536, D), "w_out": (D, D),
    "ffn1_gate": (D, DFF), "ffn1_up": (D, DFF), "ffn1_down": (DFF, D),
    "ffn2_gate": (D, DFF), "ffn2_up": (D, DFF), "ffn2_down": (DFF, D),
}
NSLOT = 2


class Builder:
    def __init__(self, n_pre, n_own, n_smp, n_kvpre=16):
        nc = self.nc = bass.Bass("TRN2", target_bir_lowering=False)
        self.S = Sched(nc)
        self.n_pre, self.n_own, self.n_smp, self.n_kvpre = n_pre, n_own, n_smp, n_kvpre
        self.kinds = ["pre"] * n_pre + ["own"] * n_own + ["smp"] * n_smp
        ng = len(self.kinds)
        nout = n_own + n_smp
        self.w32 = {n: nc.dram_tensor(n, [k, m], F32, kind="ExternalInput").ap() for n, (k, m) in WEIGHTS.items()}
        self.conv = {}
        self.xs = nc.dram_tensor("xs", [ng * P, D], F32, kind="ExternalInput").ap()
        self.gvec = nc.dram_tensor("gvec", [6, D], F32, kind="ExternalInput").ap()
        self.gpre_d = nc.dram_tensor("gpre_d", [P, 24], F32, kind="ExternalInput").ap()
        self.cst = nc.dram_tensor("cst", [P, 3 * P], F32, kind="ExternalInput").ap()
        self.ys = nc.dram_tensor("ys", [nout * P, D], F32, kind="ExternalOutput").ap()
        A = nc.alloc_sbuf_tensor
        self.ws = [A(f"ws{i}", [P, 8, 512], BF16) for i in range(NSLOT)]
        self.slot_i = 0
        self.nring = NSLOT + 1
        self.x = [A(f"x{i}", [P, D], F32) for i in range(2)]
        self.xn = A("xn", [P, D], BF16)
        self.junk = self.xn
        self.hnT = A("hnT", [P, 8, P], BF16)
        self.sg = A("sg", [P, 512], F32)
        self.gpre = A("gpre", [P, 3, 8], F32)
        self.gpost = A("gpost", [P, 3, D], F32)
        self.cstt = A("cstt", [P, 3 * P], F32)
        self.identb = A("identb", [P, P], BF16)
        self.st = A("st", [P, 16], F32)
        self.tmp = A("tmp", [P, 512], F32)
        DI = nc.dram_tensor
        self.cw_d = DI("cw_d", [P, 20 * 5], F32, kind="ExternalInput").ap()
        self.hv_d = DI("hv_d", [4, 24], F32, kind="ExternalInput").ap()
        self.nw_d = DI("nw_d", [P, 12], F32, kind="ExternalInput").ap()
        self.mask_d = DI("mask_d", [P, 24 * P], F32, kind="ExternalInput").ap()
        self.WIN = [128, 512, 2048]
        self.flg_d = DI("flg_d", [P, 2], F32, kind="ExternalInput").ap()
        self.kvo = [DI(f"kvo{g}", [self.WIN[g], 1024], F32, kind="ExternalOutput").ap() for g in range(3)]
        self.convo = DI("convo", [1 + n_smp, 3, 2560], F32, kind="ExternalOutput").ap()
        self.ssmo = DI("ssmo", [1 + n_smp, 1536, P], F32, kind="ExternalOutput").ap()
        if n_smp:
            self.cache = [DI(f"cache{g}", [n_smp, self.WIN[g], 1024], F32, kind="ExternalInput").ap() for g in range(3)]
            self.kvs = [DI(f"kvs{g}", [n_smp, self.WIN[g], 1024], F32, kind="ExternalOutput").ap() for g in range(3)]
            self.sconv = DI("sconv", [n_smp, 3, 2560], F32, kind="ExternalInput").ap()
            self.sssm = DI("sssm", [n_smp, 1536, P], F32, kind="ExternalInput").ap()
        self.flg = A("flg", [P, 2], F32)
        self.cw = A("cw", [P, 20, 5], F32)
        self.hv = A("hv", [P, 4, 24], F32)
        self.nw = A("nw", [P, 12], F32)
        self.maskf = A("maskf", [P, 512], F32)
        self.mask = A("mask", [P, 24, P], BF16)
        self.qb = A("qb", [P, 512], BF16)
        self.qb2 = A("qb2", [P, 512], BF16)
        self.qT = A("qT", [P, 12, P], BF16)
        self.NT = [2, 5, 17]
        self.kTh = [A(f"kTh{g}", [P, self.NT[g], 4, P], BF16) for g in range(3)]
        self.Vh = [A(f"Vh{g}", [P, self.NT[g], 8, 65], BF16) for g in range(3)]
        self.zs = A("zs", [P, 1536], F32)
        self.xraw = A("xraw", [P, 2560], BF16)
        self.stg = [A(f"stg{i}", [P, 10, 131], F32) for i in range(2)]
        self.carry = A("carry", [P, 20, 3], F32)
        self.caccT = A("cacc", [P, 2, 10, P], F32)
        self.cacc = [self.caccT[:, 0, :, :], self.caccT[:, 1, :, :]]
        self.xc = A("xc", [P, 20, P], BF16)
        self.xstm = A("xstm", [P, 16, P], BF16)
        self.sm = A("sm", [P, 10, 24], F32)
        self.ssdbuf = A("ssdbuf", [P, 3 * 1536], BF16)
        self.xd = self.ssdbuf[:, 0:1536].rearrange("p (h e) -> p h e", h=24)
        self.xdp = self.ssdbuf[:, 1536:3072].rearrange("p (h e) -> p h e", h=24)
        self.cbm = A("cbm", [P, 4, P], F32)
        self.Dt = A("Dt", [P, 4, P], F32)
        self.Lt = A("Lt", [P, 4, P], F32)
        self.Mt = self.ssdbuf[:, 3072:4608].rearrange("p (h e) -> p h e", h=12)
        self.y = A("y", [P, 1536], F32)
        self.ytmp = A("ytmp", [P, 1536], F32)
        self.act = self.zs[:].bitcast(BF16)[:, 0:DFF]
        self.actT = self.y[:].bitcast(BF16)[:, 0:DFF].rearrange("p (c t) -> p c t", c=22)
        self.ctmp = self.ytmp[:, 0:1280].rearrange("p (a b) -> p a b", a=10)
        self.ctmp1 = self.y[:, 0:1280].rearrange("p (a b) -> p a b", a=10)
        self.act2 = self.ytmp[:].bitcast(BF16)[:, 0:DFF]
        self.actT2 = self.caccT[:].rearrange("p a b c -> p (a b c)").bitcast(BF16)[:, 0:DFF].rearrange("p (c t) -> p c t", c=22)
        self.yn = A("yn", [P, 1536], BF16)
        self.ynT = A("ynT", [P, 12, P], BF16)
        self.ST = A("ST", [P, 1536], F32)
        self.STb = A("STb", [P, 1536], BF16)
        self.gates = A("gates", [P, 1024], F32)
        self.mrg = A("mrg", [P, 1024], F32)
        self.cur_x = [(self.x[0], "x0"), (self.x[1], "x1")]
        self.cur_g = [(self.gates, "gates"), (self.mrg, "mrg")]
        self.mrgb = A("mrgb", [P, 1024], BF16)
        self.mrgT = A("mrgT", [P, 8, P], BF16)
        self.PT = A("PT", [P, 8, P], BF16)
        self.oa = A("oa", [P, 512], BF16)
        self.oaT = A("oaT", [P, 4, P], BF16)
        self.pb = [nc.alloc_psum_tensor(f"pb{i}", [P, 512], F32) for i in range(8)]
        self.pb_i = 0

    def dump(self, name, ap, res, g=0, only_g=None):
        if not DBG or g != DBG_G:
            return
        shp = [int(v) for v in ap.shape]
        d = self.nc.dram_tensor("dbg_" + name, shp, ap.dtype, kind="ExternalOutput").ap()
        self.S.dma("sp", "dbg_" + name, d, ap, reads=[res], writes=[])

    @staticmethod
    def L(r):
        return [r] if isinstance(r, str) else list(r)

    def bank(self):
        i = self.pb_i
        self.pb_i = (i + 1) % 8
        return self.pb[i], f"pb{i}"

    def slab(self, wname, k0, nk, c0, ncols):
        S = self.S
        i = self.slot_i % self.nring
        self.slot_i = (i + 1) % self.nring
        if i < NSLOT:
            slot, wres = self.ws[i], [f"ws{i}"]
        else:
            slot, wres = self.ssdbuf[:, 0:4096].rearrange("p (k n) -> p k n", k=8), ["xd", "xdp", "Mt"]
        dst = slot[:, 0:nk, 0:ncols]
        key = (wname, k0, c0)
        res_scr = f"scr_{wname}_{k0}_{c0}"
        if key not in self.conv:
            scr = self.nc.dram_tensor(res_scr, [P, nk * ncols], BF16).ap().rearrange("p (k n) -> p k n", k=nk)
            self.conv[key] = scr
            src = self.w32[wname][k0 * P:(k0 + nk) * P, c0:c0 + ncols].rearrange("(kc p) n -> p kc n", p=P)
            S.dma("pool", f"lc{i}", dst, src, reads=[], writes=wres)
            S.dma("sp", f"sv{i}", scr, dst, reads=wres, writes=[res_scr])
        else:
            S.dma("sp", f"ld{i}", dst, self.conv[key], reads=[res_scr], writes=wres)
        return slot, wres

    def linear(self, actT, actT_res, KC, wname, c0, ncols, evac):
        self.linear_multi([(actT, actT_res)], KC, wname, c0, ncols, [evac])

    def linear_multi(self, acts, KC, wname, c0, ncols, evacs):
        S = self.S
        cb = 0
        while cb < ncols:
            n = min(512, ncols - cb)
            banks = [self.bank() for _ in acts]
            k0 = 0
            while k0 < KC:
                nk = min(8, KC - k0)
                slot, sr = self.slab(wname, k0, nk, c0 + cb, n)
                for (actT, ares), (ps, psr) in zip(acts, banks):
                    for j in range(nk):
                        first = (k0 + j == 0)
                        last = (k0 + j == KC - 1)
                        S.op("pe", (lambda eng, ps=ps, a=actT[:, k0 + j, :], r=slot[:, j, 0:n], f=first, l=last, n=n:
                                    eng.matmul(ps[:, 0:n], lhsT=a, rhs=r, start=f, stop=l)),
                             reads=self.L(ares) + sr, writes=[psr], inc=last or j == nk - 1)
                k0 += nk
            for (ps, psr), evac in zip(banks, evacs):
                evac(ps, psr, cb, n)
            cb += n

    def transpose_to(self, src, src_res, nchunks, dstT, dst_res, scale=None, scale_res="gpre"):
        S = self.S
        c = 0
        while c < nchunks:
            n = min(8, nchunks - c)
            ps, psr = self.bank()
            psb = ps[:].bitcast(BF16)
            for j in range(n):
                S.op("pe", (lambda eng, o=psb[:, j * P:(j + 1) * P], i=src[:, (c + j) * P:(c + j + 1) * P]:
                            eng.transpose(o, i, self.identb[:])),
                     reads=self.L(src_res) + ["identb"], writes=[psr], inc=(j == n - 1))
            src3 = psb[:, 0:n * P].rearrange("p (a b) -> p a b", a=n)
            if scale is None:
                S.op("act", (lambda eng, o=dstT[:, c:c + n, :], i=src3: eng.copy(out=o, in_=i)), reads=[psr], writes=self.L(dst_res))
            else:
                S.op("dve", (lambda eng, o=dstT[:, c:c + n, :], i=src3, sc=scale[:, c:c + n].unsqueeze(2).broadcast_to([P, n, P]):
                             eng.tensor_tensor(out=o, in0=i, in1=sc, op=ALU.mult)), reads=[psr, scale_res], writes=self.L(dst_res))
            c += n

    def rstd(self, ss_ap, out_ap, factor, res_in, res_out):
        S = self.S
        f2 = factor * factor
        S.op("dve", lambda eng: eng.tensor_scalar(out=out_ap, in0=ss_ap, scalar1=1.0 / (D * f2), scalar2=EPS / f2,
                                                  op0=ALU.mult, op1=ALU.add), reads=[res_in], writes=[res_out])
        S.op("act", lambda eng: eng.sqrt(out=out_ap, in_=out_ap), reads=[res_out], writes=[res_out])
        S.op("dve", lambda eng: eng.reciprocal(out=out_ap, in_=out_ap), reads=[res_out], writes=[res_out])

    def prenorm(self, x, xr, gi, hnT=None, hres="hnT"):
        S = self.S
        hnT = self.hnT if hnT is None else hnT
        S.op("act", lambda eng: eng.activation(out=self.junk[:], in_=x[:], func=AF.Square, accum_out=self.st[:, 0:1]),
             reads=[xr], writes=["xn", "st0"])
        self.rstd(self.st[:, 0:1], self.st[:, 1:2], 1.0, "st0", "st1")
        S.op("dve", lambda eng: eng.tensor_scalar(out=self.xn[:], in0=x[:], scalar1=self.st[:, 1:2], scalar2=None,
                                                  op0=ALU.mult), reads=[xr, "st1"], writes=["xn"])
        self.transpose_to(self.xn, "xn", 8, hnT, hres, scale=self.gpre[:, gi, :])

    def postnorm_evac(self, x, xr, gi, factor, sc=2):
        S = self.S
        held = []

        def evac(ps, psr, cb, n):
            k = len(held)
            S.op("act", lambda eng: eng.activation(out=self.junk[:, cb:cb + n], in_=ps[:, 0:n], func=AF.Square,
                                                   accum_out=self.st[:, sc + k:sc + 1 + k]),
                 reads=[psr], writes=["xn", f"st{sc + k}"])
            held.append((ps, psr, cb, n))

        def fin():
            S.op("dve", lambda eng: eng.tensor_tensor(out=self.st[:, sc + 2:sc + 3], in0=self.st[:, sc:sc + 1], in1=self.st[:, sc + 1:sc + 2],
                                                      op=ALU.add), reads=[f"st{sc}", f"st{sc + 1}"], writes=[f"st{sc + 2}"])
            self.rstd(self.st[:, sc + 2:sc + 3], self.st[:, sc + 3:sc + 4], factor, f"st{sc + 2}", f"st{sc + 3}")
            for ps, psr, cb, n in held:
                S.op("dve", lambda eng, ps=ps, cb=cb, n=n: eng.scalar_tensor_tensor(
                    out=self.tmp[:, 0:n], in0=ps[:, 0:n], scalar=self.st[:, sc + 3:sc + 4], in1=self.gpost[:, gi, cb:cb + n],
                    op0=ALU.mult, op1=ALU.mult), reads=[psr, f"st{sc + 3}", "gpost"], writes=["tmp"])
                S.op("dve", lambda eng, cb=cb, n=n: eng.tensor_tensor(out=x[:, cb:cb + n], in0=x[:, cb:cb + n],
                                                                     in1=self.tmp[:, 0:n], op=ALU.add),
                     reads=["tmp", xr], writes=[xr])
        return evac, fin

    def ffn(self, tiles, which):
        S = self.S
        pre, post = (0, 0) if which == 1 else (2, 2)
        gname, uname, dname = f"ffn{which}_gate", f"ffn{which}_up", f"ffn{which}_down"
        self.nring = NSLOT + 1
        bufs = [(self.hnT, "hnT", self.sg, "sg", self.act, "zs", self.actT, "y"),
                (self.mrgT, "mrgT", self.tmp, "tmp", self.act2, "ytmp", self.actT2, ("cacc0", "cacc1"))][:len(tiles)]
        for (x, xr), b in zip(tiles, bufs):
            self.prenorm(x, xr, pre, b[0], b[1])

        def mk_gate(b):
            def ev(ps, psr, cb, n):
                S.op("act", lambda eng, ps=ps: eng.activation(out=b[2][:, 0:n], in_=ps[:, 0:n], func=AF.Silu), reads=[psr], writes=[b[3]])
            return ev

        def mk_up(b, cb0):
            def ev(ps, psr, cb, n):
                S.op("dve", lambda eng, ps=ps: eng.tensor_tensor(out=b[4][:, cb0:cb0 + n], in0=ps[:, 0:n], in1=b[2][:, 0:n], op=ALU.mult),
                     reads=[psr, b[3]], writes=[b[5]])
            return ev
        acts = [(b[0], b[1]) for b in bufs]
        cb = 0
        while cb < DFF:
            n = min(512, DFF - cb)
            self.linear_multi(acts, 8, gname, cb, n, [mk_gate(b) for b in bufs])
            self.linear_multi(acts, 8, uname, cb, n, [mk_up(b, cb) for b in bufs])
            cb += n
        for b in bufs:
            self.transpose_to(b[4], b[5], 22, b[6], b[7])
        pn = [self.postnorm_evac(x, xr, post, 0.5, sc=2 + 6 * t) for t, (x, xr) in enumerate(tiles)]
        self.linear_multi([(b[6], b[7]) for b in bufs], 22, dname, 0, D, [p[0] for p in pn])
        for p in pn:
            p[1]()

    def bc(self, ap24, n):
        return ap24.unsqueeze(2).broadcast_to([P, ap24.shape[1], n])

    def mixer(self, x, xr, g, kind):
        S = self.S
        full = kind != "pre"
        do_kv = full or g >= self.n_pre - self.n_kvpre
        oi = g - self.n_pre if kind == "own" else -1
        bi = g - self.n_pre - self.n_own if kind == "smp" else -1
        (gates, gr), (mrg, mr) = self.cur_g
        V = lambda t, a, b2: t[:].rearrange("p (a b) -> p a b", a=a, b=b2)
        self.prenorm(x, xr, 1)
        triU = self.cstt[:, P:2 * P]
        ones = self.cstt[:, 2 * P:3 * P]
        slots = [0 if kind == "smp" else g % nt for nt in self.NT]
        if STOP <= 0:
            return
        def ev_x(ps, psr, cb, n):
            S.op("act", lambda eng, ps=ps: eng.copy(out=self.xraw[:, cb:cb + n], in_=ps[:, 0:n]), reads=[psr], writes=["xraw"])
        self.linear(self.hnT, "hnT", 8, "w_in", OFF_XBC, 2560, ev_x)
        for hf in range(2):
            c0 = hf * 10
            CE = "dve" if hf == 0 else "pool"
            ctmp, ctr = (self.ctmp, "ytmp") if hf == 0 else (self.ctmp1, "y")
            stg, cacc = self.stg[hf], self.cacc[hf]
            sr, ar = f"stg{hf}", f"cacc{hf}"
            S.op(CE, lambda eng, c0=c0, stg=stg: eng.tensor_copy(out=stg[:, :, 0:3], in_=self.carry[:, c0:c0 + 10, :]), reads=["carry"], writes=[sr])
            self.transpose_to(self.xraw[:, c0 * P:(c0 + 10) * P], "xraw", 10, stg[:, :, 3:131], sr)
            S.op(CE, lambda eng, c0=c0, stg=stg: eng.tensor_copy(out=self.carry[:, c0:c0 + 10, :], in_=stg[:, :, 128:131]), reads=[sr], writes=["carry"])
            for tap in range(4):
                wv = self.cw[:, c0:c0 + 10, tap:tap + 1].broadcast_to([P, 10, P])
                if tap == 0:
                    S.op(CE, lambda eng, wv=wv, stg=stg, cacc=cacc: eng.tensor_tensor(out=cacc[:], in0=stg[:, :, 0:P], in1=wv, op=ALU.mult), reads=[sr, "cw"], writes=[ar])
                else:
                    S.op(CE, lambda eng, wv=wv, stg=stg, tap=tap, ctmp=ctmp: eng.tensor_tensor(out=ctmp, in0=stg[:, :, tap:tap + P], in1=wv, op=ALU.mult), reads=[sr, "cw"], writes=[ctr])
                    S.op(CE, lambda eng, cacc=cacc, ctmp=ctmp: eng.tensor_tensor(out=cacc[:], in0=cacc[:], in1=ctmp, op=ALU.add), reads=[ar, ctr], writes=[ar])
            S.op(CE, lambda eng, c0=c0, cacc=cacc: eng.tensor_tensor(out=cacc[:], in0=cacc[:], in1=self.cw[:, c0:c0 + 10, 4:5].broadcast_to([P, 10, P]), op=ALU.add),
                 reads=[ar, "cw"], writes=[ar])
        for dg in range(3):
            if not do_kv:
                break
            base = dg * 1536
            sl = slots[dg]
            need_rows = kind == "smp" or (kind == "own" and self.WIN[dg] - (self.n_own - oi) * P >= 0)
            if full:
                def ev_q(ps, psr, cb, n):
                    S.op("act", lambda eng, ps=ps: eng.mul(out=self.qb[:], in_=ps[:, 0:512], mul=0.125), reads=[psr], writes=["qb"])
                self.linear(self.hnT, "hnT", 8, "w_in", base, 512, ev_q)
                self.transpose_to(self.qb, "qb", 4, self.qT[:, dg * 4:(dg + 1) * 4, :], "qT")

            def ev_k(ps, psr, cb, n):
                S.op("act", lambda eng, ps=ps: eng.copy(out=self.qb2[:], in_=ps[:, 0:512]), reads=[psr], writes=["qb2"])
                if need_rows:
                    S.op("dve", lambda eng, ps=ps: eng.tensor_copy(out=mrg[:, 0:512], in_=ps[:, 0:512]), reads=[psr], writes=[mr])
            self.linear(self.hnT, "hnT", 8, "w_in", base + 512, 512, ev_k)
            self.transpose_to(self.qb2, "qb2", 4, self.kTh[dg][:, sl, :, :], f"kTh{dg}_{sl}")

            def ev_v(ps, psr, cb, n, dg=dg, sl=sl):
                S.op("act", lambda eng, ps=ps: eng.copy(out=self.Vh[dg][:, sl, :, 0:64], in_=ps[:, 0:512].rearrange("p (h e) -> p h e", h=8)),
                     reads=[psr], writes=[f"Vh{dg}_{sl}"])
                if need_rows:
                    S.op("dve", lambda eng, ps=ps: eng.tensor_copy(out=mrg[:, 512:1024], in_=ps[:, 0:512]), reads=[psr], writes=[mr])
            self.linear(self.hnT, "hnT", 8, "w_in", base + 1024, 512, ev_v)
            if kind == "pre":
                S.op("dve", lambda eng, dg=dg, sl=sl: eng.tensor_copy(out=self.Vh[dg][:, sl, :, 64:65], in_=self.flg[:, 0:1].unsqueeze(1).broadcast_to([P, 8, 1])),
                     reads=["flg"], writes=[f"Vh{dg}_{sl}"])
            elif kind == "own":
                S.op("dve", lambda eng, dg=dg, sl=sl: eng.memset(self.Vh[dg][:, sl, :, 64:65], 1.0), writes=[f"Vh{dg}_{sl}"])
                r0 = self.WIN[dg] - (self.n_own - oi) * P
                if r0 >= 0:
                    S.dma("pool", f"kvo{dg}", self.kvo[dg][r0:r0 + P, :], mrg[:], reads=[mr], writes=[])
            else:
                W = self.WIN[dg]
                S.dma("pool", f"kvo{dg}", self.kvs[dg][bi, W - 4:W, :], mrg[0:4, :], reads=[mr], writes=[])
        if STOP == 10:
            return
        if full:
            def ev_z(ps, psr, cb, n):
                S.op("act", lambda eng: eng.activation(out=self.zs[:, cb:cb + n], in_=ps[:, 0:n], func=AF.Silu), reads=[psr], writes=["zs"])
            self.linear(self.hnT, "hnT", 8, "w_in", OFF_Z, 1536, ev_z)
        for hf in range(2):
            S.op("act", lambda eng, hf=hf: eng.activation(out=self.xc[:, hf * 10:(hf + 1) * 10, :], in_=self.cacc[hf][:], func=AF.Silu), reads=[f"cacc{hf}"], writes=["xc"])
        self.transpose_to(self.xc[:].rearrange("p a b -> p (a b)"), "xc", 16, self.xstm, "xstm")
        if STOP == 11:
            return
        sm = self.sm
        def ev_dt(ps, psr, cb, n):
            S.op("dve", lambda eng: eng.tensor_tensor(out=sm[:, 0, :], in0=ps[:, 0:24], in1=self.hv[:, 0, :], op=ALU.add), reads=[psr, "hv"], writes=["sm0"])
        self.linear(self.hnT, "hnT", 8, "w_in", OFF_DT, 24, ev_dt)
        S.op("act", lambda eng: eng.activation(out=sm[:, 0, :], in_=sm[:, 0, :], func=AF.Exp), reads=["sm0"], writes=["sm0"])
        S.op("act", lambda eng: eng.activation(out=sm[:, 1, :], in_=sm[:, 0, :], func=AF.Ln, bias=1.0), reads=["sm0"], writes=["sm1"])
        if kind != "own":
            fc = 0 if kind == "pre" else 1
            S.op("dve", lambda eng: eng.tensor_scalar(out=sm[:, 1, :], in0=sm[:, 1, :], scalar1=self.flg[:, fc:fc + 1], scalar2=None, op0=ALU.mult), reads=["sm1", "flg"], writes=["sm1"])
        S.op("dve", lambda eng: eng.tensor_tensor(out=sm[:, 2, :], in0=sm[:, 1, :], in1=self.hv[:, 1, :], op=ALU.mult), reads=["sm1", "hv"], writes=["sm2"])
        self.dump("zs", self.zs[:], "zs", g)
        self.dump("xc", self.xc[:], "xc", g)
        self.dump("xstm", self.xstm[:], "xstm", g)
        if STOP <= 1:
            return
        ps1, p1r = self.bank()
        S.op("pe", lambda eng: eng.matmul(ps1[:, 0:24], lhsT=triU, rhs=sm[:, 2, :], start=True, stop=True), reads=["sm2", "cstt"], writes=[p1r], inc=False)
        S.op("pe", lambda eng: eng.matmul(ps1[:, 32:56], lhsT=ones, rhs=sm[:, 2, :], start=True, stop=True), reads=["sm2", "cstt"], writes=[p1r])
        S.op("act", lambda eng: eng.copy(out=sm[:, 3, :], in_=ps1[:, 0:24]), reads=[p1r], writes=["sm3"])
        S.op("dve", lambda eng: eng.tensor_scalar(out=sm[:, 4, :], in0=ps1[:, 0:24], scalar1=-1.0, scalar2=None, op0=ALU.mult), reads=[p1r], writes=["sm4"])
        S.op("act", lambda eng: eng.activation(out=sm[:, 5, :], in_=ps1[:, 0:24], func=AF.Exp), reads=[p1r], writes=["sm5"])
        S.op("dve", lambda eng: eng.tensor_tensor(out=sm[:, 6, :], in0=ps1[:, 32:56], in1=sm[:, 3, :], op=ALU.subtract), reads=[p1r, "sm3"], writes=["sm6"])
        S.op("act", lambda eng: eng.activation(out=sm[:, 6, :], in_=sm[:, 6, :], func=AF.Exp), reads=["sm6"], writes=["sm6"])
        S.op("dve", lambda eng: eng.tensor_tensor(out=sm[:, 6, :], in0=sm[:, 6, :], in1=sm[:, 1, :], op=ALU.mult), reads=["sm6", "sm1"], writes=["sm6"])
        S.op("act", lambda eng: eng.activation(out=sm[:, 7, :], in_=ps1[:, 32:56], func=AF.Exp), reads=[p1r], writes=["sm7"])
        xs3 = self.xstm[:, 0:12, :].rearrange("p a (h e) -> p (a h) e", h=2)
        S.op("dve", lambda eng: eng.tensor_tensor(out=self.xd, in0=xs3, in1=self.bc(sm[:, 1, :], 64), op=ALU.mult), reads=["xstm", "sm1"], writes=["xd"])
        S.op("pool", lambda eng: eng.tensor_tensor(out=self.xdp, in0=xs3, in1=self.bc(sm[:, 6, :], 64), op=ALU.mult), reads=["xstm", "sm6"], writes=["xdp"])
        if full:
            ps2, p2r = self.bank()
            for gg in range(4):
                S.op("pe", lambda eng, gg=gg: eng.matmul(ps2[:, gg * P:(gg + 1) * P], lhsT=self.xc[:, 12 + gg, :], rhs=self.xc[:, 16 + gg, :], start=True, stop=True),
                     reads=["xc"], writes=[p2r], inc=(gg == 3))
            S.op("dve", lambda eng: eng.tensor_tensor(out=self.cbm[:], in0=ps2[:].rearrange("p (a b) -> p a b", a=4), in1=triU.unsqueeze(1).broadcast_to([P, 4, P]), op=ALU.mult),
                 reads=[p2r, "cstt"], writes=["cbm"])
            for gg in range(4):
                po, por = self.bank()
                S.op("pe", lambda eng, gg=gg, po=po: eng.matmul(po[:, 0:384], lhsT=self.xc[:, 16 + gg, :], rhs=self.STb[:, gg * 384:(gg + 1) * 384], start=True, stop=True),
                     reads=["xc", "STb"], writes=[por])
                S.op("dve", lambda eng, gg=gg, po=po: eng.tensor_tensor(out=self.y[:, gg * 384:(gg + 1) * 384].rearrange("p (h e) -> p h e", h=6),
                                                                         in0=po[:, 0:384].rearrange("p (h e) -> p h e", h=6),
                                                                         in1=self.bc(sm[:, 5, gg * 6:(gg + 1) * 6], 64), op=ALU.mult),
                     reads=[por, "sm5"], writes=["y"])
            for hh in range(2):
                for hb in range(3):
                    pd, pdr = self.bank()
                    for jj in range(4):
                        h = hh * 12 + hb * 4 + jj
                        S.op("pe", lambda eng, h=h, jj=jj, pd=pd: eng.matmul(pd[:, jj * P:(jj + 1) * P], lhsT=sm[:, 2, h:h + 1].broadcast_to([P, P]), rhs=triU, start=True, stop=True),
                             reads=["sm2", "cstt"], writes=[pdr], inc=(jj == 3))
                    for jj in range(4):
                        h = hh * 12 + hb * 4 + jj
                        S.op("dve", lambda eng, h=h, jj=jj, pd=pd: eng.tensor_scalar(out=self.Dt[:, jj, :], in0=pd[:, jj * P:(jj + 1) * P], scalar1=sm[:, 4, h:h + 1], scalar2=0.0,
                                                                                   op0=ALU.add, op1=ALU.min), reads=[pdr, "sm4"], writes=["Dt"])
                    S.op("act", lambda eng: eng.activation(out=self.Lt[:], in_=self.Dt[:], func=AF.Exp), reads=["Dt"], writes=["Lt"])
                    for jj in range(4):
                        h = hh * 12 + hb * 4 + jj
                        S.op("pool", lambda eng, h=h, jj=jj, hb=hb: eng.tensor_tensor(out=self.Mt[:, hb * 4 + jj, :], in0=self.Lt[:, jj, :], in1=self.cbm[:, h // 6, :], op=ALU.mult),
                             reads=["Lt", "cbm"], writes=["Mt"])
                py, pyr = [], []
                for k in range(2):
                    a, b2 = self.bank()
                    py.append(a); pyr.append(b2)
                for j12 in range(12):
                    h = hh * 12 + j12
                    S.op("pe", lambda eng, h=h, j12=j12, py=py: eng.matmul(py[j12 // 8][:, (j12 % 8) * 64:(j12 % 8) * 64 + 64], lhsT=self.Mt[:, j12, :], rhs=self.xd[:, h, :], start=True, stop=True),
                         reads=["Mt", "xd"], writes=[pyr[j12 // 8]], inc=(j12 in (7, 11)))
                S.op("dve", lambda eng, hh=hh, py=py: eng.tensor_tensor(out=self.y[:, hh * 768:hh * 768 + 512], in0=self.y[:, hh * 768:hh * 768 + 512], in1=py[0][:, 0:512], op=ALU.add),
                     reads=["y", pyr[0]], writes=["y"])
                S.op("dve", lambda eng, hh=hh, py=py: eng.tensor_tensor(out=self.y[:, hh * 768 + 512:hh * 768 + 768], in0=self.y[:, hh * 768 + 512:hh * 768 + 768], in1=py[1][:, 0:256], op=ALU.add),
                     reads=["y", pyr[1]], writes=["y"])
        for gg in range(4):
            pS, pSr = self.bank()
            S.op("pe", lambda eng, gg=gg, pS=pS: eng.matmul(pS[:, 0:384], lhsT=self.xstm[:, 12 + gg, :], rhs=self.xdp[:, gg * 6:(gg + 1) * 6, :].rearrange("p h e -> p (h e)"), start=True, stop=True),
                 reads=["xstm", "xdp"], writes=[pSr])
            stv = self.ST[:, gg * 384:(gg + 1) * 384].rearrange("p (h e) -> p h e", h=6)
            S.op("pool", lambda eng, gg=gg, stv=stv: eng.tensor_tensor(out=stv, in0=stv, in1=self.bc(sm[:, 7, gg * 6:(gg + 1) * 6], 64), op=ALU.mult), reads=["ST", "sm7"], writes=["ST"])
            S.op("dve", lambda eng, gg=gg, pS=pS: eng.tensor_tensor(out=self.ST[:, gg * 384:(gg + 1) * 384], in0=self.ST[:, gg * 384:(gg + 1) * 384], in1=pS[:, 0:384], op=ALU.add),
                 reads=["ST", pSr], writes=["ST"])
        S.op("act", lambda eng: eng.copy(out=self.STb[:], in_=self.ST[:]), reads=["ST"], writes=["STb"])
        if kind == "smp":
            self.store_conv(1 + bi, 1)
            self.store_state(1 + bi)
        elif kind == "own" and oi == self.n_own - 1:
            self.store_conv(0, 125)
            self.store_state(0)
        self.dump("sm", self.sm[:], "sm7", g)
        self.dump("ST", self.ST[:], "ST", g)
        if not full:
            return
        self.dump("y0", self.y[:], "y", g)
        if STOP <= 2:
            return
        t3 = self.ytmp[:].rearrange("p (h e) -> p h e", h=24)
        S.op("pool", lambda eng: eng.tensor_tensor(out=t3, in0=xs3, in1=self.bc(self.hv[:, 2, :], 64), op=ALU.mult), reads=["xstm", "hv"], writes=["ytmp"])
        S.op("dve", lambda eng: eng.tensor_tensor(out=self.y[:], in0=self.y[:], in1=self.ytmp[:], op=ALU.add), reads=["y", "ytmp"], writes=["y"])
        S.op("dve", lambda eng: eng.tensor_tensor(out=self.y[:], in0=self.y[:], in1=self.zs[:], op=ALU.mult), reads=["y", "zs"], writes=["y"])
        for gg in range(4):
            S.op("act", lambda eng, gg=gg: eng.activation(out=self.ytmp[:, gg * 384:(gg + 1) * 384], in_=self.y[:, gg * 384:(gg + 1) * 384], func=AF.Square, accum_out=sm[:, 8, gg:gg + 1]),
                 reads=["y"], writes=["ytmp", "sm8"])
        S.op("dve", lambda eng: eng.tensor_scalar(out=sm[:, 8, 0:4], in0=sm[:, 8, 0:4], scalar1=1.0 / 384, scalar2=EPS, op0=ALU.mult, op1=ALU.add), reads=["sm8"], writes=["sm8"])
        S.op("act", lambda eng: eng.sqrt(out=sm[:, 8, 0:4], in_=sm[:, 8, 0:4]), reads=["sm8"], writes=["sm8"])
        S.op("dve", lambda eng: eng.reciprocal(out=sm[:, 8, 0:4], in_=sm[:, 8, 0:4]), reads=["sm8"], writes=["sm8"])
        S.op("dve", lambda eng: eng.tensor_tensor(out=self.yn[:].rearrange("p (g e) -> p g e", g=4), in0=self.y[:].rearrange("p (g e) -> p g e", g=4), in1=self.bc(sm[:, 8, 0:4], 384), op=ALU.mult),
             reads=["y", "sm8"], writes=["yn"])
        self.dump("yn", self.yn[:], "yn", g)
        self.transpose_to(self.yn, "yn", 12, self.ynT, "ynT", scale=self.nw, scale_res="nw")
        def ev_gate(off):
            def ev(ps, psr, cb, n):
                S.op("act", lambda eng: eng.activation(out=gates[:, cb:cb + n], in_=ps[:, 0:n], func=AF.Sigmoid), reads=[psr], writes=[gr])
            return ev
        self.linear(self.hnT, "hnT", 8, "w_in", OFF_GATE + 1024, 1024, ev_gate(1024))
        def ev_b(ps, psr, cb, n):
            S.op("dve", lambda eng: eng.tensor_tensor(out=mrg[:, cb:cb + n], in0=ps[:, 0:n], in1=gates[:, cb:cb + n], op=ALU.mult), reads=[psr, gr], writes=[mr])
        self.linear(self.ynT, "ynT", 12, "w_branch_b", 0, 1024, ev_b)
        self.dump(mr, mrg[:], mr, g)
        if STOP <= 3:
            return
        mbase = [0, 2, 7]
        if kind == "smp":
            tiles = [(dg, o, mbase[dg] + o, o) for dg in range(3) for o in range(self.NT[dg])]
        else:
            first = self.n_pre - self.n_kvpre
            tiles = [(dg, o, mbase[dg] + o, (g - o) % self.NT[dg]) for dg in range(3) for o in range(self.NT[dg]) if g - o >= first]
        self.attention(tiles)
        self.dump("oa", self.oa[:], "oa", g)
        self.transpose_to(self.oa, "oa", 4, self.oaT, "oaT")
        self.linear(self.hnT, "hnT", 8, "w_in", OFF_GATE, 1024, ev_gate(0))
        def ev_a(ps, psr, cb, n):
            S.op("dve", lambda eng: eng.tensor_tensor(out=self.tmp[:, 0:n], in0=ps[:, 0:n], in1=gates[:, cb:cb + n], op=ALU.mult), reads=[psr, gr], writes=["tmp"])
            S.op("dve", lambda eng: eng.tensor_tensor(out=self.mrgb[:, cb:cb + n], in0=mrg[:, cb:cb + n], in1=self.tmp[:, 0:n], op=ALU.add), reads=[mr, "tmp"], writes=["mrgb"])
        self.linear(self.oaT, "oaT", 4, "w_branch_a", 0, 1024, ev_a)
        self.dump("mrgb", self.mrgb[:], "mrgb", g)
        self.transpose_to(self.mrgb, "mrgb", 8, self.mrgT, "mrgT")
        evac, fin = self.postnorm_evac(x, xr, 1, 1.0)
        self.linear(self.mrgT, "mrgT", 8, "w_out", 0, D, evac)
        fin()

    def attention(self, tiles):
        S = self.S
        outb = [self.bank() for _ in range(2)]
        scb = [self.bank() for _ in range(4)]
        PT = self.PT
        r = 0
        for h in range(8):
            c, pb = h // 2, 64 * (h % 2)
            ob, obr = outb[h // 4]
            oc = (h % 4) * 65
            for b0 in range(0, len(tiles), 8):
                batch = tiles[b0:b0 + 8]
                nb = len(batch)
                banks = scb[2 * (r % 2):2 * (r % 2) + 2]
                r += 1
                for j, (dg, o, m, sl) in enumerate(batch):
                    ps, psr = banks[j // 4]
                    S.op("pe", lambda eng, j=j, dg=dg, sl=sl, ps=ps, pb=pb, c=c: eng.matmul(ps[:, (j % 4) * P:(j % 4 + 1) * P], lhsT=self.kTh[dg][pb:pb + 64, sl, c, :], rhs=self.qT[pb:pb + 64, dg * 4 + c, :], start=True, stop=True),
                         reads=[f"kTh{dg}_{sl}", "qT"], writes=[psr], inc=(j % 4 == 3 or j == nb - 1))
                for k in range((nb + 3) // 4):
                    ps, psr = banks[k]
                    n4 = min(4, nb - 4 * k)
                    pr = f"PT{k}"
                    S.op("act", lambda eng, ps=ps, k=k, n4=n4: eng.activation(out=PT[:, 4 * k:4 * k + n4, :], in_=ps[:, 0:n4 * P].rearrange("p (a b) -> p a b", a=n4), func=AF.Exp), reads=[psr], writes=[pr])
                    ms = [t[2] for t in batch[4 * k:4 * k + n4]]
                    if ms == list(range(ms[0], ms[0] + n4)):
                        S.op("dve", lambda eng, k=k, n4=n4, m0=ms[0]: eng.tensor_tensor(out=PT[:, 4 * k:4 * k + n4, :], in0=PT[:, 4 * k:4 * k + n4, :], in1=self.mask[:, m0:m0 + n4, :], op=ALU.mult), reads=[pr, "mask"], writes=[pr])
                    else:
                        for j, m in enumerate(ms):
                            S.op("dve", lambda eng, j=4 * k + j, m=m: eng.tensor_tensor(out=PT[:, j, :], in0=PT[:, j, :], in1=self.mask[:, m, :], op=ALU.mult), reads=[pr, "mask"], writes=[pr])
                for j, (dg, o, m, sl) in enumerate(batch):
                    first = (b0 + j == 0)
                    last = (b0 + j == len(tiles) - 1)
                    S.op("pe", lambda eng, j=j, dg=dg, sl=sl, first=first, last=last, ob=ob, oc=oc, h=h: eng.matmul(ob[:, oc:oc + 65], lhsT=PT[:, j, :], rhs=self.Vh[dg][:, sl, h, :], start=first, stop=last),
                         reads=[f"PT{j // 4}", f"Vh{dg}_{sl}"], writes=[obr], inc=(j % 4 == 3 or j == nb - 1))
        for k in range(2):
            ob, obr = outb[k]
            ob3 = ob[:, 0:260].rearrange("p (h e) -> p h e", e=65)
            S.op("dve", lambda eng, ob3=ob3, k=k: eng.reciprocal(out=self.sm[:, 9, 4 * k:4 * k + 4].unsqueeze(2), in_=ob3[:, :, 64:65]), reads=[obr], writes=["sm9"])
            S.op("dve", lambda eng, ob3=ob3, k=k: eng.tensor_tensor(out=self.oa[:, 256 * k:256 * k + 256].rearrange("p (h e) -> p h e", e=64), in0=ob3[:, :, 0:64],
                                                                   in1=self.bc(self.sm[:, 9, 4 * k:4 * k + 4], 64), op=ALU.mult), reads=[obr, "sm9"], writes=["oa"])

    def store_conv(self, idx, r0):
        self.S.dma("pool", "cvo", self.convo[idx], self.xraw[r0:r0 + 3, :], reads=["xraw"], writes=[])

    def store_state(self, idx):
        S = self.S
        id32 = self.cstt[:, 0:P]
        for c0 in range(0, 12, 4):
            ps, psr = self.bank()
            for j in range(4):
                S.op("pe", lambda eng, ps=ps, j=j, c=c0 + j: eng.transpose(ps[:, j * P:(j + 1) * P], self.ST[:, c * P:(c + 1) * P], id32),
                     reads=["ST", "cstt"], writes=[psr], inc=(j == 3))
            S.op("act", lambda eng, ps=ps, c0=c0: eng.copy(out=self.ytmp[:, c0 * P:(c0 + 4) * P], in_=ps[:, 0:512]), reads=[psr], writes=["ytmp"])
        S.dma("pool", "sso", self.ssmo[idx].rearrange("(c p) n -> p c n", p=P), self.ytmp[:].rearrange("p (c n) -> p c n", c=12), reads=["ytmp"], writes=[])

    def load_sample(self, bi):
        S = self.S
        id32 = self.cstt[:, 0:P]
        for dg in range(3):
            nt = self.NT[dg]
            for j in range(nt - 1):
                sl = nt - 1 - j
                src = self.cache[dg][bi, j * P:(j + 1) * P, :]
                kb, kr = (self.qb, "qb") if j % 2 == 0 else (self.qb2, "qb2")
                S.dma("pool", f"ck{j % 2}", kb[:], src[:, 0:512], reads=[], writes=[kr])
                self.transpose_to(kb, kr, 4, self.kTh[dg][:, sl, :, :], f"kTh{dg}_{sl}")
                S.dma("pool", f"cv{dg}_{sl}", self.Vh[dg][:, sl, :, 0:64], src[:, 512:1024].rearrange("p (h e) -> p h e", h=8),
                      reads=[], writes=[f"Vh{dg}_{sl}"])
        S.dma("sp", "ssi", self.ytmp[:].rearrange("p (c n) -> p c n", c=12), self.sssm[bi].rearrange("(c p) n -> p c n", p=P), reads=[], writes=["ytmp"])
        for c0 in range(0, 12, 4):
            ps, psr = self.bank()
            for j in range(4):
                S.op("pe", lambda eng, ps=ps, j=j, c=c0 + j: eng.transpose(ps[:, j * P:(j + 1) * P], self.ytmp[:, c * P:(c + 1) * P], id32),
                     reads=["ytmp", "cstt"], writes=[psr], inc=(j == 3))
            S.op("act", lambda eng, ps=ps, c0=c0: eng.copy(out=self.ST[:, c0 * P:(c0 + 4) * P], in_=ps[:, 0:512]), reads=[psr], writes=["ST"])
        S.op("act", lambda eng: eng.copy(out=self.STb[:], in_=self.ST[:]), reads=["ST"], writes=["STb"])
        for hf in range(2):
            S.dma("sp", "sci", self.cacc[0][0:3, :, :].rearrange("p a b -> p (a b)"), self.sconv[bi][:, hf * 1280:(hf + 1) * 1280], reads=[], writes=["cacc0"])
            ps, psr = self.bank()
            for c in range(10):
                S.op("pe", lambda eng, ps=ps, c=c: eng.matmul(ps[:, c * 3:(c + 1) * 3], lhsT=self.cacc[0][0:3, c, :], rhs=self.cstt[0:3, 0:3], start=True, stop=True),
                     reads=["cacc0", "cstt"], writes=[psr], inc=(c == 9))
            S.op("act", lambda eng, ps=ps, hf=hf: eng.copy(out=self.carry[:, hf * 10:(hf + 1) * 10, :], in_=ps[:, 0:30].rearrange("p (c t) -> p c t", t=3)),
                 reads=[psr], writes=["carry"])

    def setup(self):
        S = self.S
        S.dma("sp", "c0", self.cstt[:], self.cst, reads=[], writes=["cstt"])
        S.dma("sp", "c1", self.gpost[:], self.gvec[3:6, :].partition_broadcast(P), reads=[], writes=["gpost"])
        S.dma("sp", "c2", self.gpre[:], self.gpre_d.rearrange("p (g c) -> p g c", g=3), reads=[], writes=["gpre"])
        S.op("dve", lambda eng: eng.tensor_copy(out=self.identb[:], in_=self.cstt[:, 0:P]), reads=["cstt"], writes=["identb"])
        S.dma("sp", "c3", self.cw[:], self.cw_d.rearrange("p (c t) -> p c t", t=5), reads=[], writes=["cw"])
        S.dma("sp", "c4", self.hv[:], self.hv_d.partition_broadcast(P), reads=[], writes=["hv"])
        S.dma("sp", "c5", self.nw[:], self.nw_d, reads=[], writes=["nw"])
        S.op("act", lambda eng: eng.activation(out=self.hv[:, 1, :], in_=self.hv[:, 1, :], func=AF.Exp), reads=["hv"], writes=["hv"])
        S.op("dve", lambda eng: eng.tensor_scalar(out=self.hv[:, 1, :], in0=self.hv[:, 1, :], scalar1=-1.0, scalar2=None, op0=ALU.mult), reads=["hv"], writes=["hv"])
        for i in range(6):
            S.dma("sp", "c6", self.maskf[:], self.mask_d[:, i * 512:(i + 1) * 512], reads=[], writes=["maskf"])
            S.op("dve", lambda eng, i=i: eng.tensor_copy(out=self.mask[:, i * 4:(i + 1) * 4, :], in_=self.maskf[:].rearrange("p (a b) -> p a b", a=4)), reads=["maskf"], writes=["mask"])
        S.op("dve", lambda eng: eng.memset(self.ST[:], 0.0), writes=["ST"])
        S.op("dve", lambda eng: eng.memset(self.STb[:], 0.0), writes=["STb"])
        S.op("dve", lambda eng: eng.memset(self.carry[:], 0.0), writes=["carry"])
        for dg in range(3):
            S.op("dve", lambda eng, dg=dg: eng.memset(self.Vh[dg][:], 1.0), writes=[f"Vh{dg}_{sl}" for sl in range(self.NT[dg])])
        S.dma("sp", "c7", self.flg[:], self.flg_d, reads=[], writes=["flg"])
        for dg in range(3):
            W = self.WIN[dg]
            for b in range(self.n_smp):
                for r0 in range(4, W, 256):
                    r1 = min(W, r0 + 256)
                    S.dma("act", "kvcp", self.kvs[dg][b, r0 - 4:r1 - 4, :], self.cache[dg][b, r0:r1, :], reads=[], writes=[])

    def load_x(self, gs, bufs):
        for t, g in enumerate(gs):
            self.S.dma("sp", f"xin{t}", bufs[t][0][:], self.xs[g * P:(g + 1) * P, :], reads=[], writes=[bufs[t][1]])

    def pair(self, gs, nxt):
        S = self.S
        kind = self.kinds[gs[0]]
        tiles = self.cur_x[:len(gs)]
        self.ffn(tiles, 1)
        if kind == "pre" and nxt:
            self.load_x(nxt, self.cur_g)
        for (x, xr), g in zip(tiles, gs):
            if kind == "smp":
                bi = g - self.n_pre - self.n_own
                if bi == 0:
                    for dg in range(3):
                        S.op("dve", lambda eng, dg=dg: eng.memset(self.Vh[dg][:, :, :, 64:65], 1.0), writes=[f"Vh{dg}_{sl}" for sl in range(self.NT[dg])])
                self.load_sample(bi)
            self.dump("h1", x[:], xr, g, DBG_G)
            self.mixer(x, xr, g, kind)
            self.dump("h2", x[:], xr, g, DBG_G)
        if kind != "pre":
            if nxt:
                self.load_x(nxt, self.cur_g)
            self.ffn(tiles, 2)
            for (x, xr), g in zip(tiles, gs):
                o = g - self.n_pre
                S.dma("pool", f"yout{o % 2}", self.ys[o * P:(o + 1) * P, :], x[:], reads=[xr], writes=[])
        self.cur_x, self.cur_g = self.cur_g, self.cur_x

    def build(self):
        self.setup()
        pairs = []
        g = 0
        while g < len(self.kinds):
            gs = [g]
            if g + 1 < len(self.kinds) and self.kinds[g + 1] == self.kinds[g]:
                gs.append(g + 1)
            pairs.append(gs)
            g += len(gs)
        self.load_x(pairs[0], self.cur_x)
        for i, gs in enumerate(pairs):
            self.pair(gs, pairs[i + 1] if i + 1 < len(pairs) else None)
        self.S.finish()
        self.S.emit()
        return self.nc


PARAMS = ["w_in", "conv_w", "conv_b", "dt_bias", "a_log", "d_skip", "ssd_norm_w", "w_branch_a", "w_branch_b", "w_out",
          "ffn1_gate", "ffn1_up", "ffn1_down", "ffn2_gate", "ffn2_up", "ffn2_down",
          "g_pre_ffn1", "g_post_ffn1", "g_pre_mix", "g_post_mix", "g_pre_ffn2", "g_post_ffn2"]


def _common_inputs(p):
    f = lambda a: np.ascontiguousarray(np.asarray(a, dtype=np.float32))
    ins = {n: f(p[n]) for n in WEIGHTS}
    gvec = np.stack([f(p[k]) for k in ("g_pre_ffn1", "g_pre_mix", "g_pre_ffn2", "g_post_ffn1", "g_post_mix", "g_post_ffn2")])
    ins["gvec"] = gvec
    ins["gpre_d"] = f(gvec[0:3].reshape(3, 8, P).transpose(2, 0, 1).reshape(P, 24))
    ins["cst"] = f(np.concatenate([np.eye(P), np.triu(np.ones((P, P))), np.ones((P, P))], 1))
    cw = np.concatenate([f(p["conv_w"]), f(p["conv_b"])[None]], 0)
    ins["cw_d"] = f(cw.reshape(5, 20, P).transpose(2, 1, 0).reshape(P, 100))
    ins["hv_d"] = f(np.stack([f(p["dt_bias"]), f(p["a_log"]), f(p["d_skip"]), np.zeros(24, np.float32)]))
    ins["nw_d"] = f(f(p["ssd_norm_w"]).reshape(12, P).T)
    k = np.arange(P)[:, None]
    q = np.arange(P)[None, :]
    ms = []
    for (W, dil), nt in zip(((128, 1), (512, 4), (2048, 16)), (2, 5, 17)):
        for o in range(nt):
            d = q + P * o - k
            ms.append(((d >= 0) & (d <= W) & (d % dil == 0)).astype(np.float32))
    ins["mask_d"] = f(np.stack(ms, 1).reshape(P, 24 * P))
    return ins


def _run(inp, ncores):
    f = lambda a: np.ascontiguousarray(np.asarray(a, dtype=np.float32))
    xp = f(inp["x_prompt"])
    xsm = f(inp["x_sample"])
    NB, SEQ, _ = xp.shape
    DB, DL, _ = xsm.shape
    halves = ncores // NB
    assert halves == 2 and DL == 4
    L = SEQ // 2
    n_own = L // P
    n_pre = n_own
    n_smp = DB // ncores
    caches = [f(inp[f"cache_kv_w{w}"])[0].reshape(DB, w, 1024) for w in (128, 512, 2048)]
    sconv = f(inp["state_conv"])[0]
    sssm = f(inp["state_ssm"])[0].reshape(DB, 1536, P)
    p = {k: np.asarray(inp[k])[0] for k in PARAMS}
    common = _common_inputs(p)
    nc = Builder(n_pre, n_own, n_smp, n_kvpre=min(16, n_pre)).build()
    in_maps = []
    for c in range(ncores):
        b, half = c // 2, c % 2
        xs = np.zeros(((n_pre + n_own + n_smp) * P, D), np.float32)
        if half:
            xs[0:L] = xp[b, 0:L]
        xs[L:2 * L] = xp[b, half * L:(half + 1) * L]
        for j in range(n_smp):
            xs[2 * L + j * P:2 * L + j * P + 4] = xsm[c * n_smp + j]
        flg = np.zeros((P, 2), np.float32)
        flg[:, 0] = half
        flg[0:4, 1] = 1.0
        m = dict(common)
        m["xs"] = xs
        m["flg_d"] = flg
        sl = slice(c * n_smp, (c + 1) * n_smp)
        for g in range(3):
            m[f"cache{g}"] = f(caches[g][sl])
        m["sconv"] = f(sconv[sl])
        m["sssm"] = f(sssm[sl])
        in_maps.append(m)
    res = run_bass_kernel_spmd(nc, in_maps, core_ids=list(range(ncores))).results
    y_p = np.zeros((NB, SEQ, D), np.float32)
    y_s = np.zeros((DB, DL, D), np.float32)
    WIN = (128, 512, 2048)
    kv_p = [np.zeros((1, NB, min(w, SEQ), 2, 8, 64), np.float32) for w in WIN]
    kv_s = [np.zeros((1, DB, w, 2, 8, 64), np.float32) for w in WIN]
    conv_p = np.zeros((1, NB, 3, 2560), np.float32)
    ssm_p = np.zeros((1, NB, 24, 64, 128), np.float32)
    conv_s = np.zeros((1, DB, 3, 2560), np.float32)
    ssm_s = np.zeros((1, DB, 24, 64, 128), np.float32)
    for c in range(ncores):
        r = res[c]
        b, half = c // 2, c % 2
        y_p[b, half * L:(half + 1) * L] = r["ys"][0:L]
        for j in range(n_smp):
            bb = c * n_smp + j
            y_s[bb] = r["ys"][L + j * P:L + j * P + 4]
            conv_s[0, bb] = r["convo"][1 + j]
            ssm_s[0, bb] = r["ssmo"][1 + j].reshape(24, 64, 128)
            for g in range(3):
                kv_s[g][0, bb] = r[f"kvs{g}"][j].reshape(WIN[g], 2, 8, 64)
        if half:
            conv_p[0, b] = r["convo"][0]
            ssm_p[0, b] = r["ssmo"][0].reshape(24, 64, 128)
            for g in range(3):
                n = min(WIN[g], SEQ)
                kv_p[g][0, b] = r[f"kvo{g}"][WIN[g] - n:].reshape(n, 2, 8, 64)
    return (y_p, y_s, kv_p[0], kv_p[1], kv_p[2], conv_p, ssm_p, kv_s[0], kv_s[1], kv_s[2], conv_s, ssm_s)


def kernel(**inp):
    return _run(inp, NCORES)
```
